# Optimizing a Trainium2 kernel written in Bass

```python
import jax, jax.numpy as jnp
from jax import lax
import numpy as np

D_MODEL = 1024
BATCH = 2
SEQ = 16384
DEPTH = 4

GRID_W = 64
CTX_LEN = 256
N_MIXERS = 3
N_LRU = (DEPTH + 2) // 3
N_MLSTM = (DEPTH + 1) // 3
N_RWKV = DEPTH // 3

LRU_WIDTH = 1280
LRU_BLOCKS = 10
LRU_BS = LRU_WIDTH // LRU_BLOCKS
LRU_C = 8.0
CONV_W = 4

MLSTM_HEADS = 8
MLSTM_DK = 128
MLSTM_DV = 256
MLSTM_QK = MLSTM_HEADS * MLSTM_DK
MLSTM_WIDTH = MLSTM_HEADS * MLSTM_DV
MLSTM_CHUNK = 64

RWKV_HEAD = 64
RWKV_WIDTH = D_MODEL
RWKV_HEADS = RWKV_WIDTH // RWKV_HEAD
DECAY_RANK = 64
ICL_RANK = 64
GN_EPS = 64e-5
NORM_EPS = 1e-6

kernel_name = 'hybrid_rglru_mlstm_rwkv7_prefix_trunk'


def rmsnorm(x, g):
    xf = x.astype(jnp.float32)
    y = xf * lax.rsqrt(jnp.mean(xf * xf, axis=-1, keepdims=True) + NORM_EPS)
    return (y * g.astype(jnp.float32)).astype(x.dtype)


def ada_params(cond, w, b):
    m = jax.nn.silu(cond) @ w + b
    return jnp.split(m, 3, axis=-1)


def centred_dwconv(u, w, b):
    k = w.shape[0]
    left = k // 2
    t = u.shape[1]
    up = jnp.pad(u, ((0, 0), (left, k - 1 - left), (0, 0)))
    out = b
    for j in range(k):
        out = out + up[:, j:j + t] * w[j]
    return out


def rglru_coeffs(u, gate_w, gate_b, lam):
    bsz, t, e = u.shape
    uf = u.astype(jnp.float32)
    ub = uf.reshape(bsz, t, LRU_BLOCKS, LRU_BS)
    gates = jnp.einsum('btnj,dgnjk->dgbtnk', ub, gate_w.astype(jnp.float32)).reshape(2, 2, bsz, t, e)
    gates = jax.nn.sigmoid(gates + gate_b[:, :, None, None, :].astype(jnp.float32))
    r, i = gates[:, 0], gates[:, 1]
    log_a = -LRU_C * jax.nn.softplus(-lam.astype(jnp.float32))[:, None, None, :] * r
    a = jnp.exp(log_a)
    b = jnp.sqrt(-jnp.expm1(2.0 * log_a)) * (i * uf[None])
    return a, b


def linear_scan(a, b, h0, reverse):
    def step(h, ab):
        h = ab[0] * h + ab[1]
        return h, h
    h_last, hs = lax.scan(step, h0, (jnp.moveaxis(a, 1, 0), jnp.moveaxis(b, 1, 0)), reverse=reverse)
    return jnp.moveaxis(hs, 0, 1), h_last


def rglru_bidir(a, b, h0):
    hf, lf = linear_scan(a[0], b[0], h0[0], False)
    hb, lb = linear_scan(a[1], b[1], h0[1], True)
    return hf + hb, jnp.stack([lf, lb])


def lru_mixer(hc, hx, w_in, conv_w, conv_b, gate_w, gate_b, lam, w_out, with_ctx):
    def coeffs(h):
        u, z = jnp.split(h @ w_in, 2, axis=-1)
        a, b = rglru_coeffs(centred_dwconv(u, conv_w, conv_b), gate_w, gate_b, lam)
        return a, b, z
    ac, bc, zc = coeffs(hc)
    h0 = jnp.zeros((2, hc.shape[0], LRU_WIDTH), jnp.float32)
    yc, state_c = rglru_bidir(ac, bc, h0)
    ax, bx, zx = coeffs(hx)
    yx, _ = rglru_bidir(ax, bx, state_c)
    out_x = (yx.astype(hx.dtype) * jax.nn.silu(zx)) @ w_out
    out_c = ((yc.astype(hc.dtype) * jax.nn.silu(zc)) @ w_out) if with_ctx else None
    return out_c, out_x


def mlstm_proj(h, w_in, gate_b):
    bsz, t, _ = h.shape
    splits = [MLSTM_QK, 2 * MLSTM_QK, 2 * MLSTM_QK + MLSTM_WIDTH, 2 * MLSTM_QK + 2 * MLSTM_WIDTH]
    q, k, v, z, g = jnp.split(h @ w_in, splits, axis=-1)
    heads = lambda y, d: y.reshape(bsz, t, MLSTM_HEADS, d).transpose(0, 2, 1, 3).astype(jnp.float32)
    q = heads(q, MLSTM_DK)
    k = heads(k, MLSTM_DK) * (MLSTM_DK ** -0.5)
    v = heads(v, MLSTM_DV)
    g = g.reshape(bsz, t, 2, 2, MLSTM_HEADS).astype(jnp.float32) + gate_b.astype(jnp.float32)
    g = g.transpose(2, 3, 0, 4, 1)
    log_i = g[:, 0]
    log_f = jax.nn.log_sigmoid(g[:, 1])
    return q, k, v, z, log_i, log_f


def mlstm_chunk_scan(q, k, v, log_i, log_f, state):
    bsz, nh, t, _ = q.shape
    L = MLSTM_CHUNK
    nc = t // L
    chunks = lambda y: jnp.moveaxis(y.reshape(bsz, nh, nc, L, *y.shape[3:]), 2, 0)
    causal = jnp.tril(jnp.ones((L, L), dtype=bool))

    def step(carry, inp):
        C, n, m = carry
        qc, kc, vc, ic, fc = inp
        bcum = jnp.cumsum(fc, axis=-1)
        logw = jnp.where(causal, bcum[..., :, None] - bcum[..., None, :] + ic[..., None, :], -jnp.inf)
        g = bcum + m[..., None]
        m_t = jnp.maximum(g, jnp.max(logw, axis=-1))
        s = jnp.einsum('bhtd,bhsd->bhts', qc, kc) * jnp.exp(logw - m_t[..., None])
        inter = jnp.exp(g - m_t)
        num = jnp.einsum('bhts,bhsv->bhtv', s, vc) + inter[..., None] * jnp.einsum('bhtd,bhdv->bhtv', qc, C)
        den = jnp.sum(s, axis=-1) + inter * jnp.einsum('bhtd,bhd->bht', qc, n)
        h = num / jnp.maximum(jnp.abs(den), jnp.exp(-m_t))[..., None]
        m_new = m_t[..., -1]
        wgt = jnp.exp(bcum[..., -1:] - bcum + ic - m_new[..., None])
        decay = jnp.exp(bcum[..., -1] + m - m_new)
        C = decay[..., None, None] * C + jnp.einsum('bhs,bhsd,bhsv->bhdv', wgt, kc, vc)
        n = decay[..., None] * n + jnp.einsum('bhs,bhsd->bhd', wgt, kc)
        return (C, n, m_new), h

    state, hs = lax.scan(step, state, (chunks(q), chunks(k), chunks(v), chunks(log_i), chunks(log_f)))
    return jnp.moveaxis(hs, 0, 2).reshape(bsz, nh, t, -1), state


def mlstm_bidir(q, k, v, log_i, log_f, states):
    flip = lambda y: jnp.flip(y, axis=2)
    hf, sf = mlstm_chunk_scan(q, k, v, log_i[0], log_f[0], states[0])
    hb, sb = mlstm_chunk_scan(flip(q), flip(k), flip(v), flip(log_i[1]), flip(log_f[1]), states[1])
    return hf + flip(hb), (sf, sb)


def mlstm_zero_state(bsz):
    return (jnp.zeros((bsz, MLSTM_HEADS, MLSTM_DK, MLSTM_DV), jnp.float32),
            jnp.zeros((bsz, MLSTM_HEADS, MLSTM_DK), jnp.float32),
            jnp.zeros((bsz, MLSTM_HEADS), jnp.float32))


def mlstm_mixer(hc, hx, w_in, gate_b, norm_g, w_out, with_ctx):
    def finish(hsum, z):
        bsz, _, t, _ = hsum.shape
        hn = hsum * lax.rsqrt(jnp.mean(hsum * hsum, axis=-1, keepdims=True) + NORM_EPS)
        hn = hn.transpose(0, 2, 1, 3).reshape(bsz, t, MLSTM_WIDTH) * norm_g
        return (hn.astype(z.dtype) * jax.nn.silu(z)) @ w_out
    qc, kc, vc, zc, ic, fc = mlstm_proj(hc, w_in, gate_b)
    zero = mlstm_zero_state(hc.shape[0])
    hcs, states_c = mlstm_bidir(qc, kc, vc, ic, fc, (zero, zero))
    qx, kx, vx, zx, ix, fx = mlstm_proj(hx, w_in, gate_b)
    hxs, _ = mlstm_bidir(qx, kx, vx, ix, fx, states_c)
    return (finish(hcs, zc) if with_ctx else None), finish(hxs, zx)


def shift_grid(h, rows):
    bsz, t, d = h.shape
    q = d // 4
    g = h.reshape(bsz, rows, GRID_W, d)
    left = jnp.pad(g[:, :, :-1, :q], ((0, 0), (0, 0), (1, 0), (0, 0)))
    right = jnp.pad(g[:, :, 1:, q:2 * q], ((0, 0), (0, 0), (0, 1), (0, 0)))
    up = jnp.pad(g[:, :-1, :, 2 * q:3 * q], ((0, 0), (1, 0), (0, 0), (0, 0)))
    down = jnp.pad(g[:, 1:, :, 3 * q:], ((0, 0), (0, 1), (0, 0), (0, 0)))
    return jnp.concatenate([left, right, up, down], axis=-1).reshape(bsz, t, d)


def shift_seq(h):
    half = h.shape[-1] // 2
    prev = jnp.pad(h[:, :-1, :half], ((0, 0), (1, 0), (0, 0)))
    nxt = jnp.pad(h[:, 1:, half:], ((0, 0), (0, 1), (0, 0)))
    return jnp.concatenate([prev, nxt], axis=-1)


def rwkv7_prep(h, shifted, mu, w_rkvz, w0, w1, w2, a0, a1, a2, k_k, k_a):
    hd = lambda y: y.reshape(*y.shape[:-1], RWKV_HEADS, RWKV_HEAD).astype(jnp.float32)
    xs = h[None] + (shifted - h)[None] * mu[:, None, None, :]
    r, k, v, z = jnp.einsum('gbtd,gde->gbte', xs[:4], w_rkvz)
    w_pre = w0[:, None, None, :] + jnp.einsum('xbtr,xre->xbte', jnp.tanh(jnp.einsum('btd,xdr->xbtr', xs[4], w1)), w2)
    decay = jnp.exp(-jnp.exp(-jax.nn.softplus(-w_pre.astype(jnp.float32)) - 0.5))
    a = jax.nn.sigmoid(a0[:, None, None, :] + jnp.einsum('xbtr,xre->xbte', jnp.einsum('btd,xdr->xbtr', xs[5], a1), a2))
    kk = hd(k * k_k)
    kk = kk / jnp.maximum(jnp.sqrt(jnp.sum(kk * kk, axis=-1, keepdims=True)), 1e-12)
    k_dir = k[None] * (1.0 + (a - 1.0) * k_a)
    return hd(r), hd(decay), kk, hd(a), hd(k_dir), hd(v), z


def rwkv7_scan(r, w, kk, a, k, v, s0, reverse):
    tm = lambda y: jnp.moveaxis(y, 1, 0)
    def step(s, inp):
        r_t, w_t, kk_t, a_t, k_t, v_t = inp
        s = (s * w_t[..., None, :]
             - jnp.einsum('bhvk,bhk->bhv', s, kk_t)[..., None] * (kk_t * a_t)[..., None, :]
             + v_t[..., :, None] * k_t[..., None, :])
        return s, jnp.einsum('bhvk,bhk->bhv', s, r_t)
    s_last, ys = lax.scan(step, s0, tuple(tm(y) for y in (r, w, kk, a, k, v)), reverse=reverse)
    return jnp.moveaxis(ys, 0, 1), s_last


def rwkv7_mixer(hc, hx, rows, mu, w_rkvz, w0, w1, w2, a0, a1, a2, k_k, k_a, r_k, ln_g, ln_b, w_out, with_ctx):
    def run(h, shifted, states):
        r, w, kk, a, k, v, z = rwkv7_prep(h, shifted, mu, w_rkvz, w0, w1, w2, a0, a1, a2, k_k, k_a)
        yf, sf = rwkv7_scan(r, w[0], kk, a[0], k[0], v, states[0], False)
        yb, sb = rwkv7_scan(r, w[1], kk, a[1], k[1], v, states[1], True)
        return (yf + yb, r, k, v, z), jnp.stack([sf, sb])

    def finish(y, r, k, v, z):
        mean = jnp.mean(y, axis=-1, keepdims=True)
        var = jnp.mean(jnp.square(y - mean), axis=-1, keepdims=True)
        y = ((y - mean) * lax.rsqrt(var + GN_EPS) * ln_g.reshape(RWKV_HEADS, RWKV_HEAD)
             + ln_b.reshape(RWKV_HEADS, RWKV_HEAD))
        bonus = jnp.sum(r[None] * k * r_k.reshape(RWKV_HEADS, RWKV_HEAD), axis=(0, -1))[..., None] * v
        y = (y + bonus).reshape(z.shape)
        return (y.astype(z.dtype) * jax.nn.silu(z)) @ w_out

    s0 = jnp.zeros((2, hc.shape[0], RWKV_HEADS, RWKV_HEAD, RWKV_HEAD), jnp.float32)
    parts_c, states_c = run(hc, shift_seq(hc), s0)
    parts_x, _ = run(hx, shift_grid(hx, rows), states_c)
    return (finish(*parts_c) if with_ctx else None), finish(*parts_x)


def setup_inputs(seed: int = 0) -> dict:
    key = jax.random.key(seed)
    ks = iter(jax.random.split(key, 40))
    nrm = lambda shape, scale: scale * jax.random.normal(next(ks), shape, jnp.float32)
    unif = lambda shape, lo, hi: jax.random.uniform(next(ks), shape, jnp.float32, lo, hi)
    D = D_MODEL
    lam_p = unif((N_LRU, 2, LRU_WIDTH), 0.9, 0.999) ** (1.0 / LRU_C)
    mlstm_gate_b = jnp.stack([nrm((N_MLSTM, 2, MLSTM_HEADS), 0.1),
                              unif((N_MLSTM, 2, MLSTM_HEADS), 3.0, 6.0)], axis=2)
    return {
        'x': nrm((BATCH, SEQ, D), 1.0),
        'c': nrm((BATCH, D), 1.0),
        'ctx': nrm((BATCH, CTX_LEN, D), 1.0),
        'c_ctx': nrm((D,), 1.0),
        'norm_g': 1.0 + nrm((DEPTH, D), 0.1),
        'mod_w': nrm((DEPTH, D, 3 * D), D ** -0.5),
        'mod_b': nrm((DEPTH, 3 * D), 0.02),
        'final_g': 1.0 + nrm((D,), 0.1),
        'lru_w_in': nrm((N_LRU, D, 2 * LRU_WIDTH), D ** -0.5),
        'lru_conv_w': nrm((N_LRU, CONV_W, LRU_WIDTH), CONV_W ** -0.5),
        'lru_conv_b': nrm((N_LRU, LRU_WIDTH), 0.02),
        'lru_gate_w': nrm((N_LRU, 2, 2, LRU_BLOCKS, LRU_BS, LRU_BS), LRU_BS ** -0.5),
        'lru_gate_b': nrm((N_LRU, 2, 2, LRU_WIDTH), 0.1),
        'lru_lam': jnp.log(lam_p) - jnp.log1p(-lam_p),
        'lru_w_out': nrm((N_LRU, LRU_WIDTH, D), LRU_WIDTH ** -0.5),
        'mlstm_w_in': nrm((N_MLSTM, D, 2 * MLSTM_QK + 2 * MLSTM_WIDTH + 4 * MLSTM_HEADS), D ** -0.5),
        'mlstm_gate_b': mlstm_gate_b,
        'mlstm_norm_g': 1.0 + nrm((N_MLSTM, MLSTM_WIDTH), 0.1),
        'mlstm_w_out': nrm((N_MLSTM, MLSTM_WIDTH, D), MLSTM_WIDTH ** -0.5),
        'r7_mu': unif((N_RWKV, 6, D), 0.0, 1.0),
        'r7_w_rkvz': nrm((N_RWKV, 4, D, RWKV_WIDTH), D ** -0.5),
        'r7_w0': unif((N_RWKV, 2, RWKV_WIDTH), -6.0, -1.0),
        'r7_w1': nrm((N_RWKV, 2, D, DECAY_RANK), D ** -0.5),
        'r7_w2': nrm((N_RWKV, 2, DECAY_RANK, RWKV_WIDTH), 0.1 * DECAY_RANK ** -0.5),
        'r7_a0': nrm((N_RWKV, 2, RWKV_WIDTH), 0.1),
        'r7_a1': nrm((N_RWKV, 2, D, ICL_RANK), D ** -0.5),
        'r7_a2': nrm((N_RWKV, 2, ICL_RANK, RWKV_WIDTH), 0.1 * ICL_RANK ** -0.5),
        'r7_k_k': 0.85 + nrm((N_RWKV, RWKV_WIDTH), 0.1),
        'r7_k_a': 1.0 + nrm((N_RWKV, RWKV_WIDTH), 0.1),
        'r7_r_k': nrm((N_RWKV, RWKV_WIDTH), 0.1),
        'r7_ln_g': 1.0 + nrm((N_RWKV, RWKV_WIDTH), 0.1),
        'r7_ln_b': nrm((N_RWKV, RWKV_WIDTH), 0.02),
        'r7_w_out': nrm((N_RWKV, RWKV_WIDTH, D), RWKV_WIDTH ** -0.5),
    }


def reference(x, c, ctx, c_ctx, norm_g, mod_w, mod_b, final_g,
              lru_w_in, lru_conv_w, lru_conv_b, lru_gate_w, lru_gate_b, lru_lam, lru_w_out,
              mlstm_w_in, mlstm_gate_b, mlstm_norm_g, mlstm_w_out,
              r7_mu, r7_w_rkvz, r7_w0, r7_w1, r7_w2, r7_a0, r7_a1, r7_a2,
              r7_k_k, r7_k_a, r7_r_k, r7_ln_g, r7_ln_b, r7_w_out):
    rows = x.shape[1] // GRID_W
    xc = ctx
    for i in range(DEPTH):
        kind, j = i % N_MIXERS, i // N_MIXERS
        with_ctx = i < DEPTH - 1
        shift_x, scale_x, gate_x = ada_params(c, mod_w[i], mod_b[i])
        shift_c, scale_c, gate_c = ada_params(c_ctx, mod_w[i], mod_b[i])
        hx = rmsnorm(x, norm_g[i]) * (1.0 + scale_x[:, None]) + shift_x[:, None]
        hc = rmsnorm(xc, norm_g[i]) * (1.0 + scale_c) + shift_c
        if kind == 0:
            yc, yx = lru_mixer(hc, hx, lru_w_in[j], lru_conv_w[j], lru_conv_b[j], lru_gate_w[j],
                               lru_gate_b[j], lru_lam[j], lru_w_out[j], with_ctx)
        elif kind == 1:
            yc, yx = mlstm_mixer(hc, hx, mlstm_w_in[j], mlstm_gate_b[j], mlstm_norm_g[j], mlstm_w_out[j], with_ctx)
        else:
            yc, yx = rwkv7_mixer(hc, hx, rows, r7_mu[j], r7_w_rkvz[j], r7_w0[j], r7_w1[j], r7_w2[j],
                                 r7_a0[j], r7_a1[j], r7_a2[j], r7_k_k[j], r7_k_a[j], r7_r_k[j],
                                 r7_ln_g[j], r7_ln_b[j], r7_w_out[j], with_ctx)
        x = x + gate_x[:, None] * yx
        if with_ctx:
            xc = xc + gate_c * yc
    return rmsnorm(x, final_g)
```

```python
import contextlib
import numpy as np
import concourse.bass as bass
import concourse.mybir as mybir
from concourse.bass_utils import run_bass_kernel_spmd

F32 = mybir.dt.float32
BF16 = mybir.dt.bfloat16
AF = mybir.ActivationFunctionType
ALU = mybir.AluOpType
AX = mybir.AxisListType

D = 1024
KD = 8
NORM_EPS = 1e-6
GN_EPS = 64e-5
LRU_W = 1280
LRU_NB = 10
LRU_TC = 512
LRU_NCH = 5

DBG = {}
CE = ("pe", "dve", "act", "pool")
ALLQ = ("pe", "dve", "act", "pool", "sp")


class Buf:
    __slots__ = ("t", "name", "w", "r")

    def __init__(self, t=None, name=""):
        self.t = t
        self.name = name
        self.w = None
        self.r = {}

    def __getitem__(self, idx):
        return self.t[idx]


class V2:
    def __init__(self, buf):
        self.buf = buf
        self.ap = buf.t[:, :, :].rearrange("p j t -> p (j t)")

    def __getitem__(self, idx):
        return self.ap[idx]


class Sch:
    def __init__(self, nc, stack, ndma=12):
        self.nc = nc
        self.stack = stack
        self.eng = {"pe": nc.tensor, "dve": nc.vector, "act": nc.scalar, "pool": nc.gpsimd, "sp": nc.sync}
        self.q = {e: [] for e in ALLQ}
        self.sem = {e: stack.enter_context(nc.semaphore("s_" + e)) for e in CE}
        self.cnt = {e: 0 for e in CE}
        self.known = {e: {} for e in ALLQ}
        self.dpool = {}
        self.dnext = {}
        self.dval = {}
        for qn in ("sp", "act", "pool"):
            self.dpool[qn] = [stack.enter_context(nc.semaphore(f"d_{qn}{i}")) for i in range(ndma)]
            self.dnext[qn] = 0
        self.n_inst = 0
        self.n_wait = 0
        self._uid = 0

    def sb(self, shape, dtype=F32, name=None):
        self._uid += 1
        name = f"{name or 'sb'}_{self._uid}"
        t = self.stack.enter_context(self.nc.sbuf_tensor(name, list(shape), dtype))
        return Buf(t, name)

    def ps(self, shape, dtype=F32, name=None):
        self._uid += 1
        name = name or f"ps{self._uid}"
        t = self.stack.enter_context(self.nc.psum_tensor(name, list(shape), dtype))
        return Buf(t, name)

    def dram(self, name, shape, dtype=F32, kind="Internal"):
        t = self.nc.dram_tensor(name, list(shape), dtype, kind=kind)
        return Buf(t, name)

    def _wait(self, qn, tok):
        kind, key, val = tok
        kn = self.known[qn]
        if kn.get(key, 0) >= val:
            return
        kn[key] = val
        sem = self.sem[key] if kind == "e" else key
        eng = self.eng[qn]
        eng.wait_ge(sem, val)
        self.n_wait += 1

    def _deps(self, qn, reads, writes):
        toks = []
        for b in reads:
            if b.w is not None:
                toks.append(b.w)
        for b in writes:
            if b.w is not None:
                t = b.w
                if not (t[0] == "e" and t[1] == qn == "pe"):
                    toks.append(t)
            for (k0, k1), v in b.r.items():
                if not (k0 == "e" and k1 == qn == "pe"):
                    toks.append((k0, k1, v))
        for t in toks:
            self._wait(qn, t)

    def _mark(self, tok, reads, writes):
        for b in reads:
            b.r[(tok[0], tok[1])] = tok[2]
        for b in writes:
            b.w = tok
            b.r = {}
        self.n_inst += 1

    def op(self, qn, fn, reads=(), writes=()):
        self._deps(qn, reads, writes)
        self.cnt[qn] += 1
        sem = self.sem[qn]
        fn().then_inc(sem, 1)
        tok = ("e", qn, self.cnt[qn])
        self._mark(tok, reads, writes)
        return tok

    def dma(self, qn, fn, reads=(), writes=()):
        self._deps(qn, reads, writes)
        pool = self.dpool[qn]
        i = self.dnext[qn] % len(pool)
        self.dnext[qn] += 1
        sem = pool[i]
        prev = self.dval.get(sem, 0)
        if prev:
            self._wait(qn, ("d", sem, prev))
        val = prev + 16
        self.dval[sem] = val
        eng = self.eng[qn]
        fn(eng).then_inc(sem, 16)
        tok = ("d", sem, val)
        self._mark(tok, reads, writes)
        return tok

    def finish(self, out_bufs):
        for b in out_bufs:
            if b.w is not None:
                self._wait("sp", b.w)

    def barrier(self):
        toks = [("e", e, self.cnt[e]) for e in CE if self.cnt[e]]
        toks += [("d", sem, v) for sem, v in self.dval.items() if v]
        for qn in ALLQ:
            for t in toks:
                if not (t[0] == "e" and t[1] == qn):
                    self._wait(qn, t)


class K:
    def __init__(self, nc, S):
        self.nc = nc
        self.S = S
        self.pools = {}
        self.psb = [S.ps([128, 512], F32, name=f"psb{i}") for i in range(8)]
        self.psi = 0
        self.ldq = 0
        self.dumped = set()
        self.dump_bufs = []

    def rot(self, key, shape, dtype=F32, n=2, init=None):
        p = self.pools.get(key)
        if p is None:
            bufs = [self.S.sb(shape, dtype, name=f"{key}_{i}") for i in range(n)]
            if init is not None:
                for b in bufs:
                    init(b)
            p = self.pools[key] = [bufs, 0]
        b = p[0][p[1] % len(p[0])]
        p[1] += 1
        return b

    def dump(self, name, ap, shape, dtype, reads):
        if name not in DBG.get("dump", ()) or name in self.dumped:
            return
        self.dumped.add(name)
        t = self.nc.dram_tensor("dbg_" + name, list(shape), dtype, kind="ExternalOutput")
        b = Buf(t, name)
        self.dma(t[:], ap, reads, [b], q="sp")
        self.dump_bufs.append(b)

    def psum(self):
        b = self.psb[self.psi % 8]
        self.psi += 1
        return b

    @contextlib.contextmanager
    def scope(self):
        S = self.S
        ls = contextlib.ExitStack()
        old_stack, old_pools = S.stack, self.pools
        S.stack = ls
        self.pools = dict(old_pools)
        try:
            yield
        finally:
            S.barrier()
            S.stack = old_stack
            self.pools = old_pools
            ls.close()

    def dma(self, out_ap, in_ap, reads, writes, q="sp", slow=False):
        if slow:
            self.S.dma(q, lambda e: e.dma_start(out=out_ap, in_=in_ap, allow_slow_non_contiguous=True),
                       reads=reads, writes=writes)
        else:
            self.S.dma(q, lambda e: e.dma_start(out=out_ap, in_=in_ap), reads=reads, writes=writes)

    def mm(self, ps_ap, lhsT, rhs, start, stop, reads, writes):
        nc = self.nc
        self.S.op("pe", lambda: nc.tensor.matmul(ps_ap, lhsT=lhsT, rhs=rhs, start=start, stop=stop),
                  reads=reads, writes=writes)

    def tr(self, ps_ap, in_ap, ident_ap, reads, writes):
        nc = self.nc
        self.S.op("pe", lambda: nc.tensor.transpose(ps_ap, in_ap, ident_ap), reads=reads, writes=writes)

    def act(self, out, in_, func, reads, writes, bias=None, scale=None, accum_out=None):
        nc = self.nc
        kw = {}
        if bias is not None:
            kw["bias"] = bias
        if scale is not None:
            kw["scale"] = scale
        if accum_out is not None:
            kw["accum_out"] = accum_out
        self.S.op("act", lambda: nc.scalar.activation(out=out, in_=in_, func=func, **kw), reads=reads, writes=writes)

    def ts(self, eng, out, in0, s1, s2, op0, op1, reads, writes):
        e = self.nc.vector if eng == "dve" else self.nc.gpsimd
        if op1 is None:
            self.S.op(eng, lambda: e.tensor_scalar(out=out, in0=in0, scalar1=s1, scalar2=None, op0=op0),
                      reads=reads, writes=writes)
        else:
            self.S.op(eng, lambda: e.tensor_scalar(out=out, in0=in0, scalar1=s1, scalar2=s2, op0=op0, op1=op1),
                      reads=reads, writes=writes)

    def tt(self, eng, out, in0, in1, op, reads, writes):
        e = self.nc.vector if eng == "dve" else self.nc.gpsimd
        self.S.op(eng, lambda: e.tensor_tensor(out=out, in0=in0, in1=in1, op=op), reads=reads, writes=writes)

    def stt(self, out, in0, scalar, in1, op0, op1, reads, writes):
        nc = self.nc
        self.S.op("dve", lambda: nc.vector.scalar_tensor_tensor(out=out, in0=in0, scalar=scalar, in1=in1,
                                                                op0=op0, op1=op1), reads=reads, writes=writes)

    def copy(self, eng, out, in_, reads, writes):
        nc = self.nc
        if eng == "act":
            self.S.op("act", lambda: nc.scalar.copy(out=out, in_=in_), reads=reads, writes=writes)
        elif eng == "dve":
            self.S.op("dve", lambda: nc.vector.tensor_copy(out=out, in_=in_), reads=reads, writes=writes)
        else:
            self.S.op("pool", lambda: nc.gpsimd.tensor_copy(out=out, in_=in_), reads=reads, writes=writes)

    def memset(self, eng, ap, val, writes):
        e = self.nc.vector if eng == "dve" else self.nc.gpsimd
        self.S.op(eng, lambda: e.memset(ap, val), writes=writes)

    def recip(self, out, in_, reads, writes):
        nc = self.nc
        self.S.op("dve", lambda: nc.vector.reciprocal(out=out, in_=in_), reads=reads, writes=writes)


def make_consts(k):
    nc, S = k.nc, k.S
    ones = S.sb([128, 512], F32, name="c_ones")
    k.memset("pool", ones[:], 1.0, [ones])
    k.ones = ones

    def sel(name, cm, step, op, base=0):
        b = S.sb([128, 128], F32, name=name)
        S.op("pool", lambda: nc.gpsimd.affine_select(out=b[:], in_=ones[:, 0:128], pattern=[[step, 128]],
                                                     compare_op=op, fill=0.0, base=base, channel_multiplier=cm),
             reads=[ones], writes=[b])
        return b
    k.ident = sel("c_ident", -1, 1, ALU.is_equal)
    k.m_le = sel("c_mle", -1, 1, ALU.is_ge)
    k.m_ge = sel("c_mge", 1, -1, ALU.is_ge)
    k.m_lt = sel("c_mlt", -1, 1, ALU.is_gt)
    k.m_gt = sel("c_mgt", 1, -1, ALU.is_gt)
    k.identb = S.sb([128, 128], BF16, name="c_identb")
    k.copy("dve", k.identb[:], k.ident[:], [k.ident], [k.identb])
    k.onesb = S.sb([128, 128], BF16, name="c_onesb")
    k.copy("dve", k.onesb[:], ones[:, 0:128], [ones], [k.onesb])


def emit_mod(k, li, W):
    nc, S = k.nc, k.S
    with k.scope():
        Ms = {"x": k.rot("modM_x", [128, 3072], F32, n=1), "c": k.rot("modM_c", [128, 3072], F32, n=1)}
        mb = k.rot("mod_b", [128, 3072], F32, n=1)
        k.dma(mb[:], W["mod_b"].t[li].partition_broadcast(128), [W["mod_b"]], [mb])
        gb = k.rot("mod_g", [128, 1024], F32, n=1)
        k.dma(gb[:], W["norm_g"].t[li].partition_broadcast(128), [W["norm_g"]], [gb])
        for half in range(2):
            pss = {c: [k.psum() for _ in range(3)] for c in ("x", "c")}
            for j in range(KD):
                wt = k.rot("modw", [128, 1536], F32, n=2)
                k.dma(wt[:], W["mod_w"].t[li, j * 128:(j + 1) * 128, half * 1536:(half + 1) * 1536], [W["mod_w"]], [wt])
                for c in ("x", "c"):
                    R = k.R[c]
                    for nb in range(3):
                        k.mm(pss[c][nb][:], R[:, j, :], wt[:, nb * 512:(nb + 1) * 512], j == 0, j == KD - 1,
                             [R, wt], [pss[c][nb]])
            for c in ("x", "c"):
                for nb in range(3):
                    o = half * 1536 + nb * 512
                    k.tt("dve", Ms[c][:, o:o + 512], pss[c][nb][:], mb[:, o:o + 512], ALU.add,
                         [pss[c][nb], mb], [Ms[c]])
        for c in ("x", "c"):
            M = Ms[c]
            k.stt(M[:, 1024:2048], M[:, 1024:2048], 1.0, gb[:], ALU.add, ALU.mult, [M, gb], [M])
            k.dma(k.modscr[c].t[:, :], M[:], [M], [k.modscr[c]], q="pool")
    return k.modscr


def emit_cond(k, W):
    nc, S = k.nc, k.S
    k.R = {}
    for c, nm in (("x", "c"), ("c", "c_ctx")):
        cv = S.sb([128, 8], F32, name="cv_" + c)
        k.dma(cv[:], W[nm].t.rearrange("(j p) -> p j", p=128), [W[nm]], [cv], slow=True)
        sc = S.sb([128, 8], F32, name="sc_" + c)
        k.act(sc[:], cv[:], AF.Silu, [cv], [sc])
        R = S.sb([128, 8, 128], F32, name="R_" + c)
        for j in range(KD):
            k.ts("dve", R[:, j, :], k.ones[:, 0:128], sc[:, j:j + 1], None, ALU.mult, None, [k.ones, sc], [R])
        k.R[c] = R


def rstd_of(k, ss, n, eps, width):
    rt = k.rot(f"rstd_a{width}", [128, width], F32, n=3)
    k.act(rt[:], ss[:], AF.Sqrt, [ss], [rt], bias=k.epsb[eps][:, 0:1], scale=1.0 / n)
    rs = k.rot(f"rstd_b{width}", [128, width], F32, n=3)
    k.recip(rs[:], rt[:], [rt], [rs])
    return rs


def pass_norm(k, src, Tn, M, HT):
    with k.scope():
        Ml = k.rot("M_local", [128, 3072], F32, n=1)
        k.dma(Ml[:], M.t[:, :], [M], [Ml])
        _pass_norm(k, src, Tn, Ml, HT)


def _pass_norm(k, src, Tn, M, HT):
    nc, S = k.nc, k.S
    TT = min(512, Tn)
    ns = TT // 128
    for t0 in range(0, Tn, TT):
        xs = k.rot("pn_x", [128, 4, 1024], F32, n=2)
        k.dma(xs[:, 0:ns, :], src.t[t0:t0 + TT, :].rearrange("(a p) d -> p a d", p=128), [src], [xs])
        hT = k.rot("pn_hT", [128, 8, 512], BF16, n=2)
        for a in range(ns):
            junk = k.rot("pn_junk", [128, 1024], F32, n=2)
            ss = k.rot("pn_ss", [128, 1], F32, n=3)
            k.act(junk[:], xs[:, a, :], AF.Square, [xs], [junk, ss], accum_out=ss[:])
            rs = rstd_of(k, ss, float(D), NORM_EPS, 1)
            h = k.rot("pn_h", [128, 1024], F32, n=2)
            k.stt(h[:], xs[:, a, :], rs[:, 0:1], M[:, 1024:2048], ALU.mult, ALU.mult, [xs, rs, M], [h])
            k.tt("pool", h[:], h[:], M[:, 0:1024], ALU.add, [h, M], [h])
            for hb in range(2):
                ps = k.psum()
                for jj in range(4):
                    j = hb * 4 + jj
                    k.tr(ps[:, jj * 128:(jj + 1) * 128], h[:, j * 128:(j + 1) * 128], k.ident[:], [h, k.ident], [ps])
                dst = hT[:, hb * 4:(hb + 1) * 4, a * 128:(a + 1) * 128]
                srcp = ps[:, :].rearrange("p (j t) -> p j t", j=4)
                if hb == 0:
                    k.copy("act", dst, srcp, [ps], [hT])
                else:
                    k.copy("dve", dst, srcp, [ps], [hT])
        k.dma(HT.t[:, t0:t0 + TT].rearrange("(j p) t -> p j t", p=128), hT[:, :, 0:TT], [hT], [HT], q="pool")


def load_w_bf16(k, name, src_ap_fn, src_buf, nK, N, stage_cols=None):
    S = k.S
    wb = S.sb([128, nK, N], BF16, name=name)
    sc = stage_cols or N
    i = 0
    with k.scope():
        for j in range(nK):
            for c0 in range(0, N, sc):
                st = k.rot(f"wstage{sc}", [128, sc], F32, n=2)
                k.dma(st[:], src_ap_fn(j, c0, sc), [src_buf], [st])
                eng = ("dve", "pool", "act")[i % 3]
                k.copy(eng, wb[:, j, c0:c0 + sc], st[:], [st], [wb])
                i += 1
    return wb


def pass_out(k, YG, nK, Wo, Tn, xsrc, G, xdst):
    with k.scope():
        Gl = k.rot("M_local", [128, 3072], F32, n=1)
        k.dma(Gl[:], G.t[:, :], [G], [Gl])
        _pass_out(k, YG, nK, Wo, Tn, xsrc, Gl, xdst)


def _pass_out(k, YG, nK, Wo, Tn, xsrc, G, xdst):
    TT = min(512, Tn)
    ns = TT // 128
    for t0 in range(0, Tn, TT):
        yg = k.rot(f"po_yg{nK}", [128, nK, 512], BF16, n=2)
        k.dma(yg[:, :, 0:TT], YG.t[0:nK * 128, t0:t0 + TT].rearrange("(j p) t -> p j t", p=128), [YG], [yg])
        xs = k.rot("po_x", [128, 4, 1024], F32, n=2)
        k.dma(xs[:, 0:ns, :], xsrc.t[t0:t0 + TT, :].rearrange("(a p) d -> p a d", p=128), [xsrc], [xs])
        xn = k.rot("po_xn", [128, 4, 1024], F32, n=2)
        for a in range(ns):
            for nb in range(2):
                ps = k.psum()
                for j in range(nK):
                    k.mm(ps[:], yg[:, j, a * 128:(a + 1) * 128], Wo[:, j, nb * 512:(nb + 1) * 512],
                         j == 0, j == nK - 1, [yg, Wo], [ps])
                tmp = k.rot("po_tmp", [128, 512], F32, n=3)
                k.tt("dve", tmp[:], ps[:], G[:, 2048 + nb * 512:2048 + (nb + 1) * 512], ALU.mult, [ps, G], [tmp])
                k.tt("pool", xn[:, a, nb * 512:(nb + 1) * 512], tmp[:], xs[:, a, nb * 512:(nb + 1) * 512], ALU.add,
                     [tmp, xs], [xn])
        k.dma(xdst.t[t0:t0 + TT, :].rearrange("(a p) d -> p a d", p=128), xn[:, 0:ns, :], [xn], [xdst], q="pool")


def pass_final(k, src, Tn, fg, out):
    with k.scope():
        _pass_final(k, src, Tn, fg, out)


def _pass_final(k, src, Tn, fg, out):
    TT = min(512, Tn)
    ns = TT // 128
    for t0 in range(0, Tn, TT):
        xs = k.rot("pn_x", [128, 4, 1024], F32, n=2)
        k.dma(xs[:, 0:ns, :], src.t[t0:t0 + TT, :].rearrange("(a p) d -> p a d", p=128), [src], [xs])
        xn = k.rot("po_xn", [128, 4, 1024], F32, n=2)
        for a in range(ns):
            junk = k.rot("pn_junk", [128, 1024], F32, n=2)
            ss = k.rot("pn_ss", [128, 1], F32, n=3)
            k.act(junk[:], xs[:, a, :], AF.Square, [xs], [junk, ss], accum_out=ss[:])
            rs = rstd_of(k, ss, float(D), NORM_EPS, 1)
            k.stt(xn[:, a, :], xs[:, a, :], rs[:, 0:1], fg[:], ALU.mult, ALU.mult, [xs, rs, fg], [xn])
        k.dma(out.t[t0:t0 + TT, :].rearrange("(a p) d -> p a d", p=128), xn[:, 0:ns, :], [xn], [out], q="pool")


def lru_layer(k, li, jl, W, seqs, scr):
    nc, S = k.nc, k.S
    with k.scope():
        w_in = load_w_bf16(k, f"lru_win{li}", lambda j, c0, sc: W["lru_w_in"].t[jl, j * 128:(j + 1) * 128, c0:c0 + sc],
                           W["lru_w_in"], KD, 2 * LRU_W, stage_cols=1280)
        w_out = load_w_bf16(k, f"lru_wout{li}", lambda j, c0, sc: W["lru_w_out"].t[jl, j * 128:(j + 1) * 128, c0:c0 + sc],
                            W["lru_w_out"], LRU_NB, D)
        gwv = W["lru_gate_w"].t[jl].rearrange("d g n j k -> (d g n) j k")
        gw = load_w_bf16(k, f"lru_gw{li}", lambda j, c0, sc: gwv[j], W["lru_gate_w"], 40, 128)
        cw = S.sb([128, 4, LRU_NB], F32, name=f"lru_cw{li}")
        k.dma(cw[:], W["lru_conv_w"].t[jl].rearrange("j (n p) -> p j n", p=128), [W["lru_conv_w"]], [cw], slow=True)
        cb = S.sb([128, LRU_NB], F32, name=f"lru_cb{li}")
        k.dma(cb[:], W["lru_conv_b"].t[jl].rearrange("(n p) -> p n", p=128), [W["lru_conv_b"]], [cb], slow=True)
        gbias = S.sb([128, 4, LRU_NB], F32, name=f"lru_gb{li}")
        k.dma(gbias[:], W["lru_gate_b"].t[jl].rearrange("d g (n p) -> p (d g) n", p=128), [W["lru_gate_b"]], [gbias], slow=True)
        lam = S.sb([128, 2, LRU_NB], F32, name=f"lru_lam{li}")
        k.dma(lam[:], W["lru_lam"].t[jl].rearrange("d (n p) -> p d n", p=128), [W["lru_lam"]], [lam], slow=True)
        e1 = S.sb([128, 2, LRU_NB], F32, name=f"lru_e1{li}")
        k.act(e1[:], lam[:], AF.Exp, [lam], [e1], scale=-1.0)
        l1 = S.sb([128, 2, LRU_NB], F32, name=f"lru_l1{li}")
        k.act(l1[:], e1[:], AF.Ln, [e1, k.oneb], [l1], bias=k.oneb[:, 0:1])
        coef = S.sb([128, 2, LRU_NB], F32, name=f"lru_coef{li}")
        k.ts("dve", coef[:], l1[:], -8.0, None, ALU.mult, None, [l1], [coef])
        coef2 = S.sb([128, 2, LRU_NB], F32, name=f"lru_coef2{li}")
        k.ts("dve", coef2[:], l1[:], -16.0, None, ALU.mult, None, [l1], [coef2])
        st = [S.sb([128, 2], F32, name=f"lru_st{li}_{n}") for n in range(LRU_NB)]
        for n in range(LRU_NB):
            k.memset("dve", st[n][:], 0.0, [st[n]])
        HT, UT, SZ, UC, HF, YG = (scr[n] for n in ("HT", "UT", "SZ", "UC", "HF", "YG"))

        for sq in seqs:
            Tn = sq["Tn"]
            if DBG.get("stop") == "w":
                continue
            pass_norm(k, sq["src"], Tn, sq["M"], HT)
            if DBG.get("stop") == "norm":
                continue
            with k.scope():
                TT = min(512, Tn)
                for t0 in range(0, Tn, TT):
                    hT = k.rot("l1_hT", [128, 8, 512], BF16, n=2)
                    k.dma(hT[:, :, 0:TT], HT.t[:, t0:t0 + TT].rearrange("(j p) t -> p j t", p=128), [HT], [hT])
                    ub = k.rot("l1_u", [128, LRU_NB, 512], F32, n=2)
                    zb = k.rot("l1_z", [128, LRU_NB, 512], BF16, n=2)
                    for oc in range(2 * LRU_NB):
                        ps = k.psum()
                        for j in range(KD):
                            k.mm(ps[:, 0:TT], w_in[:, j, oc * 128:(oc + 1) * 128], hT[:, j, 0:TT], j == 0, j == KD - 1,
                                 [w_in, hT], [ps])
                        if oc < LRU_NB:
                            k.copy("dve", ub[:, oc, 0:TT], ps[:, 0:TT], [ps], [ub])
                        else:
                            k.act(zb[:, oc - LRU_NB, 0:TT], ps[:, 0:TT], AF.Silu, [ps], [zb])
                    k.dma(UT.t[:, t0:t0 + TT].rearrange("(n p) t -> p n t", p=128), ub[:, :, 0:TT], [ub], [UT], q="pool")
                    k.dma(SZ.t[:, t0:t0 + TT].rearrange("(n p) t -> p n t", p=128), zb[:, :, 0:TT], [zb], [SZ], q="pool")
            if DBG.get("stop") == "p1":
                continue
            with k.scope():
                TC = min(LRU_TC, Tn)
                nsub = Tn // TC

                def chain(cid, n):
                    rows = slice(n * 128, (n + 1) * 128)
                    for d in range(2):
                        order = range(nsub) if d == 0 else range(nsub - 1, -1, -1)
                        for si in order:
                            yield from _lru_sub(k, sq, n, d, si * TC, TC, Tn, rows, cw, cb, gw, gbias, coef, coef2, st, scr, cid)

                for n0 in range(0, LRU_NB, LRU_NCH):
                    gens = [chain(ci, n0 + ci) for ci in range(min(LRU_NCH, LRU_NB - n0))]
                    live = list(range(len(gens)))
                    step = 0
                    while live:
                        for ci in list(live):
                            if step < ci:
                                continue
                            try:
                                next(gens[ci])
                            except StopIteration:
                                live.remove(ci)
                        step += 1
            if DBG.get("stop") == "p2":
                continue
            if sq["want_out"]:
                pass_out(k, YG, LRU_NB, w_out, Tn, sq["src"], sq["M"], sq["dst"])


def _lru_sub(k, sq, n, d, t0, TC, Tn, rows, cw, cb, gw, gbias, coef, coef2, st, scr, cid):
    nc, S = k.nc, k.S
    HT, UT, SZ, UC, HF, YG = (scr[x] for x in ("HT", "UT", "SZ", "UC", "HF", "YG"))
    W_ = LRU_TC
    ucf = k.rot(f"l2_uc_{cid}", [128, W_], F32, n=1)
    if d == 0:
        uh = k.rot(f"l2_uh_{cid}", [128, W_ + 3], F32, n=1)
        lo = max(t0 - 2, 0)
        hi = min(t0 + TC + 1, Tn)
        if t0 == 0:
            k.memset("pool", uh[:, 0:2], 0.0, [uh])
        if t0 + TC == Tn:
            k.memset("pool", uh[:, TC + 2:TC + 3], 0.0, [uh])
        k.dma(uh[:, lo - (t0 - 2):hi - (t0 - 2)], UT.t[rows, lo:hi], [UT], [uh])
        k.ts("dve", ucf[:, 0:TC], uh[:, 2:2 + TC], cw[:, 2, n:n + 1], cb[:, n:n + 1], ALU.mult, ALU.add,
             [uh, cw, cb], [ucf])
        for tap in (0, 1, 3):
            k.stt(ucf[:, 0:TC], uh[:, tap:tap + TC], cw[:, tap, n:n + 1], ucf[:, 0:TC], ALU.mult, ALU.add,
                  [uh, cw, ucf], [ucf])
        k.dma(UC.t[rows, t0:t0 + TC], ucf[:, 0:TC], [ucf], [UC], q="pool")
    else:
        k.dma(ucf[:, 0:TC], UC.t[rows, t0:t0 + TC], [UC], [ucf])
    yield
    ucb = k.rot(f"l2_ucb_{cid}", [128, W_], BF16, n=1)
    k.copy("pool", ucb[:, 0:TC], ucf[:, 0:TC], [ucf], [ucb])
    r = k.rot(f"l2_r_{cid}", [128, W_], F32, n=1)
    ig = k.rot(f"l2_i_{cid}", [128, W_], F32, n=1)
    for c0 in range(0, TC, 512):
        cl = min(512, TC - c0)
        for g, dstb in ((0, r), (1, ig)):
            ps = k.psum()
            gi = (d * 2 + g) * LRU_NB + n
            k.mm(ps[:, 0:cl], gw[:, gi, :], ucb[:, c0:c0 + cl], True, True, [gw, ucb], [ps])
            k.act(dstb[:, c0:c0 + cl], ps[:, 0:cl], AF.Sigmoid, [ps, gbias], [dstb],
                  bias=gbias[:, d * 2 + g, n:n + 1])
    yield
    av = k.rot(f"l2_a_{cid}", [128, W_], F32, n=1)
    k.act(av[:, 0:TC], r[:, 0:TC], AF.Exp, [r, coef], [av], scale=coef[:, d, n:n + 1])
    a2 = k.rot(f"l2_a2_{cid}", [128, W_], F32, n=1)
    k.tt("pool", a2[:, 0:TC], av[:, 0:TC], av[:, 0:TC], ALU.mult, [av], [a2])
    k.act(a2[:, 0:TC], a2[:, 0:TC], AF.Sqrt, [a2, k.oneb], [a2], bias=k.oneb[:, 0:1], scale=-1.0)
    yield
    k.tt("pool", ig[:, 0:TC], ig[:, 0:TC], a2[:, 0:TC], ALU.mult, [ig, a2], [ig])
    k.tt("dve", ig[:, 0:TC], ig[:, 0:TC], ucf[:, 0:TC], ALU.mult, [ig, ucf], [ig])
    yield
    hv = k.rot(f"l2_h_{cid}", [128, W_], F32, n=1)
    if d == 0:
        S.op("dve", lambda: nc.vector.tensor_tensor_scan(
            out=hv[:, 0:TC], data0=av[:, 0:TC], data1=ig[:, 0:TC], initial=st[n][:, 0:1],
            op0=ALU.mult, op1=ALU.add), reads=[av, ig, st[n]], writes=[hv])
        k.copy("dve", st[n][:, 0:1], hv[:, TC - 1:TC], [hv], [st[n]])
        k.dma(HF.t[rows, t0:t0 + TC], hv[:, 0:TC], [hv], [HF], q="pool")
    else:
        S.op("dve", lambda: nc.vector.tensor_tensor_scan(
            out=hv[:, TC - 1::-1], data0=av[:, TC - 1::-1], data1=ig[:, TC - 1::-1], initial=st[n][:, 1:2],
            op0=ALU.mult, op1=ALU.add), reads=[av, ig, st[n]], writes=[hv])
        k.copy("dve", st[n][:, 1:2], hv[:, 0:1], [hv], [st[n]])
        yield
        if sq["want_out"]:
            hf = k.rot(f"l2_hf_{cid}", [128, W_], F32, n=1)
            k.dma(hf[:, 0:TC], HF.t[rows, t0:t0 + TC], [HF], [hf])
            szt = k.rot(f"l2_sz_{cid}", [128, W_], BF16, n=1)
            k.dma(szt[:, 0:TC], SZ.t[rows, t0:t0 + TC], [SZ], [szt])
            k.tt("pool", hv[:, 0:TC], hv[:, 0:TC], hf[:, 0:TC], ALU.add, [hv, hf], [hv])
            ygt = k.rot(f"l2_yg_{cid}", [128, W_], BF16, n=1)
            k.tt("dve", ygt[:, 0:TC], hv[:, 0:TC], szt[:, 0:TC], ALU.mult, [hv, szt], [ygt])
            k.dma(YG.t[rows, t0:t0 + TC], ygt[:, 0:TC], [ygt], [YG], q="pool")
    yield


ML_H = 8
ML_DK = 128
ML_DV = 256
ML_W = 2048
ML_NIN = 6176


def mlstm_layer(k, li, jl, W, seqs, scr):
    nc, S = k.nc, k.S
    HT, HFT, SZT = scr["HT"], scr["HFT"], scr["SZT"]
    win = W["mlstm_w_in"]
    for sq in seqs:
        pass_norm(k, sq["src"], sq["Tn"], sq["M"], sq["HTs"])
    with k.scope():
        wz = load_w_bf16(k, f"ml_wz{li}", lambda j, c0, sc: win.t[jl, j * 128:(j + 1) * 128, 4096 + c0:4096 + c0 + sc],
                         win, KD, ML_W, stage_cols=1024)
        ng = S.sb([128, ML_W], F32, name="ml_ng")
        k.dma(ng[:], W["mlstm_norm_g"].t[jl].partition_broadcast(128), [W["mlstm_norm_g"]], [ng])
        for sq in seqs:
            if not sq["want_out"]:
                continue
            Tn = sq["Tn"]
            for t0 in range(0, Tn, 128):
                hT = k.rot("mz_hT", [128, 8, 128], BF16, n=2)
                k.dma(hT[:], sq["HTs"].t[:, t0:t0 + 128].rearrange("(j p) t -> p j t", p=128), [sq["HTs"]], [hT])
                szg = k.rot("mz_szg", [128, ML_W], BF16, n=2)
                for nb in range(4):
                    ps = k.psum()
                    for j in range(KD):
                        k.mm(ps[:], hT[:, j, :], wz[:, j, nb * 512:(nb + 1) * 512], j == 0, j == KD - 1, [hT, wz], [ps])
                    tmp = k.rot("mz_tmp", [128, 512], F32, n=2)
                    k.act(tmp[:], ps[:], AF.Silu, [ps], [tmp])
                    k.tt("dve", szg[:, nb * 512:(nb + 1) * 512], tmp[:], ng[:, nb * 512:(nb + 1) * 512], ALU.mult,
                         [tmp, ng], [szg])
                k.dma(sq["SZs"].t[t0:t0 + 128, :], szg[:], [szg], [sq["SZs"]], q="pool")
    with k.scope():
        def wsrc(j, c0, sc):
            if c0 < 4096:
                return win.t[jl, j * 128:(j + 1) * 128, c0:c0 + sc]
            return win.t[jl, j * 128:(j + 1) * 128, 6144:6176]
        wq = S.sb([128, KD, 4128], BF16, name=f"ml_wq{li}")
        with k.scope():
            ii = 0
            for j in range(KD):
                for c0 in range(0, 4096, 1024):
                    stg = k.rot("mlstage", [128, 1024], F32, n=2)
                    k.dma(stg[:], win.t[jl, j * 128:(j + 1) * 128, c0:c0 + 1024], [win], [stg])
                    k.copy(("dve", "pool", "act")[ii % 3], wq[:, j, c0:c0 + 1024], stg[:], [stg], [wq])
                    ii += 1
                stg = k.rot("mlstage", [128, 1024], F32, n=2)
                k.dma(stg[:, 0:32], win.t[jl, j * 128:(j + 1) * 128, 6144:6176], [win], [stg])
                k.copy("dve", wq[:, j, 4096:4128], stg[:, 0:32], [stg], [wq])
        gbb = S.sb([128, 32], F32, name="ml_gbb")
        k.dma(gbb[:], W["mlstm_gate_b"].t[jl].rearrange("a b c -> (a b c)").partition_broadcast(128),
              [W["mlstm_gate_b"]], [gbb])
        Cf = [[S.sb([128, 257], F32, name=f"ml_Cf{d}_{h}") for h in range(ML_H)] for d in range(2)]
        Cb = [[S.sb([128, 257], BF16, name=f"ml_Cb{d}_{h}") for h in range(ML_H)] for d in range(2)]
        for d in range(2):
            for h in range(ML_H):
                k.memset("pool", Cf[d][h][:], 0.0, [Cf[d][h]])
                k.memset("pool", Cb[d][h][:], 0.0, [Cb[d][h]])
        for sq in seqs:
            for d in range(2):
                _mlstm_sweep(k, sq, d, wq, gbb, Cf[d], Cb[d], scr)
    with k.scope():
        w_out = load_w_bf16(k, f"ml_wout{li}", lambda j, c0, sc: W["mlstm_w_out"].t[jl, j * 128:(j + 1) * 128, c0:c0 + sc],
                            W["mlstm_w_out"], 16, D)
        for sq in seqs:
            if sq["want_out"]:
                pass_out(k, sq["YGs"], 16, w_out, sq["Tn"], sq["src"], sq["M"], sq["dst"])


def _ml_state(k, Kh, Vp, Cf, Cb, bg):
    for h in range(ML_H):
        psC = k.psum()
        k.mm(psC[:, 0:257], Kh[:, h, :], Vp[:, h, :], True, True, [Kh, Vp], [psC])
        k.stt(Cf[h][:], Cf[h][:], bg[:, 8 + h:9 + h], psC[:, 0:257], ALU.mult, ALU.add, [Cf[h], bg, psC], [Cf[h]])
        k.copy("act" if h % 2 == 0 else "pool", Cb[h][:], Cf[h][:], [Cf[h]], [Cb[h]])


def _mlstm_sweep(k, sq, d, wq, gbb, Cf, Cb, scr):
    nc, S = k.nc, k.S
    Tn = sq["Tn"]
    nch = Tn // 128
    HTs, HFs, SZs, YGs = sq["HTs"], sq["HFs"], sq["SZs"], sq["YGs"]
    tri = k.m_le if d == 0 else k.m_ge
    order = range(nch) if d == 0 else range(nch - 1, -1, -1)
    fin = (d == 1) and sq["want_out"]
    dkscale = ML_DK ** -0.5

    def ones_col(b):
        k.memset("pool", b[:, :, 256:257], 1.0, [b])

    for c in order:
        t0 = c * 128
        hT = k.rot("ms_hT", [128, 8, 128], BF16, n=2)
        k.dma(hT[:], HTs.t[:, t0:t0 + 128].rearrange("(j p) t -> p j t", p=128), [HTs], [hT])
        psg = k.psum()
        for j in range(KD):
            k.mm(psg[:, 0:32], hT[:, j, :], wq[:, j, 4096:4128], j == 0, j == KD - 1, [hT, wq], [psg])
        g = k.rot("ms_g", [128, 32], F32, n=2)
        k.tt("dve", g[:], psg[:, 0:32], gbb[:], ALU.add, [psg, gbb], [g])
        gi = g[:, d * 16:d * 16 + 8]
        gf = g[:, d * 16 + 8:d * 16 + 16]
        ef = k.rot("ms_ef", [128, 8], F32, n=2)
        k.act(ef[:], gf, AF.Exp, [g], [ef], scale=-1.0)
        lfn = k.rot("ms_lfn", [128, 8], F32, n=2)
        k.act(lfn[:], ef[:], AF.Ln, [ef, k.oneb], [lfn], bias=k.oneb[:, 0:1])
        qT = k.rot("ms_qT", [128, 8, 128], BF16, n=2)
        kT = k.rot("ms_kT", [128, 8, 128], BF16, n=2)
        for (dst, col0, scl) in ((qT, 0, None), (kT, 1024, dkscale)):
            for hb in range(2):
                ps = k.psum()
                for hh in range(4):
                    h = hb * 4 + hh
                    for j in range(KD):
                        k.mm(ps[:, hh * 128:(hh + 1) * 128], wq[:, j, col0 + h * 128:col0 + (h + 1) * 128], hT[:, j, :],
                             j == 0, j == KD - 1, [wq, hT], [ps])
                dv = dst[:, hb * 4:(hb + 1) * 4, :]
                sv = ps[:, :].rearrange("p (h t) -> p h t", h=4)
                if scl is None:
                    k.copy("act", dv, sv, [ps], [dst])
                else:
                    k.ts("dve", dv, sv, scl, None, ALU.mult, None, [ps], [dst])
        psF = k.psum()
        k.mm(psF[:, 0:8], tri[:], lfn[:], True, True, [tri, lfn], [psF])
        k.mm(psF[:, 8:16], k.ones[:, 0:128], lfn[:], True, True, [k.ones, lfn], [psF])
        bg = k.rot("ms_bg", [128, 16], F32, n=2)
        k.act(bg[:], psF[:, 0:16], AF.Exp, [psF], [bg], scale=-1.0)
        al = k.rot("ms_al", [128, 8], F32, n=2)
        k.tt("dve", al[:], psF[:, 0:8], gi, ALU.add, [psF, g], [al])
        k.act(al[:], al[:], AF.Exp, [al], [al])
        om = k.rot("ms_om", [128, 8], F32, n=2)
        k.stt(om[:], al[:], dkscale, bg[:, 8:16], ALU.mult, ALU.mult, [al, bg], [om])
        Kh = k.rot("ms_Kh", [128, 8, 128], BF16, n=2)
        for hb in range(2):
            ps = k.psum()
            for j in range(KD):
                k.mm(ps[:], hT[:, j, :], wq[:, j, 1024 + hb * 512:1024 + (hb + 1) * 512], j == 0, j == KD - 1, [hT, wq], [ps])
            for hh in range(4):
                h = hb * 4 + hh
                if hh % 2 == 0:
                    k.ts("dve", Kh[:, h, :], ps[:, hh * 128:(hh + 1) * 128], om[:, h:h + 1], None, ALU.mult, None,
                         [ps, om], [Kh])
                else:
                    k.act(Kh[:, h, :], ps[:, hh * 128:(hh + 1) * 128], AF.Identity, [ps, om], [Kh], scale=om[:, h:h + 1])
        Vp = k.rot("ms_Vp", [128, 8, 257], BF16, n=2, init=ones_col)
        for vb in range(4):
            ps = k.psum()
            for j in range(KD):
                k.mm(ps[:], hT[:, j, :], wq[:, j, 2048 + vb * 512:2048 + (vb + 1) * 512], j == 0, j == KD - 1, [hT, wq], [ps])
            dv = Vp[:, vb * 2:(vb + 1) * 2, 0:256]
            sv = ps[:, :].rearrange("p (h v) -> p h v", h=2)
            k.copy("act" if vb % 2 == 0 else "dve", dv, sv, [ps], [Vp])
        psden = k.psum()
        psH = [None] * 4
        Ps = []
        mask = k.m_le if d == 0 else k.m_ge
        for h in range(ML_H):
            if h % 4 == 0:
                psS = k.psum()
            cs = slice((h % 4) * 128, (h % 4 + 1) * 128)
            k.mm(psS[:, cs], kT[:, h, :], qT[:, h, :], True, True, [kT, qT], [psS])
            P = k.rot("ms_P", [128, 128], BF16, n=4)
            k.stt(P[:], psS[:, cs], al[:, h:h + 1], mask[:], ALU.mult, ALU.mult, [psS, al, mask], [P])
            if h % 2 == 0:
                psH[h // 2] = k.psum()
            pH = psH[h // 2]
            hs_ = slice((h % 2) * 256, (h % 2 + 1) * 256)
            k.mm(pH[:, hs_], qT[:, h, :], Cb[h][:, 0:256], True, False, [qT, Cb[h]], [pH])
            k.mm(pH[:, hs_], P[:], Vp[:, h, 0:256], False, True, [P, Vp], [pH])
            k.mm(psden[:, h:h + 1], qT[:, h, :], Cb[h][:, 256:257], True, False, [qT, Cb[h]], [psden])
            k.mm(psden[:, h:h + 1], P[:], Vp[:, h, 256:257], False, True, [P, Vp], [psden])
        dd = k.rot("ms_dd", [128, 8], F32, n=2)
        k.tt("dve", dd[:], psden[:, 0:8], bg[:, 0:8], ALU.mult, [psden, bg], [dd])
        k.act(dd[:], dd[:], AF.Abs, [dd], [dd])
        k.ts("dve", dd[:], dd[:], 1.0, None, ALU.max, None, [dd], [dd])
        k.recip(dd[:], dd[:], [dd], [dd])
        scl = k.rot("ms_scl", [128, 8], F32, n=2)
        k.tt("dve", scl[:], dd[:], bg[:, 0:8], ALU.mult, [dd, bg], [scl])
        hd = k.rot("ms_hd", [128, ML_W], F32, n=1)
        if d == 0 or not fin:
            for h in range(ML_H):
                pH = psH[h // 2]
                hs_ = slice((h % 2) * 256, (h % 2 + 1) * 256)
                k.act(hd[:, h * 256:(h + 1) * 256], pH[:, hs_], AF.Identity, [pH, scl], [hd], scale=scl[:, h:h + 1])
            if d == 0:
                k.dma(HFs.t[t0:t0 + 128, :], hd[:], [hd], [HFs], q="pool")
            _ml_state(k, Kh, Vp, Cf, Cb, bg)
        else:
            hf = k.rot("ms_hf", [128, ML_W], F32, n=1)
            k.dma(hf[:], HFs.t[t0:t0 + 128, :], [HFs], [hf])
            szg = k.rot("ms_szg", [128, ML_W], BF16, n=1)
            k.dma(szg[:], SZs.t[t0:t0 + 128, :], [SZs], [szg])
            for h in range(ML_H):
                pH = psH[h // 2]
                hs_ = slice((h % 2) * 256, (h % 2 + 1) * 256)
                k.stt(hd[:, h * 256:(h + 1) * 256], pH[:, hs_], scl[:, h:h + 1], hf[:, h * 256:(h + 1) * 256],
                      ALU.mult, ALU.add, [pH, scl, hf], [hd])
            _ml_state(k, Kh, Vp, Cf, Cb, bg)
            sqv = k.rot("ms_sq", [128, ML_W], F32, n=1)
            k.tt("pool", sqv[:], hd[:], hd[:], ALU.mult, [hd], [sqv])
            ss = k.rot("ms_ss", [128, 8], F32, n=2)
            S.op("dve", lambda: nc.vector.tensor_reduce(out=ss[:], in_=sqv[:, :].rearrange("p (h v) -> p h v", h=8),
                                                        axis=AX.X, op=ALU.add), reads=[sqv], writes=[ss])
            rs = rstd_of(k, ss, float(ML_DV), NORM_EPS, 8)
            yt = k.rot("ms_yt", [128, ML_W], F32, n=1)
            for h in range(ML_H):
                k.stt(yt[:, h * 256:(h + 1) * 256], hd[:, h * 256:(h + 1) * 256], rs[:, h:h + 1], szg[:, h * 256:(h + 1) * 256],
                      ALU.mult, ALU.mult, [hd, rs, szg], [yt])
            yT = k.rot("ms_yT", [128, 16, 128], BF16, n=2)
            for tb in range(4):
                ps = k.psum()
                for jj in range(4):
                    j = tb * 4 + jj
                    k.tr(ps[:, jj * 128:(jj + 1) * 128], yt[:, j * 128:(j + 1) * 128], k.ident[:], [yt, k.ident], [ps])
                k.copy("act" if tb % 2 == 0 else "dve", yT[:, tb * 4:(tb + 1) * 4, :],
                       ps[:, :].rearrange("p (j t) -> p j t", j=4), [ps], [yT])
            k.dma(YGs.t[:, t0:t0 + 128].rearrange("(j p) t -> p j t", p=128), yT[:], [yT], [YGs], q="pool")
R7_L = 128
C_ED = float(np.exp(-0.5))


def _r7_load_h_sh(k, sq, t0, want_sh=True):
    Tn = sq["Tn"]
    HTs = sq["HTs"]
    hb = k.rot("r7_h", [128, 8, 128], BF16, n=1)
    k.dma(hb[:], HTs.t[:, t0:t0 + 128].rearrange("(j p) t -> p j t", p=128), [HTs], [hb])
    sh = k.rot("r7_sh", [128, 8, 128], BF16, n=1)

    def ld(j0, j1, c0, c1, s0):
        k.dma(sh[:, j0:j1, c0:c1], HTs.t[j0 * 128:j1 * 128, s0:s0 + (c1 - c0)].rearrange("(j p) t -> p j t", p=128),
              [HTs], [sh])

    def z(j0, j1, c0, c1):
        k.memset("pool", sh[:, j0:j1, c0:c1], 0.0, [sh])

    if sq["name"] == "x":
        ld(0, 2, 1, 128, t0)
        z(0, 2, 0, 1)
        z(0, 2, 64, 65)
        ld(2, 4, 0, 127, t0 + 1)
        z(2, 4, 63, 64)
        z(2, 4, 127, 128)
        if t0 >= 64:
            ld(4, 6, 0, 128, t0 - 64)
        else:
            ld(4, 6, 64, 128, 0)
            z(4, 6, 0, 64)
        if t0 + 192 <= Tn:
            ld(6, 8, 0, 128, t0 + 64)
        else:
            ld(6, 8, 0, 64, t0 + 64)
            z(6, 8, 64, 128)
    else:
        if t0 > 0:
            ld(0, 4, 0, 128, t0 - 1)
        else:
            ld(0, 4, 1, 128, 0)
            z(0, 4, 0, 1)
        if t0 + 129 <= Tn:
            ld(4, 8, 0, 128, t0 + 1)
        else:
            ld(4, 8, 0, 127, t0 + 1)
            z(4, 8, 127, 128)
    return hb, sh


def _bc(ap2, n):
    return ap2.unsqueeze(2).to_broadcast([128, 8, n])


def rwkv_layer(k, li, jl, W, seqs, scr):
    nc, S = k.nc, k.S
    for sq in seqs:
        pass_norm(k, sq["src"], sq["Tn"], sq["M"], sq["HTs"])
    wr = W["r7_w_rkvz"]
    with k.scope():
        def pload(name, ap, shape):
            b = S.sb(shape, F32, name=name)
            k.dma(b[:], ap, [W["r7_mu"]], [b], slow=True)
            return b
        mu = pload("r7_mu", W["r7_mu"].t[jl].rearrange("g (j p) -> p g j", p=128), [128, 6, 8])
        w0 = pload("r7_w0", W["r7_w0"].t[jl].rearrange("d (j p) -> p d j", p=128), [128, 2, 8])
        a0 = pload("r7_a0", W["r7_a0"].t[jl].rearrange("d (j p) -> p d j", p=128), [128, 2, 8])
        kkp = pload("r7_kk", W["r7_k_k"].t[jl].rearrange("(j p) -> p j", p=128), [128, 8])
        kap = pload("r7_ka", W["r7_k_a"].t[jl].rearrange("(j p) -> p j", p=128), [128, 8])
        rkp = pload("r7_rk", W["r7_r_k"].t[jl].rearrange("(j p) -> p j", p=128), [128, 8])
        omka = S.sb([128, 8], F32, name="r7_omka")
        k.ts("dve", omka[:], kap[:], -1.0, 1.0, ALU.mult, ALU.add, [kap], [omka])
        with k.scope():
            wz = load_w_bf16(k, "r7_wz", lambda j, c0, sc: wr.t[jl, 3, j * 128:(j + 1) * 128, c0:c0 + sc], wr, KD, D)
            for sq in seqs:
                if not sq["want_out"]:
                    continue
                for t0 in range(0, sq["Tn"], 128):
                    hb, sh = _r7_load_h_sh(k, sq, t0)
                    df = k.rot("rz_df", [128, 8, 128], F32, n=2)
                    k.tt("dve", df[:], sh[:], hb[:], ALU.subtract, [sh, hb], [df])
                    k.tt("pool", df[:], df[:], _bc(mu[:, 3, :], 128), ALU.mult, [df, mu], [df])
                    xz = k.rot("rz_xz", [128, 8, 128], BF16, n=2)
                    k.tt("dve", xz[:], df[:], hb[:], ALU.add, [df, hb], [xz])
                    szt = k.rot("rz_sz", [128, D], BF16, n=2)
                    for nb in range(2):
                        ps = k.psum()
                        for j in range(KD):
                            k.mm(ps[:], xz[:, j, :], wz[:, j, nb * 512:(nb + 1) * 512], j == 0, j == KD - 1, [xz, wz], [ps])
                        k.act(szt[:, nb * 512:(nb + 1) * 512], ps[:], AF.Silu, [ps], [szt])
                    k.dma(sq["SZs"].t[t0:t0 + 128, 0:D], szt[:], [szt], [sq["SZs"]], q="pool")
        with k.scope():
            wts = {}
            for gi, nm in ((0, "r"), (1, "k"), (2, "v")):
                wts[nm] = load_w_bf16(k, "r7_w" + nm, lambda j, c0, sc, gi=gi: wr.t[jl, gi, j * 128:(j + 1) * 128, c0:c0 + sc],
                                      wr, KD, D)
            lw1 = [load_w_bf16(k, f"r7_w1{d}", lambda j, c0, sc, d=d: W["r7_w1"].t[jl, d, j * 128:(j + 1) * 128, :], W["r7_w1"], KD, 64)
                   for d in range(2)]
            la1 = [load_w_bf16(k, f"r7_a1{d}", lambda j, c0, sc, d=d: W["r7_a1"].t[jl, d, j * 128:(j + 1) * 128, :], W["r7_a1"], KD, 64)
                   for d in range(2)]
            lw2 = [S.sb([64, D], BF16, name=f"r7_w2_{d}") for d in range(2)]
            la2 = [S.sb([64, D], BF16, name=f"r7_a2_{d}") for d in range(2)]
            with k.scope():
                for d in range(2):
                    for (lst, nm) in ((lw2, "r7_w2"), (la2, "r7_a2")):
                        stg = k.rot("r7_stg2", [64, D], F32, n=2)
                        k.dma(stg[:], W[nm].t[jl, d], [W[nm]], [stg])
                        k.copy("dve", lst[d][:], stg[:], [stg], [lst[d]])
            hmA = S.sb([128, 1], F32, name="r7_hmA")
            hmB = S.sb([128, 1], F32, name="r7_hmB")
            k.memset("pool", hmA[:], 0.0, [hmA])
            k.memset("pool", hmA[0:64, :], 1.0, [hmA])
            k.memset("pool", hmB[:], 1.0, [hmB])
            k.memset("pool", hmB[0:64, :], 0.0, [hmB])
            hsel = S.sb([128, 2], F32, name="r7_hsel")
            k.copy("pool", hsel[:, 0:1], hmA[:], [hmA], [hsel])
            k.copy("pool", hsel[:, 1:2], hmB[:], [hmB], [hsel])
            bones = S.sb([128, 128], F32, name="r7_bones")
            k.memset("pool", bones[:], 0.0, [bones])
            k.memset("pool", bones[0:64, 0:64], 1.0, [bones])
            k.memset("pool", bones[64:128, 64:128], 1.0, [bones])
            id4 = S.sb([128, 4, 128], BF16, name="r7_id4")
            for i4 in range(4):
                k.copy("pool", id4[:, i4, :], k.identb[:], [k.identb], [id4])
            smt = S.sb([128, 2, 7, 128], BF16, name="r7_smt")
            with k.scope():
                for l in range(7):
                    w2, w1 = 1 << (l + 1), 1 << l

                    def band(lo_off, hi_off, nm, w2=w2):
                        b = k.rot("r7_band_" + nm, [128, 128], F32, n=2)
                        S.op("pool", lambda: nc.gpsimd.affine_select(out=b[:], in_=k.ones[:, 0:128], pattern=[[1, 128]],
                                                                     compare_op=ALU.is_ge, fill=0.0, base=-lo_off, channel_multiplier=-w2),
                             reads=[k.ones], writes=[b])
                        S.op("pool", lambda: nc.gpsimd.affine_select(out=b[:], in_=b[:], pattern=[[-1, 128]],
                                                                     compare_op=ALU.is_gt, fill=0.0, base=hi_off, channel_multiplier=w2),
                             reads=[b], writes=[b])
                        return b
                    Lf = band(0, w1, "L")
                    Rt = band(w1, w2, "R")
                    for dd, (lt_, rh_) in ((0, (Rt, Lf)), (1, (Lf, Rt))):
                        ps = k.psum()
                        k.mm(ps[:, 0:128], lt_[:], rh_[:], True, True, [lt_, rh_], [ps])
                        k.copy("act", smt[:, dd, l, :], ps[:, 0:128], [ps], [smt])
            Z = [S.sb([128, 8, 64], F32, name=f"r7_Z{d}") for d in range(2)]
            Zb = [S.sb([128, 8, 64], BF16, name=f"r7_Zb{d}") for d in range(2)]
            for d in range(2):
                k.memset("pool", Z[d][:], 0.0, [Z[d]])
                k.memset("pool", Zb[d][:], 0.0, [Zb[d]])
            lng = S.sb([128, D], F32, name="r7_lng")
            k.dma(lng[:], W["r7_ln_g"].t[jl].partition_broadcast(128), [W["r7_ln_g"]], [lng])
            lnb = S.sb([128, D], F32, name="r7_lnb")
            k.dma(lnb[:], W["r7_ln_b"].t[jl].partition_broadcast(128), [W["r7_ln_b"]], [lnb])
            cst = dict(mu=mu, w0=w0, a0=a0, kkp=kkp, kap=kap, rkp=rkp, omka=omka, wts=wts, lw1=lw1, la1=la1, lw2=lw2, la2=la2,
                       hm=(hmA, hmB), hsel=hsel, bones=bones, lng=lng, lnb=lnb, id4=id4, smt=smt,
                       RS=dict(r=scr["R7r"], k=scr["R7k"], kk=scr["R7kk"], v=scr["R7v"]))
            for sq in seqs:
                for d in range(2):
                    nch = sq["Tn"] // 128
                    order = range(nch) if d == 0 else range(nch - 1, -1, -1)
                    for c in order:
                        _r7_chunk(k, sq, d, c * 128, cst, Z[d], Zb[d])
        with k.scope():
            w_out = load_w_bf16(k, "r7_wout", lambda j, c0, sc: W["r7_w_out"].t[jl, j * 128:(j + 1) * 128, c0:c0 + sc],
                                W["r7_w_out"], KD, D)
            for sq in seqs:
                if sq["want_out"]:
                    pass_out(k, sq["YGs"], KD, w_out, sq["Tn"], sq["src"], sq["M"], sq["dst"])


def _r7_chunk(k, sq, d, t0, c, Z, Zb):
    nc, S = k.nc, k.S
    mu, wts = c["mu"], c["wts"]
    fin = (d == 1) and sq["want_out"]
    F = lambda i: k.rot(f"r7_F{i}", [128, 8, 128], F32, n=1)
    hb, sh = _r7_load_h_sh(k, sq, t0)
    df = F(0)
    k.tt("dve", df[:], sh[:], hb[:], ALU.subtract, [sh, hb], [df])

    def lerp(g, eng="dve"):
        t = F(1)
        k.tt("pool", t[:], df[:], _bc(mu[:, g, :], 128), ALU.mult, [df, mu], [t])
        o = k.rot("r7_xs", [128, 8, 128], BF16, n=2)
        k.tt(eng, o[:], t[:], hb[:], ALU.add, [t, hb], [o])
        return o

    def proj_fm(xs, wt, evac):
        for jb in range(2):
            ps = k.psum()
            for jj in range(4):
                j = jb * 4 + jj
                for kk_ in range(KD):
                    k.mm(ps[:, jj * 128:(jj + 1) * 128], wt[:, kk_, j * 128:(j + 1) * 128], xs[:, kk_, :], kk_ == 0, kk_ == KD - 1,
                         [wt, xs], [ps])
            evac(ps, jb)

    def v4(ps):
        return ps[:, :].rearrange("p (j t) -> p j t", j=4)

    if DBG.get("r7") == "a":
        return
    rT = F(2)
    kT = F(3)
    Vb = k.rot("r7_Vb", [128, D], BF16, n=1)
    RS = c["RS"]
    reuse = (d == 1)
    rows = slice(t0, t0 + 128)
    if not reuse:
        xr = lerp(0)
        proj_fm(xr, wts["r"], lambda ps, jb: k.copy("act", rT[:, jb * 4:(jb + 1) * 4, :], v4(ps), [ps], [rT]))
        xk = lerp(1)
        proj_fm(xk, wts["k"], lambda ps, jb: k.copy("dve", kT[:, jb * 4:(jb + 1) * 4, :], v4(ps), [ps], [kT]))
        xv = lerp(2)
        for nb in range(2):
            ps = k.psum()
            for kk_ in range(KD):
                k.mm(ps[:], xv[:, kk_, :], wts["v"][:, kk_, nb * 512:(nb + 1) * 512], kk_ == 0, kk_ == KD - 1, [xv, wts["v"]], [ps])
            k.copy("act", Vb[:, nb * 512:(nb + 1) * 512], ps[:], [ps], [Vb])
        k.dma(RS["r"].t[rows, :].rearrange("p (j t) -> p j t", j=8), rT[:], [rT], [RS["r"]], q="pool")
        k.dma(RS["k"].t[rows, :].rearrange("p (j t) -> p j t", j=8), kT[:], [kT], [RS["k"]], q="pool")
        k.dma(RS["v"].t[rows, :], Vb[:], [Vb], [RS["v"]], q="pool")
    if DBG.get("r7") == "b":
        return
    lows = []
    for (g, w1, fn) in ((4, c["lw1"][d], AF.Tanh), (5, c["la1"][d], AF.Identity)):
        xs_ = lerp(g)
        ps = k.psum()
        for kk_ in range(KD):
            k.mm(ps[0:64, 0:128], w1[:, kk_, :], xs_[:, kk_, :], kk_ == 0, kk_ == KD - 1, [w1, xs_], [ps])
        lo = k.rot("r7_lo", [64, 128], BF16, n=2)
        k.act(lo[:], ps[0:64, 0:128], fn, [ps], [lo])
        lows.append(lo)
    if reuse:
        k.dma(rT[:], RS["r"].t[rows, :].rearrange("p (j t) -> p j t", j=8), [RS["r"]], [rT])
        k.dma(kT[:], RS["k"].t[rows, :].rearrange("p (j t) -> p j t", j=8), [RS["k"]], [kT])
        k.dma(Vb[:], RS["v"].t[rows, :], [RS["v"]], [Vb])
    lwv = F(4)
    aT = F(5)
    for (lo, w2, dst, bias) in ((lows[0], c["lw2"][d], lwv, c["w0"]), (lows[1], c["la2"][d], aT, c["a0"])):
        for jb in range(2):
            ps = k.psum()
            for jj in range(4):
                j = jb * 4 + jj
                k.mm(ps[:, jj * 128:(jj + 1) * 128], w2[:, j * 128:(j + 1) * 128], lo[:], True, True, [w2, lo], [ps])
            for jj in range(4):
                j = jb * 4 + jj
                k.act(dst[:, j, :], ps[:, jj * 128:(jj + 1) * 128], AF.Sigmoid, [ps, bias], [dst], bias=bias[:, d, j:j + 1])
    k.ts("pool", lwv[:], lwv[:], -C_ED, None, ALU.mult, None, [lwv], [lwv])
    if DBG.get("r7") == "c":
        return
    k.dump("rT", rT[:], [128, 8, 128], F32, [rT])
    k.dump("kT", kT[:], [128, 8, 128], F32, [kT])
    k.dump("lw", lwv[:], [128, 8, 128], F32, [lwv])
    k.dump("aT", aT[:], [128, 8, 128], F32, [aT])
    k.dump("Vf", Vb[:], [128, D], BF16, [Vb])
    kkr = F(0)
    if reuse:
        k.dma(kkr[:], RS["kk"].t[rows, :].rearrange("p (j t) -> p j t", j=8), [RS["kk"]], [kkr])
    if not reuse:
        pass
        k.tt("dve", kkr[:], kT[:], _bc(c["kkp"][:, :], 128), ALU.mult, [kT, c["kkp"]], [kkr])
        kk2 = F(1)
        k.tt("pool", kk2[:], kkr[:], kkr[:], ALU.mult, [kkr], [kk2])
        nrm = F(6)
        for jb in range(2):
            ps = k.psum()
            for jj in range(4):
                j = jb * 4 + jj
                k.mm(ps[:, jj * 128:(jj + 1) * 128], c["bones"][:], kk2[:, j, :], True, True, [c["bones"], kk2], [ps])
            k.act(nrm[:, jb * 4:(jb + 1) * 4, :], v4(ps), AF.Sqrt, [ps], [nrm])
        k.ts("dve", nrm[:], nrm[:], 1e-12, None, ALU.max, None, [nrm], [nrm])
        k.recip(nrm[:], nrm[:], [nrm], [nrm])
        k.tt("pool", kkr[:], kkr[:], nrm[:], ALU.mult, [kkr, nrm], [kkr])
        k.dma(RS["kk"].t[rows, :].rearrange("p (j t) -> p j t", j=8), kkr[:], [kkr], [RS["kk"]], q="pool")
    kd = F(7)
    k.tt("dve", kd[:], aT[:], _bc(c["kap"][:, :], 128), ALU.mult, [aT, c["kap"]], [kd])
    k.tt("pool", kd[:], kd[:], _bc(c["omka"][:, :], 128), ALU.add, [kd, c["omka"]], [kd])
    k.tt("dve", kd[:], kd[:], kT[:], ALU.mult, [kd, kT], [kd])
    if DBG.get("r7") == "d":
        return
    k.dump("kk", kkr[:], [128, 8, 128], F32, [kkr])
    k.dump("kd", kd[:], [128, 8, 128], F32, [kd])
    pr = F(1)
    k.tt("pool", pr[:], rT[:], _bc(c["rkp"][:, :], 128), ALU.mult, [rT, c["rkp"]], [pr])
    k.tt("dve", pr[:], pr[:], kd[:], ALU.mult, [pr, kd], [pr])
    psb = k.psum()
    for j in range(8):
        k.mm(psb[:, 2 * j:2 * j + 2], pr[:, j, :], c["hsel"][:], True, True, [pr, c["hsel"]], [psb])
    bs = k.rot("r7_bs", [128, 16], F32, n=2)
    k.copy("act", bs[:], psb[:, 0:16], [psb], [bs])
    if DBG.get("r7") == "e":
        return
    cs = F(1)
    for j in range(8):
        if d == 0:
            S.op("dve", lambda j=j: nc.vector.tensor_tensor_scan(out=cs[:, j, :], data0=k.ones[:, 0:128], data1=lwv[:, j, :], initial=0.0,
                                                                 op0=ALU.mult, op1=ALU.add), reads=[k.ones, lwv], writes=[cs])
        else:
            S.op("dve", lambda j=j: nc.vector.tensor_tensor_scan(out=cs[:, j, ::-1], data0=k.ones[:, 0:128], data1=lwv[:, j, ::-1],
                                                                 initial=0.0, op0=ALU.mult, op1=ALU.add),
                 reads=[k.ones, lwv], writes=[cs])
    gam = F(3)
    k.act(gam[:], cs[:], AF.Exp, [cs], [gam])
    gin = F(6)
    k.act(gin[:], cs[:], AF.Exp, [cs], [gin], scale=-1.0)
    gpv = lwv
    k.tt("pool", gpv[:], cs[:], lwv[:], ALU.subtract, [cs, lwv], [gpv])
    k.act(gpv[:], gpv[:], AF.Exp, [gpv], [gpv])
    k.dump("cs", cs[:], [128, 8, 128], F32, [cs])
    k.dump("gam", gam[:], [128, 8, 128], F32, [gam])
    k.dump("gin", gin[:], [128, 8, 128], F32, [gin])
    k.dump("gpv", gpv[:], [128, 8, 128], F32, [gpv])
    gL = k.rot("r7_gL", [128, 8], F32, n=2)
    lastcol = 127 if d == 0 else 0
    k.copy("pool", gL[:], gam[:, :, lastcol], [gam], [gL])
    if DBG.get("r7") == "f":
        return
    hmA, hmB = c["hm"]
    rt_ = rT
    k.tt("dve", rt_[:], rT[:], gam[:], ALU.mult, [rT, gam], [rt_])
    at_ = aT
    k.tt("dve", at_[:], kkr[:], aT[:], ALU.mult, [kkr, aT], [at_])
    k.tt("pool", at_[:], at_[:], gin[:], ALU.mult, [at_, gin], [at_])
    kt_ = kkr
    k.tt("pool", kt_[:], kkr[:], gpv[:], ALU.mult, [kkr, gpv], [kt_])
    kdt = kd
    k.tt("dve", kdt[:], kd[:], gin[:], ALU.mult, [kd, gin], [kdt])
    ops = {}
    ei = 0
    for nm, src in (("rt", rt_), ("kt", kt_), ("at", at_), ("kd", kdt)):
        hA = k.rot("r7_bA_" + nm, [128, 8, 128], BF16, n=1)
        hB = k.rot("r7_bB_" + nm, [128, 8, 128], BF16, n=1)
        if ei % 2 == 0:
            k.ts("dve", hA[:], src[:], hmA[:, 0:1], None, ALU.mult, None, [src, hmA], [hA])
            k.act(hB[:], src[:], AF.Identity, [src, hmB], [hB], scale=hmB[:, 0:1])
        else:
            k.act(hA[:], src[:], AF.Identity, [src, hmA], [hA], scale=hmA[:, 0:1])
            k.ts("dve", hB[:], src[:], hmB[:, 0:1], None, ALU.mult, None, [src, hmB], [hB])
        ei += 1
        ops[nm] = (None, hA, hB)
    if DBG.get("r7") == "g":
        return
    toks = {}
    for nm, src in (("at", at_), ("kd", kdt)):
        tk = k.rot("r7_tok_" + nm, [128, 8, 128], BF16, n=1)
        for jb in range(2):
            ps = k.psum()
            for jj in range(4):
                j = jb * 4 + jj
                k.tr(ps[:, jj * 128:(jj + 1) * 128], src[:, j, :], k.ident[:], [src, k.ident], [ps])
            k.copy("act" if jb == 0 else "dve", tk[:, jb * 4:(jb + 1) * 4, :], v4(ps), [ps], [tk])
        toks[nm] = tk
    if DBG.get("r7") == "h":
        return
    if d == 0:
        mb_lt, mb_gt, mb_le = k.m_lt, k.m_gt, k.m_le
    else:
        mb_lt, mb_gt, mb_le = k.m_gt, k.m_lt, k.m_ge
    b4 = lambda t: t[:, :].unsqueeze(1).to_broadcast([128, 4, 128])
    MkT = k.rot("r7_MkT", [128, 16, 128], BF16, n=1)
    NaT = k.rot("r7_NaT", [128, 16, 128], BF16, n=1)
    NkT = k.rot("r7_NkT", [128, 16, 128], BF16, n=1)
    TT = k.rot("r7_TT", [128, 16, 128], BF16, n=1)

    def hop(nm, h):
        return ops[nm][1 + (h % 2)][:, h // 2, :], ops[nm][1 + (h % 2)]

    def fop(nm, h):
        return hop(nm, h)

    ATs = []
    for hg in range(4):
        heads = range(hg * 4, hg * 4 + 4)
        hs4 = slice(hg * 4, hg * 4 + 4)

        def form(lnm, rnm, evac, heads=heads):
            ps = k.psum()
            for i4, h in enumerate(heads):
                la, lb_ = hop(lnm, h)
                ra, rb_ = fop(rnm, h)
                k.mm(ps[:, i4 * 128:(i4 + 1) * 128], la, ra, True, True, [lb_, rb_], [ps])
            evac(ps)
        AT = k.rot("r7_AT", [128, 4, 128], BF16, n=4)
        ATs.append(AT)
        form("kt", "at", lambda ps: k.stt(AT[:], v4(ps), -1.0, b4(mb_gt), ALU.mult, ALU.mult, [ps, mb_gt], [AT]))
        form("kd", "kt", lambda ps: k.tt("dve", MkT[:, hs4, :], v4(ps), b4(mb_lt), ALU.mult, [ps, mb_lt], [MkT]))
        form("at", "rt", lambda ps: k.tt("dve", NaT[:, hs4, :], v4(ps), b4(mb_le), ALU.mult, [ps, mb_le], [NaT]))
        form("kd", "rt", lambda ps: k.tt("dve", NkT[:, hs4, :], v4(ps), b4(mb_le), ALU.mult, [ps, mb_le], [NkT]))
    X = [c["id4"]] * 4
    XT = [c["id4"]] * 4
    for l in range(7):
        CTs, P1s, pss = [], [], []
        for hg in range(4):
            CT = k.rot("r7_CT", [128, 4, 128], BF16, n=4)
            k.tt("pool", CT[:], ATs[hg][:], c["smt"][:, d, l, :].unsqueeze(1).to_broadcast([128, 4, 128]), ALU.mult,
                 [ATs[hg], c["smt"]], [CT])
            CTs.append(CT)
        for hg in range(4):
            ps = k.psum()
            for i4 in range(4):
                k.mm(ps[:, i4 * 128:(i4 + 1) * 128], CTs[hg][:, i4, :], X[hg][:, i4, :], True, True, [CTs[hg], X[hg]], [ps])
            P1 = k.rot("r7_P1", [128, 4, 128], BF16, n=4)
            k.copy("act" if hg % 2 == 0 else "dve", P1[:], v4(ps), [ps], [P1])
            P1s.append(P1)
        for hg in range(4):
            ps = k.psum()
            for i4 in range(4):
                k.mm(ps[:, i4 * 128:(i4 + 1) * 128], k.identb[:], X[hg][:, i4, :], True, False, [k.identb, X[hg]], [ps])
                k.mm(ps[:, i4 * 128:(i4 + 1) * 128], XT[hg][:, i4, :], P1s[hg][:, i4, :], False, True, [XT[hg], P1s[hg]], [ps])
            pss.append(ps)
        Xn_l = []
        for hg in range(4):
            hs4 = slice(hg * 4, hg * 4 + 4)
            if l < 6:
                Xn = k.rot("r7_X", [128, 4, 128], BF16, n=4)
                k.copy("dve" if hg % 2 == 0 else "act", Xn[:], v4(pss[hg]), [pss[hg]], [Xn])
                Xn_l.append(Xn)
            else:
                k.copy("dve" if hg % 2 == 0 else "act", TT[:, hs4, :], v4(pss[hg]), [pss[hg]], [TT])
        if l < 6:
            XTn_l = []
            for hg in range(4):
                ps = k.psum()
                for i4 in range(4):
                    k.mm(ps[:, i4 * 128:(i4 + 1) * 128], Xn_l[hg][:, i4, :], k.identb[:], True, True, [Xn_l[hg], k.identb], [ps])
                XTn = k.rot("r7_XT", [128, 4, 128], BF16, n=4)
                k.copy("act" if hg % 2 == 0 else "dve", XTn[:], v4(ps), [ps], [XTn])
                XTn_l.append(XTn)
            X, XT = Xn_l, XTn_l
    k.dump("TT", TT[:], [128, 16, 128], BF16, [TT])
    k.dump("MkT", MkT[:], [128, 16, 128], BF16, [MkT])
    k.dump("NaT", NaT[:], [128, 16, 128], BF16, [NaT])
    k.dump("NkT", NkT[:], [128, 16, 128], BF16, [NkT])
    def v8(ps):
        return ps[:, :].rearrange("p (h v) -> p h v", h=8)
    Bs = k.rot("r7_Bs", [128, 16, 64], BF16, n=1)
    for hb_ in range(2):
        ps = k.psum()
        for i8 in range(8):
            h = hb_ * 8 + i8
            cs_ = slice(i8 * 64, (i8 + 1) * 64)
            k.mm(ps[:, cs_], MkT[:, h, :], Vb[:, h * 64:(h + 1) * 64], True, False, [MkT, Vb], [ps])
            la, lb_ = hop("kt", h)
            k.mm(ps[:, cs_], la, Zb[:, h // 2, :], False, True, [lb_, Zb], [ps])
        k.copy("act" if hb_ == 0 else "dve", Bs[:, hb_ * 8:(hb_ + 1) * 8, :], v8(ps), [ps], [Bs])
    Us = k.rot("r7_Us", [128, 16, 64], BF16, n=1)
    for hb_ in range(2):
        ps = k.psum()
        for i8 in range(8):
            h = hb_ * 8 + i8
            k.mm(ps[:, i8 * 64:(i8 + 1) * 64], TT[:, h, :], Bs[:, h, :], True, True, [TT, Bs], [ps])
        if hb_ == 0:
            k.act(Us[:, 0:8, :], v8(ps), AF.Identity, [ps], [Us], scale=-1.0)
        else:
            k.ts("dve", Us[:, 8:16, :], v8(ps), -1.0, None, ALU.mult, None, [ps], [Us])
    psY = [k.psum(), k.psum()]
    for h in range(16):
        ps = psY[h // 8]
        cs_ = slice((h % 8) * 64, (h % 8 + 1) * 64)
        la, lb_ = hop("rt", h)
        k.mm(ps[:, cs_], la, Zb[:, h // 2, :], True, False, [lb_, Zb], [ps])
        k.mm(ps[:, cs_], NaT[:, h, :], Us[:, h, :], False, False, [NaT, Us], [ps])
        k.mm(ps[:, cs_], NkT[:, h, :], Vb[:, h * 64:(h + 1) * 64], False, True, [NkT, Vb], [ps])
    yv = F(2)
    yv2 = V2(yv)
    bsum = bs
    if fin:
        yf_ = F(3)
        yf = V2(yf_)
        k.dma(yf[:], sq["HFs"].t[t0:t0 + 128, 0:D], [sq["HFs"]], [yf_])
        bsf = k.rot("r7_bsf", [128, 16], F32, n=2)
        k.dma(bsf[:], sq["HFs"].t[t0:t0 + 128, D:D + 16], [sq["HFs"]], [bsf])
        for hb_ in range(2):
            k.tt("dve", yv2[:, hb_ * 512:(hb_ + 1) * 512], psY[hb_][:], yf[:, hb_ * 512:(hb_ + 1) * 512], ALU.add, [psY[hb_], yf_], [yv])
        k.tt("pool", bsf[:], bsf[:], bs[:], ALU.add, [bsf, bs], [bsf])
        bsum = bsf
    else:
        k.copy("act", yv2[:, 0:512], psY[0][:], [psY[0]], [yv])
        k.copy("dve", yv2[:, 512:1024], psY[1][:], [psY[1]], [yv])
        if d == 0:
            k.dma(sq["HFs"].t[t0:t0 + 128, 0:D], yv2[:], [yv], [sq["HFs"]], q="pool")
            k.dma(sq["HFs"].t[t0:t0 + 128, D:D + 16], bs[:], [bs], [sq["HFs"]], q="pool")
    if DBG.get("r7") == "j":
        return
    k.dump("yv", yv[:], [128, 8, 128], F32, [yv])
    k.dump("Us", Us[:], [128, 16, 64], BF16, [Us])
    k.dump("Bs", Bs[:], [128, 16, 64], BF16, [Bs])
    for jb in range(2):
        ps = k.psum()
        for jj in range(4):
            j = jb * 4 + jj
            cs_ = slice(jj * 128, (jj + 1) * 128)
            k.mm(ps[:, cs_], toks["at"][:, j, :], Us[:, 2 * j:2 * j + 2, :].rearrange("p h v -> p (h v)"), True, False, [toks["at"], Us], [ps])
            k.mm(ps[:, cs_], toks["kd"][:, j, :], Vb[:, j * 128:(j + 1) * 128], False, True, [toks["kd"], Vb], [ps])
        pv = ps[:, :].rearrange("p (j x) -> p j x", j=4)
        js = slice(jb * 4, jb * 4 + 4)
        k.tt("dve", Z[0:64, js, :], Z[0:64, js, :], pv[0:64, :, 0:64], ALU.add, [Z, ps], [Z])
        k.tt("dve", Z[64:128, js, :], Z[64:128, js, :], pv[64:128, :, 64:128], ALU.add, [Z, ps], [Z])
    k.tt("pool", Z[:], Z[:], gL[:, :].unsqueeze(2).to_broadcast([128, 8, 64]), ALU.mult, [Z, gL], [Z])
    k.copy("act", Zb[:], Z[:], [Z], [Zb])
    k.dump("Z", Z[:], [128, 8, 64], F32, [Z])
    if not fin:
        return
    if DBG.get("r7") == "l":
        return
    y3 = yv2[:, :].rearrange("p (h c) -> p h c", h=16)
    st1 = k.rot("r7_st1", [128, 16], F32, n=2)
    S.op("dve", lambda: nc.vector.tensor_reduce(out=st1[:], in_=y3, axis=AX.X, op=ALU.add), reads=[yv], writes=[st1])
    k.ts("dve", st1[:], st1[:], -1.0 / 64.0, None, ALU.mult, None, [st1], [st1])
    b16 = lambda t: t[:, :].unsqueeze(2).to_broadcast([128, 16, 64])
    k.tt("pool", y3, y3, b16(st1), ALU.add, [yv, st1], [yv])
    sqv_ = F(5)
    sqv = V2(sqv_)
    k.tt("dve", sqv[:], yv2[:], yv2[:], ALU.mult, [yv], [sqv_])
    st2 = k.rot("r7_st2", [128, 16], F32, n=2)
    S.op("dve", lambda: nc.vector.tensor_reduce(out=st2[:], in_=sqv[:, :].rearrange("p (h c) -> p h c", h=16), axis=AX.X, op=ALU.add),
         reads=[sqv_], writes=[st2])
    rs = rstd_of(k, st2, 64.0, GN_EPS, 16)
    k.tt("pool", y3, y3, b16(rs), ALU.mult, [yv, rs], [yv])
    k.tt("dve", yv2[:], yv2[:], c["lng"][:], ALU.mult, [yv, c["lng"]], [yv])
    k.tt("pool", yv2[:], yv2[:], c["lnb"][:], ALU.add, [yv, c["lnb"]], [yv])
    bon = sqv
    k.tt("dve", bon[:, :].rearrange("p (h c) -> p h c", h=16), Vb[:, :].rearrange("p (h c) -> p h c", h=16), b16(bsum), ALU.mult,
         [Vb, bsum], [sqv_])
    k.tt("pool", yv2[:], yv2[:], bon[:], ALU.add, [yv, sqv_], [yv])
    szt_ = k.rot("r7_xs", [128, 8, 128], BF16, n=2)
    szt = V2(szt_)
    k.dma(szt[:], sq["SZs"].t[t0:t0 + 128, 0:D], [sq["SZs"]], [szt_])
    k.tt("dve", yv2[:], yv2[:], szt[:], ALU.mult, [yv, szt_], [yv])
    yT = k.rot("r7_xs", [128, 8, 128], BF16, n=2)
    for tb in range(2):
        ps = k.psum()
        for jj in range(4):
            j = tb * 4 + jj
            k.tr(ps[:, jj * 128:(jj + 1) * 128], yv2[:, j * 128:(j + 1) * 128], k.ident[:], [yv, k.ident], [ps])
        k.copy("act" if tb == 0 else "dve", yT[:, tb * 4:(tb + 1) * 4, :], v4(ps), [ps], [yT])
    k.dma(sq["YGs"].t[0:D, t0:t0 + 128].rearrange("(j p) t -> p j t", p=128), yT[:], [yT], [sq["YGs"]], q="pool")


WNAMES = ["norm_g", "mod_w", "mod_b", "final_g",
          "lru_w_in", "lru_conv_w", "lru_conv_b", "lru_gate_w", "lru_gate_b", "lru_lam", "lru_w_out",
          "mlstm_w_in", "mlstm_gate_b", "mlstm_norm_g", "mlstm_w_out",
          "r7_mu", "r7_w_rkvz", "r7_w0", "r7_w1", "r7_w2", "r7_a0", "r7_a1", "r7_a2",
          "r7_k_k", "r7_k_a", "r7_r_k", "r7_ln_g", "r7_ln_b", "r7_w_out"]


def build(T, CTX, kinds, wshapes):
    nc = bass.Bass("TRN2", target_bir_lowering=False)
    with contextlib.ExitStack() as st:
        S = Sch(nc, st)
        k = K(nc, S)
        W = {}
        W["x"] = Buf(nc.dram_tensor("x", [T, D], F32, kind="ExternalInput"), "x")
        W["ctx"] = Buf(nc.dram_tensor("ctx", [CTX, D], F32, kind="ExternalInput"), "ctx")
        W["c"] = Buf(nc.dram_tensor("c", [D], F32, kind="ExternalInput"), "c")
        W["c_ctx"] = Buf(nc.dram_tensor("c_ctx", [D], F32, kind="ExternalInput"), "c_ctx")
        for n in WNAMES:
            W[n] = Buf(nc.dram_tensor(n, list(wshapes[n]), F32, kind="ExternalInput"), n)
        OUT = Buf(nc.dram_tensor("out", [T, D], F32, kind="ExternalOutput"), "out")
        Tm = max(T, CTX)
        scr = {
            "HT": S.dram("s_HT", [D, Tm], BF16),
            "UT": S.dram("s_UT", [LRU_W, Tm], F32),
            "SZ": S.dram("s_SZ", [LRU_W, Tm], BF16),
            "UC": S.dram("s_UC", [LRU_W, Tm], F32),
            "HF": S.dram("s_HF", [LRU_W, Tm], F32),
            "YG": S.dram("s_YG", [2048, Tm], BF16),
            "YGC": S.dram("s_YGC", [2048, CTX], BF16),
            "R7r": S.dram("s_R7r", [Tm, D], F32),
            "R7k": S.dram("s_R7k", [Tm, D], F32),
            "R7kk": S.dram("s_R7kk", [Tm, D], F32),
            "R7v": S.dram("s_R7v", [Tm, D], BF16),
            "HTC": S.dram("s_HTC", [D, CTX], BF16),
            "HFT": S.dram("s_HFT", [T, 2048], F32),
            "HFTC": S.dram("s_HFTC", [CTX, 2048], F32),
            "SZT": S.dram("s_SZT", [T, 2048], BF16),
            "SZTC": S.dram("s_SZTC", [CTX, 2048], BF16),
        }
        XR = [S.dram("s_XR0", [T, D], F32), S.dram("s_XR1", [T, D], F32)]
        CR = [S.dram("s_CR0", [CTX, D], F32), S.dram("s_CR1", [CTX, D], F32)]

        k.modscr = {"x": S.dram("s_MODX", [128, 3072], F32), "c": S.dram("s_MODC", [128, 3072], F32)}
        make_consts(k)
        k.epsb = {}
        for e in (NORM_EPS, GN_EPS):
            b = S.sb([128, 1], F32, name=f"eps{len(k.epsb)}")
            k.memset("pool", b[:], e, [b])
            k.epsb[e] = b
        k.oneb = S.sb([128, 1], F32, name="oneb")
        k.memset("pool", k.oneb[:], 1.0, [k.oneb])
        emit_cond(k, W)

        xcur, ccur = W["x"], W["ctx"]
        cnt = {0: 0, 1: 0, 2: 0}
        depth = len(kinds)
        for li, kind in enumerate(kinds):
            jl = cnt[kind]
            cnt[kind] += 1
            with_ctx = li < depth - 1
            mod = emit_mod(k, li, W)
            if DBG.get("stop") == "mod":
                break
            xdst = XR[li % 2]
            cdst = CR[li % 2]
            seqs = [
                dict(name="c", src=ccur, Tn=CTX, M=mod["c"], dst=cdst, want_out=with_ctx,
                     HTs=scr["HTC"], HFs=scr["HFTC"], SZs=scr["SZTC"], YGs=scr["YGC"]),
                dict(name="x", src=xcur, Tn=T, M=mod["x"], dst=xdst, want_out=True,
                     HTs=scr["HT"], HFs=scr["HFT"], SZs=scr["SZT"], YGs=scr["YG"]),
            ]
            if kind == 0:
                lru_layer(k, li, jl, W, seqs, scr)
            elif kind == 1:
                mlstm_layer(k, li, jl, W, seqs, scr)
            else:
                rwkv_layer(k, li, jl, W, seqs, scr)
            if DBG.get("stop") and DBG.get("stop") != "p3x":
                break
            xcur = xdst
            if with_ctx:
                ccur = cdst
        fg = S.sb([128, 1024], F32, name="final_g")
        k.dma(fg[:], W["final_g"].t[:].partition_broadcast(128), [W["final_g"]], [fg])
        if DBG.get("raw"):
            src = scr[DBG["raw"]] if DBG["raw"] in scr else xcur
            with k.scope():
                for t0 in range(0, T, 128):
                    tb = k.rot("rawb", [128, 1024], F32, n=2)
                    k.dma(tb[:], src.t[t0:t0 + 128, 0:1024], [src], [tb])
                    k.dma(OUT.t[t0:t0 + 128, :], tb[:], [tb], [OUT], q="pool")
        else:
            pass_final(k, xcur, T, fg, OUT)
        S.finish([OUT] + k.dump_bufs)
        print(f"[build] insts={S.n_inst} waits={S.n_wait}")
    return nc


_CACHE = {}


def run(inputs, kinds, n_cores=2):
    x = np.ascontiguousarray(inputs["x"], dtype=np.float32)
    B, T, _ = x.shape
    CTX = inputs["ctx"].shape[1]
    wshapes = {n: inputs[n].shape for n in WNAMES}
    key = (T, CTX, tuple(kinds))
    if key not in _CACHE:
        _CACHE[key] = build(T, CTX, kinds, wshapes)
    nc = _CACHE[key]
    in_maps = []
    for b in range(B):
        m = {"x": x[b], "ctx": np.ascontiguousarray(inputs["ctx"][b], dtype=np.float32),
             "c": np.ascontiguousarray(inputs["c"][b], dtype=np.float32),
             "c_ctx": np.ascontiguousarray(inputs["c_ctx"], dtype=np.float32)}
        for n in WNAMES:
            m[n] = np.ascontiguousarray(inputs[n], dtype=np.float32)
        in_maps.append(m)
    res = run_bass_kernel_spmd(nc, in_maps, core_ids=list(range(B)))
    DBG["results"] = res.results
    return np.stack([res.results[b]["out"] for b in range(B)], axis=0)


def kernel(**inputs):
    return run(inputs, [0, 1, 2, 0])
```

```python
import contextlib
import numpy as np
import concourse.bass as bass
import concourse.mybir as mybir
from concourse.bass_utils import run_bass_kernel_spmd

F32 = mybir.dt.float32
BF16 = mybir.dt.bfloat16
AF = mybir.ActivationFunctionType
ALU = mybir.AluOpType
AX = mybir.AxisListType

D = 1024
KD = 8
NORM_EPS = 1e-6
GN_EPS = 64e-5
LRU_W = 1280
LRU_NB = 10
LRU_TC = 512
LRU_NCH = 5

DBG = {}
CE = ("pe", "dve", "act", "pool")
ALLQ = ("pe", "dve", "act", "pool", "sp")


class Buf:
    __slots__ = ("t", "name", "w", "r")

    def __init__(self, t=None, name=""):
        self.t = t
        self.name = name
        self.w = None
        self.r = {}

    def __getitem__(self, idx):
        return self.t[idx]


class V2:
    def __init__(self, buf):
        self.buf = buf
        self.ap = buf.t[:, :, :].rearrange("p j t -> p (j t)")

    def __getitem__(self, idx):
        return self.ap[idx]


class Sch:
    def __init__(self, nc, stack, ndma=12):
        self.nc = nc
        self.stack = stack
        self.eng = {"pe": nc.tensor, "dve": nc.vector, "act": nc.scalar, "pool": nc.gpsimd, "sp": nc.sync}
        self.q = {e: [] for e in ALLQ}
        self.sem = {e: stack.enter_context(nc.semaphore("s_" + e)) for e in CE}
        self.cnt = {e: 0 for e in CE}
        self.known = {e: {} for e in ALLQ}
        self.dpool = {}
        self.dnext = {}
        self.dval = {}
        for qn in ("sp", "act", "pool"):
            self.dpool[qn] = [stack.enter_context(nc.semaphore(f"d_{qn}{i}")) for i in range(ndma)]
            self.dnext[qn] = 0
        self.n_inst = 0
        self.n_wait = 0
        self._uid = 0

    def sb(self, shape, dtype=F32, name=None):
        self._uid += 1
        name = f"{name or 'sb'}_{self._uid}"
        t = self.stack.enter_context(self.nc.sbuf_tensor(name, list(shape), dtype))
        return Buf(t, name)

    def ps(self, shape, dtype=F32, name=None):
        self._uid += 1
        name = name or f"ps{self._uid}"
        t = self.stack.enter_context(self.nc.psum_tensor(name, list(shape), dtype))
        return Buf(t, name)

    def dram(self, name, shape, dtype=F32, kind="Internal"):
        t = self.nc.dram_tensor(name, list(shape), dtype, kind=kind)
        return Buf(t, name)

    def _wait(self, qn, tok):
        kind, key, val = tok
        kn = self.known[qn]
        if kn.get(key, 0) >= val:
            return
        kn[key] = val
        sem = self.sem[key] if kind == "e" else key
        eng = self.eng[qn]
        eng.wait_ge(sem, val)
        self.n_wait += 1

    def _deps(self, qn, reads, writes):
        toks = []
        for b in reads:
            if b.w is not None:
                toks.append(b.w)
        for b in writes:
            if b.w is not None:
                t = b.w
                if not (t[0] == "e" and t[1] == qn == "pe"):
                    toks.append(t)
            for (k0, k1), v in b.r.items():
                if not (k0 == "e" and k1 == qn == "pe"):
                    toks.append((k0, k1, v))
        for t in toks:
            self._wait(qn, t)

    def _mark(self, tok, reads, writes):
        for b in reads:
            b.r[(tok[0], tok[1])] = tok[2]
        for b in writes:
            b.w = tok
            b.r = {}
        self.n_inst += 1

    def op(self, qn, fn, reads=(), writes=()):
        self._deps(qn, reads, writes)
        self.cnt[qn] += 1
        sem = self.sem[qn]
        fn().then_inc(sem, 1)
        tok = ("e", qn, self.cnt[qn])
        self._mark(tok, reads, writes)
        return tok

    def dma(self, qn, fn, reads=(), writes=()):
        self._deps(qn, reads, writes)
        pool = self.dpool[qn]
        i = self.dnext[qn] % len(pool)
        self.dnext[qn] += 1
        sem = pool[i]
        prev = self.dval.get(sem, 0)
        if prev:
            self._wait(qn, ("d", sem, prev))
        val = prev + 16
        self.dval[sem] = val
        eng = self.eng[qn]
        fn(eng).then_inc(sem, 16)
        tok = ("d", sem, val)
        self._mark(tok, reads, writes)
        return tok

    def finish(self, out_bufs):
        for b in out_bufs:
            if b.w is not None:
                self._wait("sp", b.w)

    def barrier(self):
        toks = [("e", e, self.cnt[e]) for e in CE if self.cnt[e]]
        toks += [("d", sem, v) for sem, v in self.dval.items() if v]
        for qn in ALLQ:
            for t in toks:
                if not (t[0] == "e" and t[1] == qn):
                    self._wait(qn, t)


class K:
    def __init__(self, nc, S):
        self.nc = nc
        self.S = S
        self.pools = {}
        self.psb = [S.ps([128, 512], F32, name=f"psb{i}") for i in range(8)]
        self.psi = 0
        self.ldq = 0
        self.dumped = set()
        self.dump_bufs = []

    def rot(self, key, shape, dtype=F32, n=2, init=None):
        p = self.pools.get(key)
        if p is None:
            bufs = [self.S.sb(shape, dtype, name=f"{key}_{i}") for i in range(n)]
            if init is not None:
                for b in bufs:
                    init(b)
            p = self.pools[key] = [bufs, 0]
        b = p[0][p[1] % len(p[0])]
        p[1] += 1
        return b

    def dump(self, name, ap, shape, dtype, reads):
        if name not in DBG.get("dump", ()) or name in self.dumped:
            return
        self.dumped.add(name)
        t = self.nc.dram_tensor("dbg_" + name, list(shape), dtype, kind="ExternalOutput")
        b = Buf(t, name)
        self.dma(t[:], ap, reads, [b], q="sp")
        self.dump_bufs.append(b)

    def psum(self):
        b = self.psb[self.psi % 8]
        self.psi += 1
        return b

    @contextlib.contextmanager
    def scope(self):
        S = self.S
        ls = contextlib.ExitStack()
        old_stack, old_pools = S.stack, self.pools
        S.stack = ls
        self.pools = dict(old_pools)
        try:
            yield
        finally:
            S.barrier()
            S.stack = old_stack
            self.pools = old_pools
            ls.close()

    def dma(self, out_ap, in_ap, reads, writes, q="sp", slow=False):
        if slow:
            self.S.dma(q, lambda e: e.dma_start(out=out_ap, in_=in_ap, allow_slow_non_contiguous=True),
                       reads=reads, writes=writes)
        else:
            self.S.dma(q, lambda e: e.dma_start(out=out_ap, in_=in_ap), reads=reads, writes=writes)

    def mm(self, ps_ap, lhsT, rhs, start, stop, reads, writes):
        nc = self.nc
        self.S.op("pe", lambda: nc.tensor.matmul(ps_ap, lhsT=lhsT, rhs=rhs, start=start, stop=stop),
                  reads=reads, writes=writes)

    def tr(self, ps_ap, in_ap, ident_ap, reads, writes):
        nc = self.nc
        self.S.op("pe", lambda: nc.tensor.transpose(ps_ap, in_ap, ident_ap), reads=reads, writes=writes)

    def act(self, out, in_, func, reads, writes, bias=None, scale=None, accum_out=None):
        nc = self.nc
        kw = {}
        if bias is not None:
            kw["bias"] = bias
        if scale is not None:
            kw["scale"] = scale
        if accum_out is not None:
            kw["accum_out"] = accum_out
        self.S.op("act", lambda: nc.scalar.activation(out=out, in_=in_, func=func, **kw), reads=reads, writes=writes)

    def ts(self, eng, out, in0, s1, s2, op0, op1, reads, writes):
        e = self.nc.vector if eng == "dve" else self.nc.gpsimd
        if op1 is None:
            self.S.op(eng, lambda: e.tensor_scalar(out=out, in0=in0, scalar1=s1, scalar2=None, op0=op0),
                      reads=reads, writes=writes)
        else:
            self.S.op(eng, lambda: e.tensor_scalar(out=out, in0=in0, scalar1=s1, scalar2=s2, op0=op0, op1=op1),
                      reads=reads, writes=writes)

    def tt(self, eng, out, in0, in1, op, reads, writes):
        e = self.nc.vector if eng == "dve" else self.nc.gpsimd
        self.S.op(eng, lambda: e.tensor_tensor(out=out, in0=in0, in1=in1, op=op), reads=reads, writes=writes)

    def stt(self, out, in0, scalar, in1, op0, op1, reads, writes):
        nc = self.nc
        self.S.op("dve", lambda: nc.vector.scalar_tensor_tensor(out=out, in0=in0, scalar=scalar, in1=in1,
                                                                op0=op0, op1=op1), reads=reads, writes=writes)

    def copy(self, eng, out, in_, reads, writes):
        nc = self.nc
        if eng == "act":
            self.S.op("act", lambda: nc.scalar.copy(out=out, in_=in_), reads=reads, writes=writes)
        elif eng == "dve":
            self.S.op("dve", lambda: nc.vector.tensor_copy(out=out, in_=in_), reads=reads, writes=writes)
        else:
            self.S.op("pool", lambda: nc.gpsimd.tensor_copy(out=out, in_=in_), reads=reads, writes=writes)

    def memset(self, eng, ap, val, writes):
        e = self.nc.vector if eng == "dve" else self.nc.gpsimd
        self.S.op(eng, lambda: e.memset(ap, val), writes=writes)

    def recip(self, out, in_, reads, writes):
        nc = self.nc
        self.S.op("dve", lambda: nc.vector.reciprocal(out=out, in_=in_), reads=reads, writes=writes)


def make_consts(k):
    nc, S = k.nc, k.S
    ones = S.sb([128, 512], F32, name="c_ones")
    k.memset("pool", ones[:], 1.0, [ones])
    k.ones = ones

    def sel(name, cm, step, op, base=0):
        b = S.sb([128, 128], F32, name=name)
        S.op("pool", lambda: nc.gpsimd.affine_select(out=b[:], in_=ones[:, 0:128], pattern=[[step, 128]],
                                                     compare_op=op, fill=0.0, base=base, channel_multiplier=cm),
             reads=[ones], writes=[b])
        return b
    k.ident = sel("c_ident", -1, 1, ALU.is_equal)
    k.m_le = sel("c_mle", -1, 1, ALU.is_ge)
    k.m_ge = sel("c_mge", 1, -1, ALU.is_ge)
    k.m_lt = sel("c_mlt", -1, 1, ALU.is_gt)
    k.m_gt = sel("c_mgt", 1, -1, ALU.is_gt)
    k.identb = S.sb([128, 128], BF16, name="c_identb")
    k.copy("dve", k.identb[:], k.ident[:], [k.ident], [k.identb])
    k.onesb = S.sb([128, 128], BF16, name="c_onesb")
    k.copy("dve", k.onesb[:], ones[:, 0:128], [ones], [k.onesb])


def emit_mod(k, li, W):
    nc, S = k.nc, k.S
    with k.scope():
        Ms = {"x": k.rot("modM_x", [128, 3072], F32, n=1), "c": k.rot("modM_c", [128, 3072], F32, n=1)}
        mb = k.rot("mod_b", [128, 3072], F32, n=1)
        k.dma(mb[:], W["mod_b"].t[li].partition_broadcast(128), [W["mod_b"]], [mb])
        gb = k.rot("mod_g", [128, 1024], F32, n=1)
        k.dma(gb[:], W["norm_g"].t[li].partition_broadcast(128), [W["norm_g"]], [gb])
        for half in range(2):
            pss = {c: [k.psum() for _ in range(3)] for c in ("x", "c")}
            for j in range(KD):
                wt = k.rot("modw", [128, 1536], F32, n=2)
                k.dma(wt[:], W["mod_w"].t[li, j * 128:(j + 1) * 128, half * 1536:(half + 1) * 1536], [W["mod_w"]], [wt])
                for c in ("x", "c"):
                    R = k.R[c]
                    for nb in range(3):
                        k.mm(pss[c][nb][:], R[:, j, :], wt[:, nb * 512:(nb + 1) * 512], j == 0, j == KD - 1,
                             [R, wt], [pss[c][nb]])
            for c in ("x", "c"):
                for nb in range(3):
                    o = half * 1536 + nb * 512
                    k.tt("dve", Ms[c][:, o:o + 512], pss[c][nb][:], mb[:, o:o + 512], ALU.add,
                         [pss[c][nb], mb], [Ms[c]])
        for c in ("x", "c"):
            M = Ms[c]
            k.stt(M[:, 1024:2048], M[:, 1024:2048], 1.0, gb[:], ALU.add, ALU.mult, [M, gb], [M])
            k.dma(k.modscr[c].t[:, :], M[:], [M], [k.modscr[c]], q="pool")
    return k.modscr


def emit_cond(k, W):
    nc, S = k.nc, k.S
    k.R = {}
    for c, nm in (("x", "c"), ("c", "c_ctx")):
        cv = S.sb([128, 8], F32, name="cv_" + c)
        k.dma(cv[:], W[nm].t.rearrange("(j p) -> p j", p=128), [W[nm]], [cv], slow=True)
        sc = S.sb([128, 8], F32, name="sc_" + c)
        k.act(sc[:], cv[:], AF.Silu, [cv], [sc])
        R = S.sb([128, 8, 128], F32, name="R_" + c)
        for j in range(KD):
            k.ts("dve", R[:, j, :], k.ones[:, 0:128], sc[:, j:j + 1], None, ALU.mult, None, [k.ones, sc], [R])
        k.R[c] = R


def rstd_of(k, ss, n, eps, width):
    rt = k.rot(f"rstd_a{width}", [128, width], F32, n=3)
    k.act(rt[:], ss[:], AF.Sqrt, [ss], [rt], bias=k.epsb[eps][:, 0:1], scale=1.0 / n)
    rs = k.rot(f"rstd_b{width}", [128, width], F32, n=3)
    k.recip(rs[:], rt[:], [rt], [rs])
    return rs


def pass_norm(k, src, Tn, M, HT):
    with k.scope():
        Ml = k.rot("M_local", [128, 3072], F32, n=1)
        k.dma(Ml[:], M.t[:, :], [M], [Ml])
        _pass_norm(k, src, Tn, Ml, HT)


def _pass_norm(k, src, Tn, M, HT):
    nc, S = k.nc, k.S
    TT = min(512, Tn)
    ns = TT // 128
    for t0 in range(0, Tn, TT):
        xs = k.rot("pn_x", [128, 4, 1024], F32, n=2)
        k.dma(xs[:, 0:ns, :], src.t[t0:t0 + TT, :].rearrange("(a p) d -> p a d", p=128), [src], [xs])
        hT = k.rot("pn_hT", [128, 8, 512], BF16, n=2)
        for a in range(ns):
            junk = k.rot("pn_junk", [128, 1024], F32, n=2)
            ss = k.rot("pn_ss", [128, 1], F32, n=3)
            k.act(junk[:], xs[:, a, :], AF.Square, [xs], [junk, ss], accum_out=ss[:])
            rs = rstd_of(k, ss, float(D), NORM_EPS, 1)
            h = k.rot("pn_h", [128, 1024], F32, n=2)
            k.stt(h[:], xs[:, a, :], rs[:, 0:1], M[:, 1024:2048], ALU.mult, ALU.mult, [xs, rs, M], [h])
            k.tt("pool", h[:], h[:], M[:, 0:1024], ALU.add, [h, M], [h])
            for hb in range(2):
                ps = k.psum()
                for jj in range(4):
                    j = hb * 4 + jj
                    k.tr(ps[:, jj * 128:(jj + 1) * 128], h[:, j * 128:(j + 1) * 128], k.ident[:], [h, k.ident], [ps])
                dst = hT[:, hb * 4:(hb + 1) * 4, a * 128:(a + 1) * 128]
                srcp = ps[:, :].rearrange("p (j t) -> p j t", j=4)
                if hb == 0:
                    k.copy("act", dst, srcp, [ps], [hT])
                else:
                    k.copy("dve", dst, srcp, [ps], [hT])
        k.dma(HT.t[:, t0:t0 + TT].rearrange("(j p) t -> p j t", p=128), hT[:, :, 0:TT], [hT], [HT], q="pool")


def load_w_bf16(k, name, src_ap_fn, src_buf, nK, N, stage_cols=None):
    S = k.S
    wb = S.sb([128, nK, N], BF16, name=name)
    sc = stage_cols or N
    i = 0
    with k.scope():
        for j in range(nK):
            for c0 in range(0, N, sc):
                st = k.rot(f"wstage{sc}", [128, sc], F32, n=2)
                k.dma(st[:], src_ap_fn(j, c0, sc), [src_buf], [st])
                eng = ("dve", "pool", "act")[i % 3]
                k.copy(eng, wb[:, j, c0:c0 + sc], st[:], [st], [wb])
                i += 1
    return wb


def pass_out(k, YG, nK, Wo, Tn, xsrc, G, xdst):
    with k.scope():
        Gl = k.rot("M_local", [128, 3072], F32, n=1)
        k.dma(Gl[:], G.t[:, :], [G], [Gl])
        _pass_out(k, YG, nK, Wo, Tn, xsrc, Gl, xdst)


def _pass_out(k, YG, nK, Wo, Tn, xsrc, G, xdst):
    TT = min(512, Tn)
    ns = TT // 128
    for t0 in range(0, Tn, TT):
        yg = k.rot(f"po_yg{nK}", [128, nK, 512], BF16, n=2)
        k.dma(yg[:, :, 0:TT], YG.t[0:nK * 128, t0:t0 + TT].rearrange("(j p) t -> p j t", p=128), [YG], [yg])
        xs = k.rot("po_x", [128, 4, 1024], F32, n=2)
        k.dma(xs[:, 0:ns, :], xsrc.t[t0:t0 + TT, :].rearrange("(a p) d -> p a d", p=128), [xsrc], [xs])
        xn = k.rot("po_xn", [128, 4, 1024], F32, n=2)
        for a in range(ns):
            for nb in range(2):
                ps = k.psum()
                for j in range(nK):
                    k.mm(ps[:], yg[:, j, a * 128:(a + 1) * 128], Wo[:, j, nb * 512:(nb + 1) * 512],
                         j == 0, j == nK - 1, [yg, Wo], [ps])
                tmp = k.rot("po_tmp", [128, 512], F32, n=3)
                k.tt("dve", tmp[:], ps[:], G[:, 2048 + nb * 512:2048 + (nb + 1) * 512], ALU.mult, [ps, G], [tmp])
                k.tt("pool", xn[:, a, nb * 512:(nb + 1) * 512], tmp[:], xs[:, a, nb * 512:(nb + 1) * 512], ALU.add,
                     [tmp, xs], [xn])
        k.dma(xdst.t[t0:t0 + TT, :].rearrange("(a p) d -> p a d", p=128), xn[:, 0:ns, :], [xn], [xdst], q="pool")


def pass_final(k, src, Tn, fg, out):
    with k.scope():
        _pass_final(k, src, Tn, fg, out)


def _pass_final(k, src, Tn, fg, out):
    TT = min(512, Tn)
    ns = TT // 128
    for t0 in range(0, Tn, TT):
        xs = k.rot("pn_x", [128, 4, 1024], F32, n=2)
        k.dma(xs[:, 0:ns, :], src.t[t0:t0 + TT, :].rearrange("(a p) d -> p a d", p=128), [src], [xs])
        xn = k.rot("po_xn", [128, 4, 1024], F32, n=2)
        for a in range(ns):
            junk = k.rot("pn_junk", [128, 1024], F32, n=2)
            ss = k.rot("pn_ss", [128, 1], F32, n=3)
            k.act(junk[:], xs[:, a, :], AF.Square, [xs], [junk, ss], accum_out=ss[:])
            rs = rstd_of(k, ss, float(D), NORM_EPS, 1)
            k.stt(xn[:, a, :], xs[:, a, :], rs[:, 0:1], fg[:], ALU.mult, ALU.mult, [xs, rs, fg], [xn])
        k.dma(out.t[t0:t0 + TT, :].rearrange("(a p) d -> p a d", p=128), xn[:, 0:ns, :], [xn], [out], q="pool")


def lru_layer(k, li, jl, W, seqs, scr):
    nc, S = k.nc, k.S
    with k.scope():
        w_in = load_w_bf16(k, f"lru_win{li}", lambda j, c0, sc: W["lru_w_in"].t[jl, j * 128:(j + 1) * 128, c0:c0 + sc],
                           W["lru_w_in"], KD, 2 * LRU_W, stage_cols=1280)
        w_out = load_w_bf16(k, f"lru_wout{li}", lambda j, c0, sc: W["lru_w_out"].t[jl, j * 128:(j + 1) * 128, c0:c0 + sc],
                            W["lru_w_out"], LRU_NB, D)
        gwv = W["lru_gate_w"].t[jl].rearrange("d g n j k -> (d g n) j k")
        gw = load_w_bf16(k, f"lru_gw{li}", lambda j, c0, sc: gwv[j], W["lru_gate_w"], 40, 128)
        cw = S.sb([128, 4, LRU_NB], F32, name=f"lru_cw{li}")
        k.dma(cw[:], W["lru_conv_w"].t[jl].rearrange("j (n p) -> p j n", p=128), [W["lru_conv_w"]], [cw], slow=True)
        cb = S.sb([128, LRU_NB], F32, name=f"lru_cb{li}")
        k.dma(cb[:], W["lru_conv_b"].t[jl].rearrange("(n p) -> p n", p=128), [W["lru_conv_b"]], [cb], slow=True)
        gbias = S.sb([128, 4, LRU_NB], F32, name=f"lru_gb{li}")
        k.dma(gbias[:], W["lru_gate_b"].t[jl].rearrange("d g (n p) -> p (d g) n", p=128), [W["lru_gate_b"]], [gbias], slow=True)
        lam = S.sb([128, 2, LRU_NB], F32, name=f"lru_lam{li}")
        k.dma(lam[:], W["lru_lam"].t[jl].rearrange("d (n p) -> p d n", p=128), [W["lru_lam"]], [lam], slow=True)
        e1 = S.sb([128, 2, LRU_NB], F32, name=f"lru_e1{li}")
        k.act(e1[:], lam[:], AF.Exp, [lam], [e1], scale=-1.0)
        l1 = S.sb([128, 2, LRU_NB], F32, name=f"lru_l1{li}")
        k.act(l1[:], e1[:], AF.Ln, [e1, k.oneb], [l1], bias=k.oneb[:, 0:1])
        coef = S.sb([128, 2, LRU_NB], F32, name=f"lru_coef{li}")
        k.ts("dve", coef[:], l1[:], -8.0, None, ALU.mult, None, [l1], [coef])
        coef2 = S.sb([128, 2, LRU_NB], F32, name=f"lru_coef2{li}")
        k.ts("dve", coef2[:], l1[:], -16.0, None, ALU.mult, None, [l1], [coef2])
        st = [S.sb([128, 2], F32, name=f"lru_st{li}_{n}") for n in range(LRU_NB)]
        for n in range(LRU_NB):
            k.memset("dve", st[n][:], 0.0, [st[n]])
        HT, UT, SZ, UC, HF, YG = (scr[n] for n in ("HT", "UT", "SZ", "UC", "HF", "YG"))

        for sq in seqs:
            Tn = sq["Tn"]
            if DBG.get("stop") == "w":
                continue
            pass_norm(k, sq["src"], Tn, sq["M"], HT)
            if DBG.get("stop") == "norm":
                continue
            with k.scope():
                TT = min(512, Tn)
                for t0 in range(0, Tn, TT):
                    hT = k.rot("l1_hT", [128, 8, 512], BF16, n=2)
                    k.dma(hT[:, :, 0:TT], HT.t[:, t0:t0 + TT].rearrange("(j p) t -> p j t", p=128), [HT], [hT])
                    ub = k.rot("l1_u", [128, LRU_NB, 512], F32, n=2)
                    zb = k.rot("l1_z", [128, LRU_NB, 512], BF16, n=2)
                    for oc in range(2 * LRU_NB):
                        ps = k.psum()
                        for j in range(KD):
                            k.mm(ps[:, 0:TT], w_in[:, j, oc * 128:(oc + 1) * 128], hT[:, j, 0:TT], j == 0, j == KD - 1,
                                 [w_in, hT], [ps])
                        if oc < LRU_NB:
                            k.copy("dve", ub[:, oc, 0:TT], ps[:, 0:TT], [ps], [ub])
                        else:
                            k.act(zb[:, oc - LRU_NB, 0:TT], ps[:, 0:TT], AF.Silu, [ps], [zb])
                    k.dma(UT.t[:, t0:t0 + TT].rearrange("(n p) t -> p n t", p=128), ub[:, :, 0:TT], [ub], [UT], q="pool")
                    k.dma(SZ.t[:, t0:t0 + TT].rearrange("(n p) t -> p n t", p=128), zb[:, :, 0:TT], [zb], [SZ], q="pool")
            if DBG.get("stop") == "p1":
                continue
            with k.scope():
                TC = min(LRU_TC, Tn)
                nsub = Tn // TC

                def chain(cid, n):
                    rows = slice(n * 128, (n + 1) * 128)
                    for d in range(2):
                        order = range(nsub) if d == 0 else range(nsub - 1, -1, -1)
                        for si in order:
                            yield from _lru_sub(k, sq, n, d, si * TC, TC, Tn, rows, cw, cb, gw, gbias, coef, coef2, st, scr, cid)

                for n0 in range(0, LRU_NB, LRU_NCH):
                    gens = [chain(ci, n0 + ci) for ci in range(min(LRU_NCH, LRU_NB - n0))]
                    live = list(range(len(gens)))
                    step = 0
                    while live:
                        for ci in list(live):
                            if step < ci:
                                continue
                            try:
                                next(gens[ci])
                            except StopIteration:
                                live.remove(ci)
                        step += 1
            if DBG.get("stop") == "p2":
                continue
            if sq["want_out"]:
                pass_out(k, YG, LRU_NB, w_out, Tn, sq["src"], sq["M"], sq["dst"])


def _lru_sub(k, sq, n, d, t0, TC, Tn, rows, cw, cb, gw, gbias, coef, coef2, st, scr, cid):
    nc, S = k.nc, k.S
    HT, UT, SZ, UC, HF, YG = (scr[x] for x in ("HT", "UT", "SZ", "UC", "HF", "YG"))
    W_ = LRU_TC
    ucf = k.rot(f"l2_uc_{cid}", [128, W_], F32, n=1)
    if d == 0:
        uh = k.rot(f"l2_uh_{cid}", [128, W_ + 3], F32, n=1)
        lo = max(t0 - 2, 0)
        hi = min(t0 + TC + 1, Tn)
        if t0 == 0:
            k.memset("pool", uh[:, 0:2], 0.0, [uh])
        if t0 + TC == Tn:
            k.memset("pool", uh[:, TC + 2:TC + 3], 0.0, [uh])
        k.dma(uh[:, lo - (t0 - 2):hi - (t0 - 2)], UT.t[rows, lo:hi], [UT], [uh])
        k.ts("dve", ucf[:, 0:TC], uh[:, 2:2 + TC], cw[:, 2, n:n + 1], cb[:, n:n + 1], ALU.mult, ALU.add,
             [uh, cw, cb], [ucf])
        for tap in (0, 1, 3):
            k.stt(ucf[:, 0:TC], uh[:, tap:tap + TC], cw[:, tap, n:n + 1], ucf[:, 0:TC], ALU.mult, ALU.add,
                  [uh, cw, ucf], [ucf])
        k.dma(UC.t[rows, t0:t0 + TC], ucf[:, 0:TC], [ucf], [UC], q="pool")
    else:
        k.dma(ucf[:, 0:TC], UC.t[rows, t0:t0 + TC], [UC], [ucf])
    yield
    ucb = k.rot(f"l2_ucb_{cid}", [128, W_], BF16, n=1)
    k.copy("pool", ucb[:, 0:TC], ucf[:, 0:TC], [ucf], [ucb])
    r = k.rot(f"l2_r_{cid}", [128, W_], F32, n=1)
    ig = k.rot(f"l2_i_{cid}", [128, W_], F32, n=1)
    for c0 in range(0, TC, 512):
        cl = min(512, TC - c0)
        for g, dstb in ((0, r), (1, ig)):
            ps = k.psum()
            gi = (d * 2 + g) * LRU_NB + n
            k.mm(ps[:, 0:cl], gw[:, gi, :], ucb[:, c0:c0 + cl], True, True, [gw, ucb], [ps])
            k.act(dstb[:, c0:c0 + cl], ps[:, 0:cl], AF.Sigmoid, [ps, gbias], [dstb],
                  bias=gbias[:, d * 2 + g, n:n + 1])
    yield
    av = k.rot(f"l2_a_{cid}", [128, W_], F32, n=1)
    k.act(av[:, 0:TC], r[:, 0:TC], AF.Exp, [r, coef], [av], scale=coef[:, d, n:n + 1])
    a2 = k.rot(f"l2_a2_{cid}", [128, W_], F32, n=1)
    k.tt("pool", a2[:, 0:TC], av[:, 0:TC], av[:, 0:TC], ALU.mult, [av], [a2])
    k.act(a2[:, 0:TC], a2[:, 0:TC], AF.Sqrt, [a2, k.oneb], [a2], bias=k.oneb[:, 0:1], scale=-1.0)
    yield
    k.tt("pool", ig[:, 0:TC], ig[:, 0:TC], a2[:, 0:TC], ALU.mult, [ig, a2], [ig])
    k.tt("dve", ig[:, 0:TC], ig[:, 0:TC], ucf[:, 0:TC], ALU.mult, [ig, ucf], [ig])
    yield
    hv = k.rot(f"l2_h_{cid}", [128, W_], F32, n=1)
    if d == 0:
        S.op("dve", lambda: nc.vector.tensor_tensor_scan(
            out=hv[:, 0:TC], data0=av[:, 0:TC], data1=ig[:, 0:TC], initial=st[n][:, 0:1],
            op0=ALU.mult, op1=ALU.add), reads=[av, ig, st[n]], writes=[hv])
        k.copy("dve", st[n][:, 0:1], hv[:, TC - 1:TC], [hv], [st[n]])
        k.dma(HF.t[rows, t0:t0 + TC], hv[:, 0:TC], [hv], [HF], q="pool")
    else:
        S.op("dve", lambda: nc.vector.tensor_tensor_scan(
            out=hv[:, TC - 1::-1], data0=av[:, TC - 1::-1], data1=ig[:, TC - 1::-1], initial=st[n][:, 1:2],
            op0=ALU.mult, op1=ALU.add), reads=[av, ig, st[n]], writes=[hv])
        k.copy("dve", st[n][:, 1:2], hv[:, 0:1], [hv], [st[n]])
        yield
        if sq["want_out"]:
            hf = k.rot(f"l2_hf_{cid}", [128, W_], F32, n=1)
            k.dma(hf[:, 0:TC], HF.t[rows, t0:t0 + TC], [HF], [hf])
            szt = k.rot(f"l2_sz_{cid}", [128, W_], BF16, n=1)
            k.dma(szt[:, 0:TC], SZ.t[rows, t0:t0 + TC], [SZ], [szt])
            k.tt("pool", hv[:, 0:TC], hv[:, 0:TC], hf[:, 0:TC], ALU.add, [hv, hf], [hv])
            ygt = k.rot(f"l2_yg_{cid}", [128, W_], BF16, n=1)
            k.tt("dve", ygt[:, 0:TC], hv[:, 0:TC], szt[:, 0:TC], ALU.mult, [hv, szt], [ygt])
            k.dma(YG.t[rows, t0:t0 + TC], ygt[:, 0:TC], [ygt], [YG], q="pool")
    yield


ML_H = 8
ML_DK = 128
ML_DV = 256
ML_W = 2048
ML_NIN = 6176


def mlstm_layer(k, li, jl, W, seqs, scr):
    nc, S = k.nc, k.S
    HT, HFT, SZT = scr["HT"], scr["HFT"], scr["SZT"]
    win = W["mlstm_w_in"]
    for sq in seqs:
        pass_norm(k, sq["src"], sq["Tn"], sq["M"], sq["HTs"])
    with k.scope():
        wz = load_w_bf16(k, f"ml_wz{li}", lambda j, c0, sc: win.t[jl, j * 128:(j + 1) * 128, 4096 + c0:4096 + c0 + sc],
                         win, KD, ML_W, stage_cols=1024)
        ng = S.sb([128, ML_W], F32, name="ml_ng")
        k.dma(ng[:], W["mlstm_norm_g"].t[jl].partition_broadcast(128), [W["mlstm_norm_g"]], [ng])
        for sq in seqs:
            if not sq["want_out"]:
                continue
            Tn = sq["Tn"]
            for t0 in range(0, Tn, 128):
                hT = k.rot("mz_hT", [128, 8, 128], BF16, n=2)
                k.dma(hT[:], sq["HTs"].t[:, t0:t0 + 128].rearrange("(j p) t -> p j t", p=128), [sq["HTs"]], [hT])
                szg = k.rot("mz_szg", [128, ML_W], BF16, n=2)
                for nb in range(4):
                    ps = k.psum()
                    for j in range(KD):
                        k.mm(ps[:], hT[:, j, :], wz[:, j, nb * 512:(nb + 1) * 512], j == 0, j == KD - 1, [hT, wz], [ps])
                    tmp = k.rot("mz_tmp", [128, 512], F32, n=2)
                    k.act(tmp[:], ps[:], AF.Silu, [ps], [tmp])
                    k.tt("dve", szg[:, nb * 512:(nb + 1) * 512], tmp[:], ng[:, nb * 512:(nb + 1) * 512], ALU.mult,
                         [tmp, ng], [szg])
                k.dma(sq["SZs"].t[t0:t0 + 128, :], szg[:], [szg], [sq["SZs"]], q="pool")
    with k.scope():
        def wsrc(j, c0, sc):
            if c0 < 4096:
                return win.t[jl, j * 128:(j + 1) * 128, c0:c0 + sc]
            return win.t[jl, j * 128:(j + 1) * 128, 6144:6176]
        wq = S.sb([128, KD, 4128], BF16, name=f"ml_wq{li}")
        with k.scope():
            ii = 0
            for j in range(KD):
                for c0 in range(0, 4096, 1024):
                    stg = k.rot("mlstage", [128, 1024], F32, n=2)
                    k.dma(stg[:], win.t[jl, j * 128:(j + 1) * 128, c0:c0 + 1024], [win], [stg])
                    k.copy(("dve", "pool", "act")[ii % 3], wq[:, j, c0:c0 + 1024], stg[:], [stg], [wq])
                    ii += 1
                stg = k.rot("mlstage", [128, 1024], F32, n=2)
                k.dma(stg[:, 0:32], win.t[jl, j * 128:(j + 1) * 128, 6144:6176], [win], [stg])
                k.copy("dve", wq[:, j, 4096:4128], stg[:, 0:32], [stg], [wq])
        gbb = S.sb([128, 32], F32, name="ml_gbb")
        k.dma(gbb[:], W["mlstm_gate_b"].t[jl].rearrange("a b c -> (a b c)").partition_broadcast(128),
              [W["mlstm_gate_b"]], [gbb])
        Cf = [[S.sb([128, 257], F32, name=f"ml_Cf{d}_{h}") for h in range(ML_H)] for d in range(2)]
        Cb = [[S.sb([128, 257], BF16, name=f"ml_Cb{d}_{h}") for h in range(ML_H)] for d in range(2)]
        for d in range(2):
            for h in range(ML_H):
                k.memset("pool", Cf[d][h][:], 0.0, [Cf[d][h]])
                k.memset("pool", Cb[d][h][:], 0.0, [Cb[d][h]])
        for sq in seqs:
            for d in range(2):
                _mlstm_sweep(k, sq, d, wq, gbb, Cf[d], Cb[d], scr)
    with k.scope():
        w_out = load_w_bf16(k, f"ml_wout{li}", lambda j, c0, sc: W["mlstm_w_out"].t[jl, j * 128:(j + 1) * 128, c0:c0 + sc],
                            W["mlstm_w_out"], 16, D)
        for sq in seqs:
            if sq["want_out"]:
                pass_out(k, sq["YGs"], 16, w_out, sq["Tn"], sq["src"], sq["M"], sq["dst"])


def _ml_state(k, Kh, Vp, Cf, Cb, bg):
    for h in range(ML_H):
        psC = k.psum()
        k.mm(psC[:, 0:257], Kh[:, h, :], Vp[:, h, :], True, True, [Kh, Vp], [psC])
        k.stt(Cf[h][:], Cf[h][:], bg[:, 8 + h:9 + h], psC[:, 0:257], ALU.mult, ALU.add, [Cf[h], bg, psC], [Cf[h]])
        k.copy("act" if h % 2 == 0 else "pool", Cb[h][:], Cf[h][:], [Cf[h]], [Cb[h]])


def _mlstm_sweep(k, sq, d, wq, gbb, Cf, Cb, scr):
    nc, S = k.nc, k.S
    Tn = sq["Tn"]
    nch = Tn // 128
    HTs, HFs, SZs, YGs = sq["HTs"], sq["HFs"], sq["SZs"], sq["YGs"]
    tri = k.m_le if d == 0 else k.m_ge
    order = range(nch) if d == 0 else range(nch - 1, -1, -1)
    fin = (d == 1) and sq["want_out"]
    dkscale = ML_DK ** -0.5

    def ones_col(b):
        k.memset("pool", b[:, :, 256:257], 1.0, [b])

    for c in order:
        t0 = c * 128
        hT = k.rot("ms_hT", [128, 8, 128], BF16, n=2)
        k.dma(hT[:], HTs.t[:, t0:t0 + 128].rearrange("(j p) t -> p j t", p=128), [HTs], [hT])
        psg = k.psum()
        for j in range(KD):
            k.mm(psg[:, 0:32], hT[:, j, :], wq[:, j, 4096:4128], j == 0, j == KD - 1, [hT, wq], [psg])
        g = k.rot("ms_g", [128, 32], F32, n=2)
        k.tt("dve", g[:], psg[:, 0:32], gbb[:], ALU.add, [psg, gbb], [g])
        gi = g[:, d * 16:d * 16 + 8]
        gf = g[:, d * 16 + 8:d * 16 + 16]
        ef = k.rot("ms_ef", [128, 8], F32, n=2)
        k.act(ef[:], gf, AF.Exp, [g], [ef], scale=-1.0)
        lfn = k.rot("ms_lfn", [128, 8], F32, n=2)
        k.act(lfn[:], ef[:], AF.Ln, [ef, k.oneb], [lfn], bias=k.oneb[:, 0:1])
        qT = k.rot("ms_qT", [128, 8, 128], BF16, n=2)
        kT = k.rot("ms_kT", [128, 8, 128], BF16, n=2)
        for (dst, col0, scl) in ((qT, 0, None), (kT, 1024, dkscale)):
            for hb in range(2):
                ps = k.psum()
                for hh in range(4):
                    h = hb * 4 + hh
                    for j in range(KD):
                        k.mm(ps[:, hh * 128:(hh + 1) * 128], wq[:, j, col0 + h * 128:col0 + (h + 1) * 128], hT[:, j, :],
                             j == 0, j == KD - 1, [wq, hT], [ps])
                dv = dst[:, hb * 4:(hb + 1) * 4, :]
                sv = ps[:, :].rearrange("p (h t) -> p h t", h=4)
                if scl is None:
                    k.copy("act", dv, sv, [ps], [dst])
                else:
                    k.ts("dve", dv, sv, scl, None, ALU.mult, None, [ps], [dst])
        psF = k.psum()
        k.mm(psF[:, 0:8], tri[:], lfn[:], True, True, [tri, lfn], [psF])
        k.mm(psF[:, 8:16], k.ones[:, 0:128], lfn[:], True, True, [k.ones, lfn], [psF])
        Fs = k.rot("ms_Fs", [128, 16], F32, n=2)
        k.copy("dve", Fs[:], psF[:, 0:16], [psF], [Fs])
        bg = k.rot("ms_bg", [128, 16], F32, n=2)
        k.act(bg[:], Fs[:], AF.Exp, [Fs], [bg], scale=-1.0)
        al = k.rot("ms_al", [128, 8], F32, n=2)
        k.tt("dve", al[:], Fs[:, 0:8], gi, ALU.add, [Fs, g], [al])
        k.act(al[:], al[:], AF.Exp, [al], [al])
        om = k.rot("ms_om", [128, 8], F32, n=2)
        k.stt(om[:], al[:], dkscale, bg[:, 8:16], ALU.mult, ALU.mult, [al, bg], [om])
        Kh = k.rot("ms_Kh", [128, 8, 128], BF16, n=2)
        for hb in range(2):
            ps = k.psum()
            for j in range(KD):
                k.mm(ps[:], hT[:, j, :], wq[:, j, 1024 + hb * 512:1024 + (hb + 1) * 512], j == 0, j == KD - 1, [hT, wq], [ps])
            for hh in range(4):
                h = hb * 4 + hh
                if hb == 0:
                    k.ts("dve", Kh[:, h, :], ps[:, hh * 128:(hh + 1) * 128], om[:, h:h + 1], None, ALU.mult, None,
                         [ps, om], [Kh])
                else:
                    k.act(Kh[:, h, :], ps[:, hh * 128:(hh + 1) * 128], AF.Identity, [ps, om], [Kh], scale=om[:, h:h + 1])
        Vp = k.rot("ms_Vp", [128, 8, 257], BF16, n=2, init=ones_col)
        for vb in range(4):
            ps = k.psum()
            for j in range(KD):
                k.mm(ps[:], hT[:, j, :], wq[:, j, 2048 + vb * 512:2048 + (vb + 1) * 512], j == 0, j == KD - 1, [hT, wq], [ps])
            dv = Vp[:, vb * 2:(vb + 1) * 2, 0:256]
            sv = ps[:, :].rearrange("p (h v) -> p h v", h=2)
            k.copy("act" if vb % 2 == 0 else "dve", dv, sv, [ps], [Vp])
        psden = k.psum()
        psH = [None] * 4
        Ps = []
        mask = k.m_le if d == 0 else k.m_ge
        for h in range(ML_H):
            if h % 4 == 0:
                psS = k.psum()
            cs = slice((h % 4) * 128, (h % 4 + 1) * 128)
            k.mm(psS[:, cs], kT[:, h, :], qT[:, h, :], True, True, [kT, qT], [psS])
            P = k.rot("ms_P", [128, 128], BF16, n=4)
            k.stt(P[:], psS[:, cs], al[:, h:h + 1], mask[:], ALU.mult, ALU.mult, [psS, al, mask], [P])
            if h % 2 == 0:
                psH[h // 2] = k.psum()
            pH = psH[h // 2]
            hs_ = slice((h % 2) * 256, (h % 2 + 1) * 256)
            k.mm(pH[:, hs_], qT[:, h, :], Cb[h][:, 0:256], True, False, [qT, Cb[h]], [pH])
            k.mm(pH[:, hs_], P[:], Vp[:, h, 0:256], False, True, [P, Vp], [pH])
            k.mm(psden[:, h:h + 1], qT[:, h, :], Cb[h][:, 256:257], True, False, [qT, Cb[h]], [psden])
            k.mm(psden[:, h:h + 1], P[:], Vp[:, h, 256:257], False, True, [P, Vp], [psden])
        dd = k.rot("ms_dd", [128, 8], F32, n=2)
        k.tt("dve", dd[:], psden[:, 0:8], bg[:, 0:8], ALU.mult, [psden, bg], [dd])
        k.act(dd[:], dd[:], AF.Abs, [dd], [dd])
        k.ts("dve", dd[:], dd[:], 1.0, None, ALU.max, None, [dd], [dd])
        k.recip(dd[:], dd[:], [dd], [dd])
        scl = k.rot("ms_scl", [128, 8], F32, n=2)
        k.tt("dve", scl[:], dd[:], bg[:, 0:8], ALU.mult, [dd, bg], [scl])
        hd = k.rot("ms_hd", [128, ML_W], F32, n=1)
        if d == 0 or not fin:
            for h in range(ML_H):
                pH = psH[h // 2]
                hs_ = slice((h % 2) * 256, (h % 2 + 1) * 256)
                k.act(hd[:, h * 256:(h + 1) * 256], pH[:, hs_], AF.Identity, [pH, scl], [hd], scale=scl[:, h:h + 1])
            if d == 0:
                k.dma(HFs.t[t0:t0 + 128, :], hd[:], [hd], [HFs], q="pool")
            _ml_state(k, Kh, Vp, Cf, Cb, bg)
        else:
            hf = k.rot("ms_hf", [128, ML_W], F32, n=1)
            k.dma(hf[:], HFs.t[t0:t0 + 128, :], [HFs], [hf])
            szg = k.rot("ms_szg", [128, ML_W], BF16, n=1)
            k.dma(szg[:], SZs.t[t0:t0 + 128, :], [SZs], [szg])
            for h in range(ML_H):
                pH = psH[h // 2]
                hs_ = slice((h % 2) * 256, (h % 2 + 1) * 256)
                k.stt(hd[:, h * 256:(h + 1) * 256], pH[:, hs_], scl[:, h:h + 1], hf[:, h * 256:(h + 1) * 256],
                      ALU.mult, ALU.add, [pH, scl, hf], [hd])
            _ml_state(k, Kh, Vp, Cf, Cb, bg)
            sqv = k.rot("ms_sq", [128, ML_W], F32, n=1)
            k.tt("pool", sqv[:], hd[:], hd[:], ALU.mult, [hd], [sqv])
            ss = k.rot("ms_ss", [128, 8], F32, n=2)
            S.op("dve", lambda: nc.vector.tensor_reduce(out=ss[:], in_=sqv[:, :].rearrange("p (h v) -> p h v", h=8),
                                                        axis=AX.X, op=ALU.add), reads=[sqv], writes=[ss])
            rs = rstd_of(k, ss, float(ML_DV), NORM_EPS, 8)
            yt = k.rot("ms_yt", [128, ML_W], F32, n=1)
            for h in range(ML_H):
                k.stt(yt[:, h * 256:(h + 1) * 256], hd[:, h * 256:(h + 1) * 256], rs[:, h:h + 1], szg[:, h * 256:(h + 1) * 256],
                      ALU.mult, ALU.mult, [hd, rs, szg], [yt])
            yT = k.rot("ms_yT", [128, 16, 128], BF16, n=2)
            for tb in range(4):
                ps = k.psum()
                for jj in range(4):
                    j = tb * 4 + jj
                    k.tr(ps[:, jj * 128:(jj + 1) * 128], yt[:, j * 128:(j + 1) * 128], k.ident[:], [yt, k.ident], [ps])
                k.copy("act" if tb % 2 == 0 else "dve", yT[:, tb * 4:(tb + 1) * 4, :],
                       ps[:, :].rearrange("p (j t) -> p j t", j=4), [ps], [yT])
            k.dma(YGs.t[:, t0:t0 + 128].rearrange("(j p) t -> p j t", p=128), yT[:], [yT], [YGs], q="pool")
R7_L = 128
C_ED = float(np.exp(-0.5))


def _r7_load_h_sh(k, sq, t0, want_sh=True):
    Tn = sq["Tn"]
    HTs = sq["HTs"]
    hb = k.rot("r7_h", [128, 8, 128], BF16, n=1)
    k.dma(hb[:], HTs.t[:, t0:t0 + 128].rearrange("(j p) t -> p j t", p=128), [HTs], [hb])
    sh = k.rot("r7_sh", [128, 8, 128], BF16, n=1)

    def ld(j0, j1, c0, c1, s0):
        k.dma(sh[:, j0:j1, c0:c1], HTs.t[j0 * 128:j1 * 128, s0:s0 + (c1 - c0)].rearrange("(j p) t -> p j t", p=128),
              [HTs], [sh])

    def z(j0, j1, c0, c1):
        k.memset("pool", sh[:, j0:j1, c0:c1], 0.0, [sh])

    if sq["name"] == "x":
        ld(0, 2, 1, 128, t0)
        z(0, 2, 0, 1)
        z(0, 2, 64, 65)
        ld(2, 4, 0, 127, t0 + 1)
        z(2, 4, 63, 64)
        z(2, 4, 127, 128)
        if t0 >= 64:
            ld(4, 6, 0, 128, t0 - 64)
        else:
            ld(4, 6, 64, 128, 0)
            z(4, 6, 0, 64)
        if t0 + 192 <= Tn:
            ld(6, 8, 0, 128, t0 + 64)
        else:
            ld(6, 8, 0, 64, t0 + 64)
            z(6, 8, 64, 128)
    else:
        if t0 > 0:
            ld(0, 4, 0, 128, t0 - 1)
        else:
            ld(0, 4, 1, 128, 0)
            z(0, 4, 0, 1)
        if t0 + 129 <= Tn:
            ld(4, 8, 0, 128, t0 + 1)
        else:
            ld(4, 8, 0, 127, t0 + 1)
            z(4, 8, 127, 128)
    return hb, sh


def _bc(ap2, n):
    return ap2.unsqueeze(2).to_broadcast([128, 8, n])


def rwkv_layer(k, li, jl, W, seqs, scr):
    nc, S = k.nc, k.S
    for sq in seqs:
        pass_norm(k, sq["src"], sq["Tn"], sq["M"], sq["HTs"])
    wr = W["r7_w_rkvz"]
    with k.scope():
        def pload(name, ap, shape):
            b = S.sb(shape, F32, name=name)
            k.dma(b[:], ap, [W["r7_mu"]], [b], slow=True)
            return b
        mu = pload("r7_mu", W["r7_mu"].t[jl].rearrange("g (j p) -> p g j", p=128), [128, 6, 8])
        w0 = pload("r7_w0", W["r7_w0"].t[jl].rearrange("d (j p) -> p d j", p=128), [128, 2, 8])
        a0 = pload("r7_a0", W["r7_a0"].t[jl].rearrange("d (j p) -> p d j", p=128), [128, 2, 8])
        kkp = pload("r7_kk", W["r7_k_k"].t[jl].rearrange("(j p) -> p j", p=128), [128, 8])
        kap = pload("r7_ka", W["r7_k_a"].t[jl].rearrange("(j p) -> p j", p=128), [128, 8])
        rkp = pload("r7_rk", W["r7_r_k"].t[jl].rearrange("(j p) -> p j", p=128), [128, 8])
        omka = S.sb([128, 8], F32, name="r7_omka")
        k.ts("dve", omka[:], kap[:], -1.0, 1.0, ALU.mult, ALU.add, [kap], [omka])
        with k.scope():
            wz = load_w_bf16(k, "r7_wz", lambda j, c0, sc: wr.t[jl, 3, j * 128:(j + 1) * 128, c0:c0 + sc], wr, KD, D)
            for sq in seqs:
                if not sq["want_out"]:
                    continue
                for t0 in range(0, sq["Tn"], 128):
                    hb, sh = _r7_load_h_sh(k, sq, t0)
                    df = k.rot("rz_df", [128, 8, 128], F32, n=2)
                    k.tt("dve", df[:], sh[:], hb[:], ALU.subtract, [sh, hb], [df])
                    k.tt("pool", df[:], df[:], _bc(mu[:, 3, :], 128), ALU.mult, [df, mu], [df])
                    xz = k.rot("rz_xz", [128, 8, 128], BF16, n=2)
                    k.tt("dve", xz[:], df[:], hb[:], ALU.add, [df, hb], [xz])
                    szt = k.rot("rz_sz", [128, D], BF16, n=2)
                    for nb in range(2):
                        ps = k.psum()
                        for j in range(KD):
                            k.mm(ps[:], xz[:, j, :], wz[:, j, nb * 512:(nb + 1) * 512], j == 0, j == KD - 1, [xz, wz], [ps])
                        k.act(szt[:, nb * 512:(nb + 1) * 512], ps[:], AF.Silu, [ps], [szt])
                    k.dma(sq["SZs"].t[t0:t0 + 128, 0:D], szt[:], [szt], [sq["SZs"]], q="pool")
        with k.scope():
            wts = {}
            for gi, nm in ((0, "r"), (1, "k"), (2, "v")):
                wts[nm] = load_w_bf16(k, "r7_w" + nm, lambda j, c0, sc, gi=gi: wr.t[jl, gi, j * 128:(j + 1) * 128, c0:c0 + sc],
                                      wr, KD, D)
            lw1 = [load_w_bf16(k, f"r7_w1{d}", lambda j, c0, sc, d=d: W["r7_w1"].t[jl, d, j * 128:(j + 1) * 128, :], W["r7_w1"], KD, 64)
                   for d in range(2)]
            la1 = [load_w_bf16(k, f"r7_a1{d}", lambda j, c0, sc, d=d: W["r7_a1"].t[jl, d, j * 128:(j + 1) * 128, :], W["r7_a1"], KD, 64)
                   for d in range(2)]
            lw2 = [S.sb([64, D], BF16, name=f"r7_w2_{d}") for d in range(2)]
            la2 = [S.sb([64, D], BF16, name=f"r7_a2_{d}") for d in range(2)]
            with k.scope():
                for d in range(2):
                    for (lst, nm) in ((lw2, "r7_w2"), (la2, "r7_a2")):
                        stg = k.rot("r7_stg2", [64, D], F32, n=2)
                        k.dma(stg[:], W[nm].t[jl, d], [W[nm]], [stg])
                        k.copy("dve", lst[d][:], stg[:], [stg], [lst[d]])
            hmA = S.sb([128, 1], F32, name="r7_hmA")
            hmB = S.sb([128, 1], F32, name="r7_hmB")
            k.memset("pool", hmA[:], 0.0, [hmA])
            k.memset("pool", hmA[0:64, :], 1.0, [hmA])
            k.memset("pool", hmB[:], 1.0, [hmB])
            k.memset("pool", hmB[0:64, :], 0.0, [hmB])
            hsel = S.sb([128, 2], F32, name="r7_hsel")
            k.copy("pool", hsel[:, 0:1], hmA[:], [hmA], [hsel])
            k.copy("pool", hsel[:, 1:2], hmB[:], [hmB], [hsel])
            bones = S.sb([128, 128], F32, name="r7_bones")
            k.memset("pool", bones[:], 0.0, [bones])
            k.memset("pool", bones[0:64, 0:64], 1.0, [bones])
            k.memset("pool", bones[64:128, 64:128], 1.0, [bones])
            id4 = S.sb([128, 4, 128], BF16, name="r7_id4")
            for i4 in range(4):
                k.copy("pool", id4[:, i4, :], k.identb[:], [k.identb], [id4])
            smt = S.sb([128, 2, 7, 128], BF16, name="r7_smt")
            with k.scope():
                for l in range(7):
                    w2, w1 = 1 << (l + 1), 1 << l

                    def band(lo_off, hi_off, nm, w2=w2):
                        b = k.rot("r7_band_" + nm, [128, 128], F32, n=2)
                        S.op("pool", lambda: nc.gpsimd.affine_select(out=b[:], in_=k.ones[:, 0:128], pattern=[[1, 128]],
                                                                     compare_op=ALU.is_ge, fill=0.0, base=-lo_off, channel_multiplier=-w2),
                             reads=[k.ones], writes=[b])
                        S.op("pool", lambda: nc.gpsimd.affine_select(out=b[:], in_=b[:], pattern=[[-1, 128]],
                                                                     compare_op=ALU.is_gt, fill=0.0, base=hi_off, channel_multiplier=w2),
                             reads=[b], writes=[b])
                        return b
                    Lf = band(0, w1, "L")
                    Rt = band(w1, w2, "R")
                    for dd, (lt_, rh_) in ((0, (Rt, Lf)), (1, (Lf, Rt))):
                        ps = k.psum()
                        k.mm(ps[:, 0:128], lt_[:], rh_[:], True, True, [lt_, rh_], [ps])
                        k.copy("act", smt[:, dd, l, :], ps[:, 0:128], [ps], [smt])
            Z = [S.sb([128, 8, 64], F32, name=f"r7_Z{d}") for d in range(2)]
            Zb = [S.sb([128, 8, 64], BF16, name=f"r7_Zb{d}") for d in range(2)]
            for d in range(2):
                k.memset("pool", Z[d][:], 0.0, [Z[d]])
                k.memset("pool", Zb[d][:], 0.0, [Zb[d]])
            lng = S.sb([128, D], F32, name="r7_lng")
            k.dma(lng[:], W["r7_ln_g"].t[jl].partition_broadcast(128), [W["r7_ln_g"]], [lng])
            lnb = S.sb([128, D], F32, name="r7_lnb")
            k.dma(lnb[:], W["r7_ln_b"].t[jl].partition_broadcast(128), [W["r7_ln_b"]], [lnb])
            cst = dict(mu=mu, w0=w0, a0=a0, kkp=kkp, kap=kap, rkp=rkp, omka=omka, wts=wts, lw1=lw1, la1=la1, lw2=lw2, la2=la2,
                       hm=(hmA, hmB), hsel=hsel, bones=bones, lng=lng, lnb=lnb, id4=id4, smt=smt,
                       RS=dict(r=scr["R7r"], k=scr["R7k"], kk=scr["R7kk"], v=scr["R7v"]))
            for sq in seqs:
                for d in range(2):
                    nch = sq["Tn"] // 128
                    order = range(nch) if d == 0 else range(nch - 1, -1, -1)
                    for c in order:
                        _r7_chunk(k, sq, d, c * 128, cst, Z[d], Zb[d])
        with k.scope():
            w_out = load_w_bf16(k, "r7_wout", lambda j, c0, sc: W["r7_w_out"].t[jl, j * 128:(j + 1) * 128, c0:c0 + sc],
                                W["r7_w_out"], KD, D)
            for sq in seqs:
                if sq["want_out"]:
                    pass_out(k, sq["YGs"], KD, w_out, sq["Tn"], sq["src"], sq["M"], sq["dst"])


def _r7_chunk(k, sq, d, t0, c, Z, Zb):
    nc, S = k.nc, k.S
    mu, wts = c["mu"], c["wts"]
    fin = (d == 1) and sq["want_out"]
    F = lambda i: k.rot(f"r7_F{i}", [128, 8, 128], F32, n=1)
    hb, sh = _r7_load_h_sh(k, sq, t0)
    df = F(0)
    k.tt("dve", df[:], sh[:], hb[:], ALU.subtract, [sh, hb], [df])

    def lerp(g, eng="dve"):
        t = F(1)
        k.tt("pool", t[:], df[:], _bc(mu[:, g, :], 128), ALU.mult, [df, mu], [t])
        o = k.rot("r7_xs", [128, 8, 128], BF16, n=2)
        k.tt(eng, o[:], t[:], hb[:], ALU.add, [t, hb], [o])
        return o

    def proj_fm(xs, wt, evac):
        for jb in range(2):
            ps = k.psum()
            for jj in range(4):
                j = jb * 4 + jj
                for kk_ in range(KD):
                    k.mm(ps[:, jj * 128:(jj + 1) * 128], wt[:, kk_, j * 128:(j + 1) * 128], xs[:, kk_, :], kk_ == 0, kk_ == KD - 1,
                         [wt, xs], [ps])
            evac(ps, jb)

    def v4(ps):
        return ps[:, :].rearrange("p (j t) -> p j t", j=4)

    if DBG.get("r7") == "a":
        return
    rT = F(2)
    kT = F(3)
    Vb = k.rot("r7_Vb", [128, D], BF16, n=1)
    RS = c["RS"]
    reuse = (d == 1)
    rows = slice(t0, t0 + 128)
    if not reuse:
        xr = lerp(0)
        proj_fm(xr, wts["r"], lambda ps, jb: k.copy("act", rT[:, jb * 4:(jb + 1) * 4, :], v4(ps), [ps], [rT]))
        xk = lerp(1)
        proj_fm(xk, wts["k"], lambda ps, jb: k.copy("dve", kT[:, jb * 4:(jb + 1) * 4, :], v4(ps), [ps], [kT]))
        xv = lerp(2)
        for nb in range(2):
            ps = k.psum()
            for kk_ in range(KD):
                k.mm(ps[:], xv[:, kk_, :], wts["v"][:, kk_, nb * 512:(nb + 1) * 512], kk_ == 0, kk_ == KD - 1, [xv, wts["v"]], [ps])
            k.copy("act", Vb[:, nb * 512:(nb + 1) * 512], ps[:], [ps], [Vb])
        k.dma(RS["r"].t[rows, :].rearrange("p (j t) -> p j t", j=8), rT[:], [rT], [RS["r"]], q="pool")
        k.dma(RS["k"].t[rows, :].rearrange("p (j t) -> p j t", j=8), kT[:], [kT], [RS["k"]], q="pool")
        k.dma(RS["v"].t[rows, :], Vb[:], [Vb], [RS["v"]], q="pool")
    if DBG.get("r7") == "b":
        return
    lows = []
    for (g, w1, fn) in ((4, c["lw1"][d], AF.Tanh), (5, c["la1"][d], AF.Identity)):
        xs_ = lerp(g)
        ps = k.psum()
        for kk_ in range(KD):
            k.mm(ps[0:64, 0:128], w1[:, kk_, :], xs_[:, kk_, :], kk_ == 0, kk_ == KD - 1, [w1, xs_], [ps])
        lo = k.rot("r7_lo", [64, 128], BF16, n=2)
        k.act(lo[:], ps[0:64, 0:128], fn, [ps], [lo])
        lows.append(lo)
    if reuse:
        k.dma(rT[:], RS["r"].t[rows, :].rearrange("p (j t) -> p j t", j=8), [RS["r"]], [rT])
        k.dma(kT[:], RS["k"].t[rows, :].rearrange("p (j t) -> p j t", j=8), [RS["k"]], [kT])
        k.dma(Vb[:], RS["v"].t[rows, :], [RS["v"]], [Vb])
    lwv = F(4)
    aT = F(5)
    for (lo, w2, dst, bias) in ((lows[0], c["lw2"][d], lwv, c["w0"]), (lows[1], c["la2"][d], aT, c["a0"])):
        for jb in range(2):
            ps = k.psum()
            for jj in range(4):
                j = jb * 4 + jj
                k.mm(ps[:, jj * 128:(jj + 1) * 128], w2[:, j * 128:(j + 1) * 128], lo[:], True, True, [w2, lo], [ps])
            for jj in range(4):
                j = jb * 4 + jj
                k.act(dst[:, j, :], ps[:, jj * 128:(jj + 1) * 128], AF.Sigmoid, [ps, bias], [dst], bias=bias[:, d, j:j + 1])
    k.ts("pool", lwv[:], lwv[:], -C_ED, None, ALU.mult, None, [lwv], [lwv])
    if DBG.get("r7") == "c":
        return
    k.dump("rT", rT[:], [128, 8, 128], F32, [rT])
    k.dump("kT", kT[:], [128, 8, 128], F32, [kT])
    k.dump("lw", lwv[:], [128, 8, 128], F32, [lwv])
    k.dump("aT", aT[:], [128, 8, 128], F32, [aT])
    k.dump("Vf", Vb[:], [128, D], BF16, [Vb])
    kkr = F(0)
    if reuse:
        k.dma(kkr[:], RS["kk"].t[rows, :].rearrange("p (j t) -> p j t", j=8), [RS["kk"]], [kkr])
    if not reuse:
        pass
        k.tt("dve", kkr[:], kT[:], _bc(c["kkp"][:, :], 128), ALU.mult, [kT, c["kkp"]], [kkr])
        kk2 = F(1)
        k.tt("pool", kk2[:], kkr[:], kkr[:], ALU.mult, [kkr], [kk2])
        nrm = F(6)
        for jb in range(2):
            ps = k.psum()
            for jj in range(4):
                j = jb * 4 + jj
                k.mm(ps[:, jj * 128:(jj + 1) * 128], c["bones"][:], kk2[:, j, :], True, True, [c["bones"], kk2], [ps])
            k.act(nrm[:, jb * 4:(jb + 1) * 4, :], v4(ps), AF.Sqrt, [ps], [nrm])
        k.ts("dve", nrm[:], nrm[:], 1e-12, None, ALU.max, None, [nrm], [nrm])
        k.recip(nrm[:], nrm[:], [nrm], [nrm])
        k.tt("pool", kkr[:], kkr[:], nrm[:], ALU.mult, [kkr, nrm], [kkr])
        k.dma(RS["kk"].t[rows, :].rearrange("p (j t) -> p j t", j=8), kkr[:], [kkr], [RS["kk"]], q="pool")
    kd = F(7)
    k.tt("dve", kd[:], aT[:], _bc(c["kap"][:, :], 128), ALU.mult, [aT, c["kap"]], [kd])
    k.tt("pool", kd[:], kd[:], _bc(c["omka"][:, :], 128), ALU.add, [kd, c["omka"]], [kd])
    k.tt("dve", kd[:], kd[:], kT[:], ALU.mult, [kd, kT], [kd])
    if DBG.get("r7") == "d":
        return
    k.dump("kk", kkr[:], [128, 8, 128], F32, [kkr])
    k.dump("kd", kd[:], [128, 8, 128], F32, [kd])
    pr = F(1)
    k.tt("pool", pr[:], rT[:], _bc(c["rkp"][:, :], 128), ALU.mult, [rT, c["rkp"]], [pr])
    k.tt("dve", pr[:], pr[:], kd[:], ALU.mult, [pr, kd], [pr])
    psb = k.psum()
    for j in range(8):
        k.mm(psb[:, 2 * j:2 * j + 2], pr[:, j, :], c["hsel"][:], True, True, [pr, c["hsel"]], [psb])
    bs = k.rot("r7_bs", [128, 16], F32, n=2)
    k.copy("act", bs[:], psb[:, 0:16], [psb], [bs])
    if DBG.get("r7") == "e":
        return
    cs = F(1)
    for j in range(8):
        if d == 0:
            S.op("dve", lambda j=j: nc.vector.tensor_tensor_scan(out=cs[:, j, :], data0=k.ones[:, 0:128], data1=lwv[:, j, :], initial=0.0,
                                                                 op0=ALU.mult, op1=ALU.add), reads=[k.ones, lwv], writes=[cs])
        else:
            S.op("dve", lambda j=j: nc.vector.tensor_tensor_scan(out=cs[:, j, ::-1], data0=k.ones[:, 0:128], data1=lwv[:, j, ::-1],
                                                                 initial=0.0, op0=ALU.mult, op1=ALU.add),
                 reads=[k.ones, lwv], writes=[cs])
    gam = F(3)
    k.act(gam[:], cs[:], AF.Exp, [cs], [gam])
    gin = F(6)
    k.act(gin[:], cs[:], AF.Exp, [cs], [gin], scale=-1.0)
    gpv = lwv
    k.tt("pool", gpv[:], cs[:], lwv[:], ALU.subtract, [cs, lwv], [gpv])
    k.act(gpv[:], gpv[:], AF.Exp, [gpv], [gpv])
    k.dump("cs", cs[:], [128, 8, 128], F32, [cs])
    k.dump("gam", gam[:], [128, 8, 128], F32, [gam])
    k.dump("gin", gin[:], [128, 8, 128], F32, [gin])
    k.dump("gpv", gpv[:], [128, 8, 128], F32, [gpv])
    gL = k.rot("r7_gL", [128, 8], F32, n=2)
    lastcol = 127 if d == 0 else 0
    k.copy("pool", gL[:], gam[:, :, lastcol], [gam], [gL])
    if DBG.get("r7") == "f":
        return
    hmA, hmB = c["hm"]
    rt_ = rT
    k.tt("dve", rt_[:], rT[:], gam[:], ALU.mult, [rT, gam], [rt_])
    at_ = aT
    k.tt("dve", at_[:], kkr[:], aT[:], ALU.mult, [kkr, aT], [at_])
    k.tt("pool", at_[:], at_[:], gin[:], ALU.mult, [at_, gin], [at_])
    kt_ = kkr
    k.tt("pool", kt_[:], kkr[:], gpv[:], ALU.mult, [kkr, gpv], [kt_])
    kdt = kd
    k.tt("dve", kdt[:], kd[:], gin[:], ALU.mult, [kd, gin], [kdt])
    ops = {}
    ei = 0
    for nm, src in (("rt", rt_), ("kt", kt_), ("at", at_), ("kd", kdt)):
        hA = k.rot("r7_bA_" + nm, [128, 8, 128], BF16, n=1)
        hB = k.rot("r7_bB_" + nm, [128, 8, 128], BF16, n=1)
        if ei % 2 == 0:
            k.ts("dve", hA[:], src[:], hmA[:, 0:1], None, ALU.mult, None, [src, hmA], [hA])
            k.act(hB[:], src[:], AF.Identity, [src, hmB], [hB], scale=hmB[:, 0:1])
        else:
            k.act(hA[:], src[:], AF.Identity, [src, hmA], [hA], scale=hmA[:, 0:1])
            k.ts("dve", hB[:], src[:], hmB[:, 0:1], None, ALU.mult, None, [src, hmB], [hB])
        ei += 1
        ops[nm] = (None, hA, hB)
    if DBG.get("r7") == "g":
        return
    toks = {}
    for nm, src in (("at", at_), ("kd", kdt)):
        tk = k.rot("r7_tok_" + nm, [128, 8, 128], BF16, n=1)
        for jb in range(2):
            ps = k.psum()
            for jj in range(4):
                j = jb * 4 + jj
                k.tr(ps[:, jj * 128:(jj + 1) * 128], src[:, j, :], k.ident[:], [src, k.ident], [ps])
            k.copy("act" if jb == 0 else "dve", tk[:, jb * 4:(jb + 1) * 4, :], v4(ps), [ps], [tk])
        toks[nm] = tk
    if DBG.get("r7") == "h":
        return
    if d == 0:
        mb_lt, mb_gt, mb_le = k.m_lt, k.m_gt, k.m_le
    else:
        mb_lt, mb_gt, mb_le = k.m_gt, k.m_lt, k.m_ge
    b4 = lambda t: t[:, :].unsqueeze(1).to_broadcast([128, 4, 128])
    MkT = k.rot("r7_MkT", [128, 16, 128], BF16, n=1)
    NaT = k.rot("r7_NaT", [128, 16, 128], BF16, n=1)
    NkT = k.rot("r7_NkT", [128, 16, 128], BF16, n=1)
    TT = k.rot("r7_TT", [128, 16, 128], BF16, n=1)

    def hop(nm, h):
        return ops[nm][1 + (h % 2)][:, h // 2, :], ops[nm][1 + (h % 2)]

    def fop(nm, h):
        return hop(nm, h)

    ATs = []
    for hg in range(4):
        heads = range(hg * 4, hg * 4 + 4)
        hs4 = slice(hg * 4, hg * 4 + 4)

        def form(lnm, rnm, evac, heads=heads):
            ps = k.psum()
            for i4, h in enumerate(heads):
                la, lb_ = hop(lnm, h)
                ra, rb_ = fop(rnm, h)
                k.mm(ps[:, i4 * 128:(i4 + 1) * 128], la, ra, True, True, [lb_, rb_], [ps])
            evac(ps)
        AT = k.rot("r7_AT", [128, 4, 128], BF16, n=4)
        ATs.append(AT)
        form("kt", "at", lambda ps: k.stt(AT[:], v4(ps), -1.0, b4(mb_gt), ALU.mult, ALU.mult, [ps, mb_gt], [AT]))
        form("kd", "kt", lambda ps: k.tt("dve", MkT[:, hs4, :], v4(ps), b4(mb_lt), ALU.mult, [ps, mb_lt], [MkT]))
        form("at", "rt", lambda ps: k.tt("dve", NaT[:, hs4, :], v4(ps), b4(mb_le), ALU.mult, [ps, mb_le], [NaT]))
        form("kd", "rt", lambda ps: k.tt("dve", NkT[:, hs4, :], v4(ps), b4(mb_le), ALU.mult, [ps, mb_le], [NkT]))
    X = [c["id4"]] * 4
    XT = [c["id4"]] * 4
    for l in range(7):
        CTs, P1s, pss = [], [], []
        for hg in range(4):
            CT = k.rot("r7_CT", [128, 4, 128], BF16, n=4)
            k.tt("pool", CT[:], ATs[hg][:], c["smt"][:, d, l, :].unsqueeze(1).to_broadcast([128, 4, 128]), ALU.mult,
                 [ATs[hg], c["smt"]], [CT])
            CTs.append(CT)
        for hg in range(4):
            ps = k.psum()
            for i4 in range(4):
                k.mm(ps[:, i4 * 128:(i4 + 1) * 128], CTs[hg][:, i4, :], X[hg][:, i4, :], True, True, [CTs[hg], X[hg]], [ps])
            P1 = k.rot("r7_P1", [128, 4, 128], BF16, n=4)
            k.copy("act" if hg % 2 == 0 else "dve", P1[:], v4(ps), [ps], [P1])
            P1s.append(P1)
        for hg in range(4):
            ps = k.psum()
            for i4 in range(4):
                k.mm(ps[:, i4 * 128:(i4 + 1) * 128], k.identb[:], X[hg][:, i4, :], True, False, [k.identb, X[hg]], [ps])
                k.mm(ps[:, i4 * 128:(i4 + 1) * 128], XT[hg][:, i4, :], P1s[hg][:, i4, :], False, True, [XT[hg], P1s[hg]], [ps])
            pss.append(ps)
        Xn_l = []
        for hg in range(4):
            hs4 = slice(hg * 4, hg * 4 + 4)
            if l < 6:
                Xn = k.rot("r7_X", [128, 4, 128], BF16, n=4)
                k.copy("dve" if hg % 2 == 0 else "act", Xn[:], v4(pss[hg]), [pss[hg]], [Xn])
                Xn_l.append(Xn)
            else:
                k.copy("dve" if hg % 2 == 0 else "act", TT[:, hs4, :], v4(pss[hg]), [pss[hg]], [TT])
        if l < 6:
            XTn_l = []
            for hg in range(4):
                ps = k.psum()
                for i4 in range(4):
                    k.mm(ps[:, i4 * 128:(i4 + 1) * 128], Xn_l[hg][:, i4, :], k.identb[:], True, True, [Xn_l[hg], k.identb], [ps])
                XTn = k.rot("r7_XT", [128, 4, 128], BF16, n=4)
                k.copy("act" if hg % 2 == 0 else "dve", XTn[:], v4(ps), [ps], [XTn])
                XTn_l.append(XTn)
            X, XT = Xn_l, XTn_l
    k.dump("TT", TT[:], [128, 16, 128], BF16, [TT])
    k.dump("MkT", MkT[:], [128, 16, 128], BF16, [MkT])
    k.dump("NaT", NaT[:], [128, 16, 128], BF16, [NaT])
    k.dump("NkT", NkT[:], [128, 16, 128], BF16, [NkT])
    def v8(ps):
        return ps[:, :].rearrange("p (h v) -> p h v", h=8)
    Bs = k.rot("r7_Bs", [128, 16, 64], BF16, n=1)
    for hb_ in range(2):
        ps = k.psum()
        for i8 in range(8):
            h = hb_ * 8 + i8
            cs_ = slice(i8 * 64, (i8 + 1) * 64)
            k.mm(ps[:, cs_], MkT[:, h, :], Vb[:, h * 64:(h + 1) * 64], True, False, [MkT, Vb], [ps])
            la, lb_ = hop("kt", h)
            k.mm(ps[:, cs_], la, Zb[:, h // 2, :], False, True, [lb_, Zb], [ps])
        k.copy("act" if hb_ == 0 else "dve", Bs[:, hb_ * 8:(hb_ + 1) * 8, :], v8(ps), [ps], [Bs])
    Us = k.rot("r7_Us", [128, 16, 64], BF16, n=1)
    for hb_ in range(2):
        ps = k.psum()
        for i8 in range(8):
            h = hb_ * 8 + i8
            k.mm(ps[:, i8 * 64:(i8 + 1) * 64], TT[:, h, :], Bs[:, h, :], True, True, [TT, Bs], [ps])
        if hb_ == 0:
            k.act(Us[:, 0:8, :], v8(ps), AF.Identity, [ps], [Us], scale=-1.0)
        else:
            k.ts("dve", Us[:, 8:16, :], v8(ps), -1.0, None, ALU.mult, None, [ps], [Us])
    psY = [k.psum(), k.psum()]
    for h in range(16):
        ps = psY[h // 8]
        cs_ = slice((h % 8) * 64, (h % 8 + 1) * 64)
        la, lb_ = hop("rt", h)
        k.mm(ps[:, cs_], la, Zb[:, h // 2, :], True, False, [lb_, Zb], [ps])
        k.mm(ps[:, cs_], NaT[:, h, :], Us[:, h, :], False, False, [NaT, Us], [ps])
        k.mm(ps[:, cs_], NkT[:, h, :], Vb[:, h * 64:(h + 1) * 64], False, True, [NkT, Vb], [ps])
    yv = F(2)
    yv2 = V2(yv)
    bsum = bs
    if fin:
        yf_ = F(3)
        yf = V2(yf_)
        k.dma(yf[:], sq["HFs"].t[t0:t0 + 128, 0:D], [sq["HFs"]], [yf_])
        bsf = k.rot("r7_bsf", [128, 16], F32, n=2)
        k.dma(bsf[:], sq["HFs"].t[t0:t0 + 128, D:D + 16], [sq["HFs"]], [bsf])
        for hb_ in range(2):
            k.tt("dve", yv2[:, hb_ * 512:(hb_ + 1) * 512], psY[hb_][:], yf[:, hb_ * 512:(hb_ + 1) * 512], ALU.add, [psY[hb_], yf_], [yv])
        k.tt("pool", bsf[:], bsf[:], bs[:], ALU.add, [bsf, bs], [bsf])
        bsum = bsf
    else:
        k.copy("act", yv2[:, 0:512], psY[0][:], [psY[0]], [yv])
        k.copy("dve", yv2[:, 512:1024], psY[1][:], [psY[1]], [yv])
        if d == 0:
            k.dma(sq["HFs"].t[t0:t0 + 128, 0:D], yv2[:], [yv], [sq["HFs"]], q="pool")
            k.dma(sq["HFs"].t[t0:t0 + 128, D:D + 16], bs[:], [bs], [sq["HFs"]], q="pool")
    if DBG.get("r7") == "j":
        return
    k.dump("yv", yv[:], [128, 8, 128], F32, [yv])
    k.dump("Us", Us[:], [128, 16, 64], BF16, [Us])
    k.dump("Bs", Bs[:], [128, 16, 64], BF16, [Bs])
    for jb in range(2):
        ps = k.psum()
        for jj in range(4):
            j = jb * 4 + jj
            cs_ = slice(jj * 128, (jj + 1) * 128)
            k.mm(ps[:, cs_], toks["at"][:, j, :], Us[:, 2 * j:2 * j + 2, :].rearrange("p h v -> p (h v)"), True, False, [toks["at"], Us], [ps])
            k.mm(ps[:, cs_], toks["kd"][:, j, :], Vb[:, j * 128:(j + 1) * 128], False, True, [toks["kd"], Vb], [ps])
        pv = ps[:, :].rearrange("p (j x) -> p j x", j=4)
        js = slice(jb * 4, jb * 4 + 4)
        k.tt("dve", Z[0:64, js, :], Z[0:64, js, :], pv[0:64, :, 0:64], ALU.add, [Z, ps], [Z])
        k.tt("dve", Z[64:128, js, :], Z[64:128, js, :], pv[64:128, :, 64:128], ALU.add, [Z, ps], [Z])
    k.tt("pool", Z[:], Z[:], gL[:, :].unsqueeze(2).to_broadcast([128, 8, 64]), ALU.mult, [Z, gL], [Z])
    k.copy("act", Zb[:], Z[:], [Z], [Zb])
    k.dump("Z", Z[:], [128, 8, 64], F32, [Z])
    if not fin:
        return
    if DBG.get("r7") == "l":
        return
    y3 = yv2[:, :].rearrange("p (h c) -> p h c", h=16)
    st1 = k.rot("r7_st1", [128, 16], F32, n=2)
    S.op("dve", lambda: nc.vector.tensor_reduce(out=st1[:], in_=y3, axis=AX.X, op=ALU.add), reads=[yv], writes=[st1])
    k.ts("dve", st1[:], st1[:], -1.0 / 64.0, None, ALU.mult, None, [st1], [st1])
    b16 = lambda t: t[:, :].unsqueeze(2).to_broadcast([128, 16, 64])
    k.tt("pool", y3, y3, b16(st1), ALU.add, [yv, st1], [yv])
    sqv_ = F(5)
    sqv = V2(sqv_)
    k.tt("dve", sqv[:], yv2[:], yv2[:], ALU.mult, [yv], [sqv_])
    st2 = k.rot("r7_st2", [128, 16], F32, n=2)
    S.op("dve", lambda: nc.vector.tensor_reduce(out=st2[:], in_=sqv[:, :].rearrange("p (h c) -> p h c", h=16), axis=AX.X, op=ALU.add),
         reads=[sqv_], writes=[st2])
    rs = rstd_of(k, st2, 64.0, GN_EPS, 16)
    k.tt("pool", y3, y3, b16(rs), ALU.mult, [yv, rs], [yv])
    k.tt("dve", yv2[:], yv2[:], c["lng"][:], ALU.mult, [yv, c["lng"]], [yv])
    k.tt("pool", yv2[:], yv2[:], c["lnb"][:], ALU.add, [yv, c["lnb"]], [yv])
    bon = sqv
    k.tt("dve", bon[:, :].rearrange("p (h c) -> p h c", h=16), Vb[:, :].rearrange("p (h c) -> p h c", h=16), b16(bsum), ALU.mult,
         [Vb, bsum], [sqv_])
    k.tt("pool", yv2[:], yv2[:], bon[:], ALU.add, [yv, sqv_], [yv])
    szt_ = k.rot("r7_xs", [128, 8, 128], BF16, n=2)
    szt = V2(szt_)
    k.dma(szt[:], sq["SZs"].t[t0:t0 + 128, 0:D], [sq["SZs"]], [szt_])
    k.tt("dve", yv2[:], yv2[:], szt[:], ALU.mult, [yv, szt_], [yv])
    yT = k.rot("r7_xs", [128, 8, 128], BF16, n=2)
    for tb in range(2):
        ps = k.psum()
        for jj in range(4):
            j = tb * 4 + jj
            k.tr(ps[:, jj * 128:(jj + 1) * 128], yv2[:, j * 128:(j + 1) * 128], k.ident[:], [yv, k.ident], [ps])
        k.copy("act" if tb == 0 else "dve", yT[:, tb * 4:(tb + 1) * 4, :], v4(ps), [ps], [yT])
    k.dma(sq["YGs"].t[0:D, t0:t0 + 128].rearrange("(j p) t -> p j t", p=128), yT[:], [yT], [sq["YGs"]], q="pool")


WNAMES = ["norm_g", "mod_w", "mod_b", "final_g",
          "lru_w_in", "lru_conv_w", "lru_conv_b", "lru_gate_w", "lru_gate_b", "lru_lam", "lru_w_out",
          "mlstm_w_in", "mlstm_gate_b", "mlstm_norm_g", "mlstm_w_out",
          "r7_mu", "r7_w_rkvz", "r7_w0", "r7_w1", "r7_w2", "r7_a0", "r7_a1", "r7_a2",
          "r7_k_k", "r7_k_a", "r7_r_k", "r7_ln_g", "r7_ln_b", "r7_w_out"]


def build(T, CTX, kinds, wshapes):
    nc = bass.Bass("TRN2", target_bir_lowering=False)
    with contextlib.ExitStack() as st:
        S = Sch(nc, st)
        k = K(nc, S)
        W = {}
        W["x"] = Buf(nc.dram_tensor("x", [T, D], F32, kind="ExternalInput"), "x")
        W["ctx"] = Buf(nc.dram_tensor("ctx", [CTX, D], F32, kind="ExternalInput"), "ctx")
        W["c"] = Buf(nc.dram_tensor("c", [D], F32, kind="ExternalInput"), "c")
        W["c_ctx"] = Buf(nc.dram_tensor("c_ctx", [D], F32, kind="ExternalInput"), "c_ctx")
        for n in WNAMES:
            W[n] = Buf(nc.dram_tensor(n, list(wshapes[n]), F32, kind="ExternalInput"), n)
        OUT = Buf(nc.dram_tensor("out", [T, D], F32, kind="ExternalOutput"), "out")
        Tm = max(T, CTX)
        scr = {
            "HT": S.dram("s_HT", [D, Tm], BF16),
            "UT": S.dram("s_UT", [LRU_W, Tm], F32),
            "SZ": S.dram("s_SZ", [LRU_W, Tm], BF16),
            "UC": S.dram("s_UC", [LRU_W, Tm], F32),
            "HF": S.dram("s_HF", [LRU_W, Tm], F32),
            "YG": S.dram("s_YG", [2048, Tm], BF16),
            "YGC": S.dram("s_YGC", [2048, CTX], BF16),
            "R7r": S.dram("s_R7r", [Tm, D], F32),
            "R7k": S.dram("s_R7k", [Tm, D], F32),
            "R7kk": S.dram("s_R7kk", [Tm, D], F32),
            "R7v": S.dram("s_R7v", [Tm, D], BF16),
            "HTC": S.dram("s_HTC", [D, CTX], BF16),
            "HFT": S.dram("s_HFT", [T, 2048], F32),
            "HFTC": S.dram("s_HFTC", [CTX, 2048], F32),
            "SZT": S.dram("s_SZT", [T, 2048], BF16),
            "SZTC": S.dram("s_SZTC", [CTX, 2048], BF16),
        }
        XR = [S.dram("s_XR0", [T, D], F32), S.dram("s_XR1", [T, D], F32)]
        CR = [S.dram("s_CR0", [CTX, D], F32), S.dram("s_CR1", [CTX, D], F32)]

        k.modscr = {"x": S.dram("s_MODX", [128, 3072], F32), "c": S.dram("s_MODC", [128, 3072], F32)}
        make_consts(k)
        k.epsb = {}
        for e in (NORM_EPS, GN_EPS):
            b = S.sb([128, 1], F32, name=f"eps{len(k.epsb)}")
            k.memset("pool", b[:], e, [b])
            k.epsb[e] = b
        k.oneb = S.sb([128, 1], F32, name="oneb")
        k.memset("pool", k.oneb[:], 1.0, [k.oneb])
        emit_cond(k, W)

        xcur, ccur = W["x"], W["ctx"]
        cnt = {0: 0, 1: 0, 2: 0}
        depth = len(kinds)
        for li, kind in enumerate(kinds):
            jl = cnt[kind]
            cnt[kind] += 1
            with_ctx = li < depth - 1
            mod = emit_mod(k, li, W)
            if DBG.get("stop") == "mod":
                break
            xdst = XR[li % 2]
            cdst = CR[li % 2]
            seqs = [
                dict(name="c", src=ccur, Tn=CTX, M=mod["c"], dst=cdst, want_out=with_ctx,
                     HTs=scr["HTC"], HFs=scr["HFTC"], SZs=scr["SZTC"], YGs=scr["YGC"]),
                dict(name="x", src=xcur, Tn=T, M=mod["x"], dst=xdst, want_out=True,
                     HTs=scr["HT"], HFs=scr["HFT"], SZs=scr["SZT"], YGs=scr["YG"]),
            ]
            if kind == 0:
                lru_layer(k, li, jl, W, seqs, scr)
            elif kind == 1:
                mlstm_layer(k, li, jl, W, seqs, scr)
            else:
                rwkv_layer(k, li, jl, W, seqs, scr)
            if DBG.get("stop") and DBG.get("stop") != "p3x":
                break
            xcur = xdst
            if with_ctx:
                ccur = cdst
        fg = S.sb([128, 1024], F32, name="final_g")
        k.dma(fg[:], W["final_g"].t[:].partition_broadcast(128), [W["final_g"]], [fg])
        if DBG.get("raw"):
            src = scr[DBG["raw"]] if DBG["raw"] in scr else xcur
            with k.scope():
                for t0 in range(0, T, 128):
                    tb = k.rot("rawb", [128, 1024], F32, n=2)
                    k.dma(tb[:], src.t[t0:t0 + 128, 0:1024], [src], [tb])
                    k.dma(OUT.t[t0:t0 + 128, :], tb[:], [tb], [OUT], q="pool")
        else:
            pass_final(k, xcur, T, fg, OUT)
        S.finish([OUT] + k.dump_bufs)
        print(f"[build] insts={S.n_inst} waits={S.n_wait}")
    return nc


_CACHE = {}


def run(inputs, kinds, n_cores=2):
    x = np.ascontiguousarray(inputs["x"], dtype=np.float32)
    B, T, _ = x.shape
    CTX = inputs["ctx"].shape[1]
    wshapes = {n: inputs[n].shape for n in WNAMES}
    key = (T, CTX, tuple(kinds))
    if key not in _CACHE:
        _CACHE[key] = build(T, CTX, kinds, wshapes)
    nc = _CACHE[key]
    in_maps = []
    for b in range(B):
        m = {"x": x[b], "ctx": np.ascontiguousarray(inputs["ctx"][b], dtype=np.float32),
             "c": np.ascontiguousarray(inputs["c"][b], dtype=np.float32),
             "c_ctx": np.ascontiguousarray(inputs["c_ctx"], dtype=np.float32)}
        for n in WNAMES:
            m[n] = np.ascontiguousarray(inputs[n], dtype=np.float32)
        in_maps.append(m)
    res = run_bass_kernel_spmd(nc, in_maps, core_ids=list(range(B)))
    DBG["results"] = res.results
    return np.stack([res.results[b]["out"] for b in range(B)], axis=0)


def kernel(**inputs):
    return run(inputs, [0, 1, 2, 0])
```

```python
import contextlib
import numpy as np
import concourse.bass as bass
import concourse.mybir as mybir
from concourse.bass_utils import run_bass_kernel_spmd

F32 = mybir.dt.float32
BF16 = mybir.dt.bfloat16
AF = mybir.ActivationFunctionType
ALU = mybir.AluOpType
AX = mybir.AxisListType

D = 1024
KD = 8
NORM_EPS = 1e-6
GN_EPS = 64e-5
LRU_W = 1280
LRU_NB = 10
LRU_TC = 512
LRU_NCH = 5

DBG = {}
CE = ("pe", "dve", "act", "pool")
ALLQ = ("pe", "dve", "act", "pool", "sp")


class Buf:
    __slots__ = ("t", "name", "w", "r")

    def __init__(self, t=None, name=""):
        self.t = t
        self.name = name
        self.w = None
        self.r = {}

    def __getitem__(self, idx):
        return self.t[idx]


class V2:
    def __init__(self, buf):
        self.buf = buf
        self.ap = buf.t[:, :, :].rearrange("p j t -> p (j t)")

    def __getitem__(self, idx):
        return self.ap[idx]


class Sch:
    def __init__(self, nc, stack, ndma=12):
        self.nc = nc
        self.stack = stack
        self.eng = {"pe": nc.tensor, "dve": nc.vector, "act": nc.scalar, "pool": nc.gpsimd, "sp": nc.sync}
        self.q = {e: [] for e in ALLQ}
        self.sem = {e: stack.enter_context(nc.semaphore("s_" + e)) for e in CE}
        self.cnt = {e: 0 for e in CE}
        self.known = {e: {} for e in ALLQ}
        self.dpool = {}
        self.dnext = {}
        self.dval = {}
        for qn in ("sp", "act", "pool"):
            self.dpool[qn] = [stack.enter_context(nc.semaphore(f"d_{qn}{i}")) for i in range(ndma)]
            self.dnext[qn] = 0
        self.n_inst = 0
        self.n_wait = 0
        self._uid = 0

    def sb(self, shape, dtype=F32, name=None):
        self._uid += 1
        name = f"{name or 'sb'}_{self._uid}"
        t = self.stack.enter_context(self.nc.sbuf_tensor(name, list(shape), dtype))
        return Buf(t, name)

    def ps(self, shape, dtype=F32, name=None):
        self._uid += 1
        name = name or f"ps{self._uid}"
        t = self.stack.enter_context(self.nc.psum_tensor(name, list(shape), dtype))
        return Buf(t, name)

    def dram(self, name, shape, dtype=F32, kind="Internal"):
        t = self.nc.dram_tensor(name, list(shape), dtype, kind=kind)
        return Buf(t, name)

    def _wait(self, qn, tok):
        kind, key, val = tok
        kn = self.known[qn]
        if kn.get(key, 0) >= val:
            return
        kn[key] = val
        sem = self.sem[key] if kind == "e" else key
        eng = self.eng[qn]
        eng.wait_ge(sem, val)
        self.n_wait += 1

    def _deps(self, qn, reads, writes):
        toks = []
        for b in reads:
            if b.w is not None:
                toks.append(b.w)
        for b in writes:
            if b.w is not None:
                t = b.w
                if not (t[0] == "e" and t[1] == qn == "pe"):
                    toks.append(t)
            for (k0, k1), v in b.r.items():
                if not (k0 == "e" and k1 == qn == "pe"):
                    toks.append((k0, k1, v))
        for t in toks:
            self._wait(qn, t)

    def _mark(self, tok, reads, writes):
        for b in reads:
            b.r[(tok[0], tok[1])] = tok[2]
        for b in writes:
            b.w = tok
            b.r = {}
        self.n_inst += 1

    def op(self, qn, fn, reads=(), writes=()):
        self._deps(qn, reads, writes)
        self.cnt[qn] += 1
        sem = self.sem[qn]
        fn().then_inc(sem, 1)
        tok = ("e", qn, self.cnt[qn])
        self._mark(tok, reads, writes)
        return tok

    def dma(self, qn, fn, reads=(), writes=()):
        self._deps(qn, reads, writes)
        pool = self.dpool[qn]
        i = self.dnext[qn] % len(pool)
        self.dnext[qn] += 1
        sem = pool[i]
        prev = self.dval.get(sem, 0)
        if prev:
            self._wait(qn, ("d", sem, prev))
        val = prev + 16
        self.dval[sem] = val
        eng = self.eng[qn]
        fn(eng).then_inc(sem, 16)
        tok = ("d", sem, val)
        self._mark(tok, reads, writes)
        return tok

    def finish(self, out_bufs):
        for b in out_bufs:
            if b.w is not None:
                self._wait("sp", b.w)

    def barrier(self):
        toks = [("e", e, self.cnt[e]) for e in CE if self.cnt[e]]
        toks += [("d", sem, v) for sem, v in self.dval.items() if v]
        for qn in ALLQ:
            for t in toks:
                if not (t[0] == "e" and t[1] == qn):
                    self._wait(qn, t)


class K:
    def __init__(self, nc, S):
        self.nc = nc
        self.S = S
        self.pools = {}
        self.psb = [S.ps([128, 512], F32, name=f"psb{i}") for i in range(8)]
        self.psi = 0
        self.ldq = 0
        self.dumped = set()
        self.dump_bufs = []

    def rot(self, key, shape, dtype=F32, n=2, init=None):
        p = self.pools.get(key)
        if p is None:
            bufs = [self.S.sb(shape, dtype, name=f"{key}_{i}") for i in range(n)]
            if init is not None:
                for b in bufs:
                    init(b)
            p = self.pools[key] = [bufs, 0]
        b = p[0][p[1] % len(p[0])]
        p[1] += 1
        return b

    def dump(self, name, ap, shape, dtype, reads):
        if name not in DBG.get("dump", ()) or name in self.dumped:
            return
        self.dumped.add(name)
        t = self.nc.dram_tensor("dbg_" + name, list(shape), dtype, kind="ExternalOutput")
        b = Buf(t, name)
        self.dma(t[:], ap, reads, [b], q="sp")
        self.dump_bufs.append(b)

    def psum(self):
        b = self.psb[self.psi % 8]
        self.psi += 1
        return b

    @contextlib.contextmanager
    def scope(self):
        S = self.S
        ls = contextlib.ExitStack()
        old_stack, old_pools = S.stack, self.pools
        S.stack = ls
        self.pools = dict(old_pools)
        try:
            yield
        finally:
            S.barrier()
            S.stack = old_stack
            self.pools = old_pools
            ls.close()

    def dma(self, out_ap, in_ap, reads, writes, q="sp", slow=False):
        if slow:
            self.S.dma(q, lambda e: e.dma_start(out=out_ap, in_=in_ap, allow_slow_non_contiguous=True),
                       reads=reads, writes=writes)
        else:
            self.S.dma(q, lambda e: e.dma_start(out=out_ap, in_=in_ap), reads=reads, writes=writes)

    def mm(self, ps_ap, lhsT, rhs, start, stop, reads, writes):
        nc = self.nc
        self.S.op("pe", lambda: nc.tensor.matmul(ps_ap, lhsT=lhsT, rhs=rhs, start=start, stop=stop),
                  reads=reads, writes=writes)

    def tr(self, ps_ap, in_ap, ident_ap, reads, writes):
        nc = self.nc
        self.S.op("pe", lambda: nc.tensor.transpose(ps_ap, in_ap, ident_ap), reads=reads, writes=writes)

    def act(self, out, in_, func, reads, writes, bias=None, scale=None, accum_out=None):
        nc = self.nc
        kw = {}
        if bias is not None:
            kw["bias"] = bias
        if scale is not None:
            kw["scale"] = scale
        if accum_out is not None:
            kw["accum_out"] = accum_out
        self.S.op("act", lambda: nc.scalar.activation(out=out, in_=in_, func=func, **kw), reads=reads, writes=writes)

    def ts(self, eng, out, in0, s1, s2, op0, op1, reads, writes):
        e = self.nc.vector if eng == "dve" else self.nc.gpsimd
        if op1 is None:
            self.S.op(eng, lambda: e.tensor_scalar(out=out, in0=in0, scalar1=s1, scalar2=None, op0=op0),
                      reads=reads, writes=writes)
        else:
            self.S.op(eng, lambda: e.tensor_scalar(out=out, in0=in0, scalar1=s1, scalar2=s2, op0=op0, op1=op1),
                      reads=reads, writes=writes)

    def tt(self, eng, out, in0, in1, op, reads, writes):
        e = self.nc.vector if eng == "dve" else self.nc.gpsimd
        self.S.op(eng, lambda: e.tensor_tensor(out=out, in0=in0, in1=in1, op=op), reads=reads, writes=writes)

    def stt(self, out, in0, scalar, in1, op0, op1, reads, writes):
        nc = self.nc
        self.S.op("dve", lambda: nc.vector.scalar_tensor_tensor(out=out, in0=in0, scalar=scalar, in1=in1,
                                                                op0=op0, op1=op1), reads=reads, writes=writes)

    def copy(self, eng, out, in_, reads, writes):
        nc = self.nc
        if eng == "act":
            self.S.op("act", lambda: nc.scalar.copy(out=out, in_=in_), reads=reads, writes=writes)
        elif eng == "dve":
            self.S.op("dve", lambda: nc.vector.tensor_copy(out=out, in_=in_), reads=reads, writes=writes)
        else:
            self.S.op("pool", lambda: nc.gpsimd.tensor_copy(out=out, in_=in_), reads=reads, writes=writes)

    def memset(self, eng, ap, val, writes):
        e = self.nc.vector if eng == "dve" else self.nc.gpsimd
        self.S.op(eng, lambda: e.memset(ap, val), writes=writes)

    def recip(self, out, in_, reads, writes):
        nc = self.nc
        self.S.op("dve", lambda: nc.vector.reciprocal(out=out, in_=in_), reads=reads, writes=writes)


def make_consts(k):
    nc, S = k.nc, k.S
    ones = S.sb([128, 512], F32, name="c_ones")
    k.memset("pool", ones[:], 1.0, [ones])
    k.ones = ones

    def sel(name, cm, step, op, base=0):
        b = S.sb([128, 128], F32, name=name)
        S.op("pool", lambda: nc.gpsimd.affine_select(out=b[:], in_=ones[:, 0:128], pattern=[[step, 128]],
                                                     compare_op=op, fill=0.0, base=base, channel_multiplier=cm),
             reads=[ones], writes=[b])
        return b
    k.ident = sel("c_ident", -1, 1, ALU.is_equal)
    k.m_le = sel("c_mle", -1, 1, ALU.is_ge)
    k.m_ge = sel("c_mge", 1, -1, ALU.is_ge)
    k.m_lt = sel("c_mlt", -1, 1, ALU.is_gt)
    k.m_gt = sel("c_mgt", 1, -1, ALU.is_gt)
    k.identb = S.sb([128, 128], BF16, name="c_identb")
    k.copy("dve", k.identb[:], k.ident[:], [k.ident], [k.identb])
    k.onesb = S.sb([128, 128], BF16, name="c_onesb")
    k.copy("dve", k.onesb[:], ones[:, 0:128], [ones], [k.onesb])


def emit_mod(k, li, W):
    nc, S = k.nc, k.S
    with k.scope():
        Ms = {"x": k.rot("modM_x", [128, 3072], F32, n=1), "c": k.rot("modM_c", [128, 3072], F32, n=1)}
        mb = k.rot("mod_b", [128, 3072], F32, n=1)
        k.dma(mb[:], W["mod_b"].t[li].partition_broadcast(128), [W["mod_b"]], [mb])
        gb = k.rot("mod_g", [128, 1024], F32, n=1)
        k.dma(gb[:], W["norm_g"].t[li].partition_broadcast(128), [W["norm_g"]], [gb])
        for half in range(2):
            pss = {c: [k.psum() for _ in range(3)] for c in ("x", "c")}
            for j in range(KD):
                wt = k.rot("modw", [128, 1536], F32, n=2)
                k.dma(wt[:], W["mod_w"].t[li, j * 128:(j + 1) * 128, half * 1536:(half + 1) * 1536], [W["mod_w"]], [wt])
                for c in ("x", "c"):
                    R = k.R[c]
                    for nb in range(3):
                        k.mm(pss[c][nb][:], R[:, j, :], wt[:, nb * 512:(nb + 1) * 512], j == 0, j == KD - 1,
                             [R, wt], [pss[c][nb]])
            for c in ("x", "c"):
                for nb in range(3):
                    o = half * 1536 + nb * 512
                    k.tt("dve", Ms[c][:, o:o + 512], pss[c][nb][:], mb[:, o:o + 512], ALU.add,
                         [pss[c][nb], mb], [Ms[c]])
        for c in ("x", "c"):
            M = Ms[c]
            k.stt(M[:, 1024:2048], M[:, 1024:2048], 1.0, gb[:], ALU.add, ALU.mult, [M, gb], [M])
            k.dma(k.modscr[c].t[:, :], M[:], [M], [k.modscr[c]], q="pool")
    return k.modscr


def emit_cond(k, W):
    nc, S = k.nc, k.S
    k.R = {}
    for c, nm in (("x", "c"), ("c", "c_ctx")):
        cv = S.sb([128, 8], F32, name="cv_" + c)
        k.dma(cv[:], W[nm].t.rearrange("(j p) -> p j", p=128), [W[nm]], [cv], slow=True)
        sc = S.sb([128, 8], F32, name="sc_" + c)
        k.act(sc[:], cv[:], AF.Silu, [cv], [sc])
        R = S.sb([128, 8, 128], F32, name="R_" + c)
        for j in range(KD):
            k.ts("dve", R[:, j, :], k.ones[:, 0:128], sc[:, j:j + 1], None, ALU.mult, None, [k.ones, sc], [R])
        k.R[c] = R


def rstd_of(k, ss, n, eps, width):
    rt = k.rot(f"rstd_a{width}", [128, width], F32, n=3)
    k.act(rt[:], ss[:], AF.Sqrt, [ss], [rt], bias=k.epsb[eps][:, 0:1], scale=1.0 / n)
    rs = k.rot(f"rstd_b{width}", [128, width], F32, n=3)
    k.recip(rs[:], rt[:], [rt], [rs])
    return rs


def pass_norm(k, src, Tn, M, HT):
    with k.scope():
        Ml = k.rot("M_local", [128, 3072], F32, n=1)
        k.dma(Ml[:], M.t[:, :], [M], [Ml])
        _pass_norm(k, src, Tn, Ml, HT)


def _pass_norm(k, src, Tn, M, HT):
    nc, S = k.nc, k.S
    TT = min(512, Tn)
    ns = TT // 128
    for t0 in range(0, Tn, TT):
        xs = k.rot("pn_x", [128, 4, 1024], F32, n=2)
        k.dma(xs[:, 0:ns, :], src.t[t0:t0 + TT, :].rearrange("(a p) d -> p a d", p=128), [src], [xs])
        hT = k.rot("pn_hT", [128, 8, 512], BF16, n=2)
        for a in range(ns):
            junk = k.rot("pn_junk", [128, 1024], F32, n=2)
            ss = k.rot("pn_ss", [128, 1], F32, n=3)
            k.act(junk[:], xs[:, a, :], AF.Square, [xs], [junk, ss], accum_out=ss[:])
            rs = rstd_of(k, ss, float(D), NORM_EPS, 1)
            h = k.rot("pn_h", [128, 1024], F32, n=2)
            k.stt(h[:], xs[:, a, :], rs[:, 0:1], M[:, 1024:2048], ALU.mult, ALU.mult, [xs, rs, M], [h])
            k.tt("pool", h[:], h[:], M[:, 0:1024], ALU.add, [h, M], [h])
            for hb in range(2):
                ps = k.psum()
                for jj in range(4):
                    j = hb * 4 + jj
                    k.tr(ps[:, jj * 128:(jj + 1) * 128], h[:, j * 128:(j + 1) * 128], k.ident[:], [h, k.ident], [ps])
                dst = hT[:, hb * 4:(hb + 1) * 4, a * 128:(a + 1) * 128]
                srcp = ps[:, :].rearrange("p (j t) -> p j t", j=4)
                if hb == 0:
                    k.copy("act", dst, srcp, [ps], [hT])
                else:
                    k.copy("dve", dst, srcp, [ps], [hT])
        k.dma(HT.t[:, t0:t0 + TT].rearrange("(j p) t -> p j t", p=128), hT[:, :, 0:TT], [hT], [HT], q="pool")


def load_w_bf16(k, name, src_ap_fn, src_buf, nK, N, stage_cols=None):
    S = k.S
    wb = S.sb([128, nK, N], BF16, name=name)
    sc = stage_cols or N
    i = 0
    with k.scope():
        for j in range(nK):
            for c0 in range(0, N, sc):
                st = k.rot(f"wstage{sc}", [128, sc], F32, n=2)
                k.dma(st[:], src_ap_fn(j, c0, sc), [src_buf], [st])
                eng = ("dve", "pool", "act")[i % 3]
                k.copy(eng, wb[:, j, c0:c0 + sc], st[:], [st], [wb])
                i += 1
    return wb


def pass_out(k, YG, nK, Wo, Tn, xsrc, G, xdst):
    with k.scope():
        Gl = k.rot("M_local", [128, 3072], F32, n=1)
        k.dma(Gl[:], G.t[:, :], [G], [Gl])
        _pass_out(k, YG, nK, Wo, Tn, xsrc, Gl, xdst)


def _pass_out(k, YG, nK, Wo, Tn, xsrc, G, xdst):
    TT = min(512, Tn)
    ns = TT // 128
    for t0 in range(0, Tn, TT):
        yg = k.rot(f"po_yg{nK}", [128, nK, 512], BF16, n=2)
        k.dma(yg[:, :, 0:TT], YG.t[0:nK * 128, t0:t0 + TT].rearrange("(j p) t -> p j t", p=128), [YG], [yg])
        xs = k.rot("po_x", [128, 4, 1024], F32, n=2)
        k.dma(xs[:, 0:ns, :], xsrc.t[t0:t0 + TT, :].rearrange("(a p) d -> p a d", p=128), [xsrc], [xs])
        xn = k.rot("po_xn", [128, 4, 1024], F32, n=2)
        for a in range(ns):
            for nb in range(2):
                ps = k.psum()
                for j in range(nK):
                    k.mm(ps[:], yg[:, j, a * 128:(a + 1) * 128], Wo[:, j, nb * 512:(nb + 1) * 512],
                         j == 0, j == nK - 1, [yg, Wo], [ps])
                tmp = k.rot("po_tmp", [128, 512], F32, n=3)
                k.tt("dve", tmp[:], ps[:], G[:, 2048 + nb * 512:2048 + (nb + 1) * 512], ALU.mult, [ps, G], [tmp])
                k.tt("pool", xn[:, a, nb * 512:(nb + 1) * 512], tmp[:], xs[:, a, nb * 512:(nb + 1) * 512], ALU.add,
                     [tmp, xs], [xn])
        k.dma(xdst.t[t0:t0 + TT, :].rearrange("(a p) d -> p a d", p=128), xn[:, 0:ns, :], [xn], [xdst], q="pool")


def pass_final(k, src, Tn, fg, out):
    with k.scope():
        _pass_final(k, src, Tn, fg, out)


def _pass_final(k, src, Tn, fg, out):
    TT = min(512, Tn)
    ns = TT // 128
    for t0 in range(0, Tn, TT):
        xs = k.rot("pn_x", [128, 4, 1024], F32, n=2)
        k.dma(xs[:, 0:ns, :], src.t[t0:t0 + TT, :].rearrange("(a p) d -> p a d", p=128), [src], [xs])
        xn = k.rot("po_xn", [128, 4, 1024], F32, n=2)
        for a in range(ns):
            junk = k.rot("pn_junk", [128, 1024], F32, n=2)
            ss = k.rot("pn_ss", [128, 1], F32, n=3)
            k.act(junk[:], xs[:, a, :], AF.Square, [xs], [junk, ss], accum_out=ss[:])
            rs = rstd_of(k, ss, float(D), NORM_EPS, 1)
            k.stt(xn[:, a, :], xs[:, a, :], rs[:, 0:1], fg[:], ALU.mult, ALU.mult, [xs, rs, fg], [xn])
        k.dma(out.t[t0:t0 + TT, :].rearrange("(a p) d -> p a d", p=128), xn[:, 0:ns, :], [xn], [out], q="pool")


def lru_layer(k, li, jl, W, seqs, scr):
    nc, S = k.nc, k.S
    with k.scope():
        w_in = load_w_bf16(k, f"lru_win{li}", lambda j, c0, sc: W["lru_w_in"].t[jl, j * 128:(j + 1) * 128, c0:c0 + sc],
                           W["lru_w_in"], KD, 2 * LRU_W, stage_cols=1280)
        w_out = load_w_bf16(k, f"lru_wout{li}", lambda j, c0, sc: W["lru_w_out"].t[jl, j * 128:(j + 1) * 128, c0:c0 + sc],
                            W["lru_w_out"], LRU_NB, D)
        gwv = W["lru_gate_w"].t[jl].rearrange("d g n j k -> (d g n) j k")
        gw = load_w_bf16(k, f"lru_gw{li}", lambda j, c0, sc: gwv[j], W["lru_gate_w"], 40, 128)
        cw = S.sb([128, 4, LRU_NB], F32, name=f"lru_cw{li}")
        k.dma(cw[:], W["lru_conv_w"].t[jl].rearrange("j (n p) -> p j n", p=128), [W["lru_conv_w"]], [cw], slow=True)
        cb = S.sb([128, LRU_NB], F32, name=f"lru_cb{li}")
        k.dma(cb[:], W["lru_conv_b"].t[jl].rearrange("(n p) -> p n", p=128), [W["lru_conv_b"]], [cb], slow=True)
        gbias = S.sb([128, 4, LRU_NB], F32, name=f"lru_gb{li}")
        k.dma(gbias[:], W["lru_gate_b"].t[jl].rearrange("d g (n p) -> p (d g) n", p=128), [W["lru_gate_b"]], [gbias], slow=True)
        lam = S.sb([128, 2, LRU_NB], F32, name=f"lru_lam{li}")
        k.dma(lam[:], W["lru_lam"].t[jl].rearrange("d (n p) -> p d n", p=128), [W["lru_lam"]], [lam], slow=True)
        e1 = S.sb([128, 2, LRU_NB], F32, name=f"lru_e1{li}")
        k.act(e1[:], lam[:], AF.Exp, [lam], [e1], scale=-1.0)
        l1 = S.sb([128, 2, LRU_NB], F32, name=f"lru_l1{li}")
        k.act(l1[:], e1[:], AF.Ln, [e1, k.oneb], [l1], bias=k.oneb[:, 0:1])
        coef = S.sb([128, 2, LRU_NB], F32, name=f"lru_coef{li}")
        k.ts("dve", coef[:], l1[:], -8.0, None, ALU.mult, None, [l1], [coef])
        coef2 = S.sb([128, 2, LRU_NB], F32, name=f"lru_coef2{li}")
        k.ts("dve", coef2[:], l1[:], -16.0, None, ALU.mult, None, [l1], [coef2])
        st = [S.sb([128, 2], F32, name=f"lru_st{li}_{n}") for n in range(LRU_NB)]
        for n in range(LRU_NB):
            k.memset("dve", st[n][:], 0.0, [st[n]])
        HT, UT, SZ, UC, HF, YG = (scr[n] for n in ("HT", "UT", "SZ", "UC", "HF", "YG"))

        for sq in seqs:
            Tn = sq["Tn"]
            if DBG.get("stop") == "w":
                continue
            pass_norm(k, sq["src"], Tn, sq["M"], HT)
            if DBG.get("stop") == "norm":
                continue
            with k.scope():
                TT = min(512, Tn)
                for t0 in range(0, Tn, TT):
                    hT = k.rot("l1_hT", [128, 8, 512], BF16, n=2)
                    k.dma(hT[:, :, 0:TT], HT.t[:, t0:t0 + TT].rearrange("(j p) t -> p j t", p=128), [HT], [hT])
                    ub = k.rot("l1_u", [128, LRU_NB, 512], F32, n=2)
                    zb = k.rot("l1_z", [128, LRU_NB, 512], BF16, n=2)
                    for oc in range(2 * LRU_NB):
                        ps = k.psum()
                        for j in range(KD):
                            k.mm(ps[:, 0:TT], w_in[:, j, oc * 128:(oc + 1) * 128], hT[:, j, 0:TT], j == 0, j == KD - 1,
                                 [w_in, hT], [ps])
                        if oc < LRU_NB:
                            k.copy("dve", ub[:, oc, 0:TT], ps[:, 0:TT], [ps], [ub])
                        else:
                            k.act(zb[:, oc - LRU_NB, 0:TT], ps[:, 0:TT], AF.Silu, [ps], [zb])
                    k.dma(UT.t[:, t0:t0 + TT].rearrange("(n p) t -> p n t", p=128), ub[:, :, 0:TT], [ub], [UT], q="pool")
                    k.dma(SZ.t[:, t0:t0 + TT].rearrange("(n p) t -> p n t", p=128), zb[:, :, 0:TT], [zb], [SZ], q="pool")
            if DBG.get("stop") == "p1":
                continue
            with k.scope():
                TC = min(LRU_TC, Tn)
                nsub = Tn // TC

                def chain(cid, n):
                    rows = slice(n * 128, (n + 1) * 128)
                    for d in range(2):
                        order = range(nsub) if d == 0 else range(nsub - 1, -1, -1)
                        for si in order:
                            yield from _lru_sub(k, sq, n, d, si * TC, TC, Tn, rows, cw, cb, gw, gbias, coef, coef2, st, scr, cid)

                for n0 in range(0, LRU_NB, LRU_NCH):
                    gens = [chain(ci, n0 + ci) for ci in range(min(LRU_NCH, LRU_NB - n0))]
                    live = list(range(len(gens)))
                    step = 0
                    while live:
                        for ci in list(live):
                            if step < ci:
                                continue
                            try:
                                next(gens[ci])
                            except StopIteration:
                                live.remove(ci)
                        step += 1
            if DBG.get("stop") == "p2":
                continue
            if sq["want_out"]:
                pass_out(k, YG, LRU_NB, w_out, Tn, sq["src"], sq["M"], sq["dst"])


def _lru_sub(k, sq, n, d, t0, TC, Tn, rows, cw, cb, gw, gbias, coef, coef2, st, scr, cid):
    nc, S = k.nc, k.S
    HT, UT, SZ, UC, HF, YG = (scr[x] for x in ("HT", "UT", "SZ", "UC", "HF", "YG"))
    W_ = LRU_TC
    ucf = k.rot(f"l2_uc_{cid}", [128, W_], F32, n=1)
    if d == 0:
        uh = k.rot(f"l2_uh_{cid}", [128, W_ + 3], F32, n=1)
        lo = max(t0 - 2, 0)
        hi = min(t0 + TC + 1, Tn)
        if t0 == 0:
            k.memset("pool", uh[:, 0:2], 0.0, [uh])
        if t0 + TC == Tn:
            k.memset("pool", uh[:, TC + 2:TC + 3], 0.0, [uh])
        k.dma(uh[:, lo - (t0 - 2):hi - (t0 - 2)], UT.t[rows, lo:hi], [UT], [uh])
        k.ts("dve", ucf[:, 0:TC], uh[:, 2:2 + TC], cw[:, 2, n:n + 1], cb[:, n:n + 1], ALU.mult, ALU.add,
             [uh, cw, cb], [ucf])
        for tap in (0, 1, 3):
            k.stt(ucf[:, 0:TC], uh[:, tap:tap + TC], cw[:, tap, n:n + 1], ucf[:, 0:TC], ALU.mult, ALU.add,
                  [uh, cw, ucf], [ucf])
        k.dma(UC.t[rows, t0:t0 + TC], ucf[:, 0:TC], [ucf], [UC], q="pool")
    else:
        k.dma(ucf[:, 0:TC], UC.t[rows, t0:t0 + TC], [UC], [ucf])
    yield
    ucb = k.rot(f"l2_ucb_{cid}", [128, W_], BF16, n=1)
    k.copy("pool", ucb[:, 0:TC], ucf[:, 0:TC], [ucf], [ucb])
    r = k.rot(f"l2_r_{cid}", [128, W_], F32, n=1)
    ig = k.rot(f"l2_i_{cid}", [128, W_], F32, n=1)
    for c0 in range(0, TC, 512):
        cl = min(512, TC - c0)
        for g, dstb in ((0, r), (1, ig)):
            ps = k.psum()
            gi = (d * 2 + g) * LRU_NB + n
            k.mm(ps[:, 0:cl], gw[:, gi, :], ucb[:, c0:c0 + cl], True, True, [gw, ucb], [ps])
            k.act(dstb[:, c0:c0 + cl], ps[:, 0:cl], AF.Sigmoid, [ps, gbias], [dstb],
                  bias=gbias[:, d * 2 + g, n:n + 1])
    yield
    av = k.rot(f"l2_a_{cid}", [128, W_], F32, n=1)
    k.act(av[:, 0:TC], r[:, 0:TC], AF.Exp, [r, coef], [av], scale=coef[:, d, n:n + 1])
    a2 = k.rot(f"l2_a2_{cid}", [128, W_], F32, n=1)
    k.tt("pool", a2[:, 0:TC], av[:, 0:TC], av[:, 0:TC], ALU.mult, [av], [a2])
    k.act(a2[:, 0:TC], a2[:, 0:TC], AF.Sqrt, [a2, k.oneb], [a2], bias=k.oneb[:, 0:1], scale=-1.0)
    yield
    k.tt("pool", ig[:, 0:TC], ig[:, 0:TC], a2[:, 0:TC], ALU.mult, [ig, a2], [ig])
    k.tt("dve", ig[:, 0:TC], ig[:, 0:TC], ucf[:, 0:TC], ALU.mult, [ig, ucf], [ig])
    yield
    hv = k.rot(f"l2_h_{cid}", [128, W_], F32, n=1)
    if d == 0:
        S.op("dve", lambda: nc.vector.tensor_tensor_scan(
            out=hv[:, 0:TC], data0=av[:, 0:TC], data1=ig[:, 0:TC], initial=st[n][:, 0:1],
            op0=ALU.mult, op1=ALU.add), reads=[av, ig, st[n]], writes=[hv])
        k.copy("dve", st[n][:, 0:1], hv[:, TC - 1:TC], [hv], [st[n]])
        k.dma(HF.t[rows, t0:t0 + TC], hv[:, 0:TC], [hv], [HF], q="pool")
    else:
        S.op("dve", lambda: nc.vector.tensor_tensor_scan(
            out=hv[:, TC - 1::-1], data0=av[:, TC - 1::-1], data1=ig[:, TC - 1::-1], initial=st[n][:, 1:2],
            op0=ALU.mult, op1=ALU.add), reads=[av, ig, st[n]], writes=[hv])
        k.copy("dve", st[n][:, 1:2], hv[:, 0:1], [hv], [st[n]])
        yield
        if sq["want_out"]:
            hf = k.rot(f"l2_hf_{cid}", [128, W_], F32, n=1)
            k.dma(hf[:, 0:TC], HF.t[rows, t0:t0 + TC], [HF], [hf])
            szt = k.rot(f"l2_sz_{cid}", [128, W_], BF16, n=1)
            k.dma(szt[:, 0:TC], SZ.t[rows, t0:t0 + TC], [SZ], [szt])
            k.tt("pool", hv[:, 0:TC], hv[:, 0:TC], hf[:, 0:TC], ALU.add, [hv, hf], [hv])
            ygt = k.rot(f"l2_yg_{cid}", [128, W_], BF16, n=1)
            k.tt("dve", ygt[:, 0:TC], hv[:, 0:TC], szt[:, 0:TC], ALU.mult, [hv, szt], [ygt])
            k.dma(YG.t[rows, t0:t0 + TC], ygt[:, 0:TC], [ygt], [YG], q="pool")
    yield


ML_H = 8
ML_DK = 128
ML_DV = 256
ML_W = 2048
ML_NIN = 6176


def mlstm_layer(k, li, jl, W, seqs, scr):
    nc, S = k.nc, k.S
    HT, HFT, SZT = scr["HT"], scr["HFT"], scr["SZT"]
    win = W["mlstm_w_in"]
    for sq in seqs:
        pass_norm(k, sq["src"], sq["Tn"], sq["M"], sq["HTs"])
    with k.scope():
        wz = load_w_bf16(k, f"ml_wz{li}", lambda j, c0, sc: win.t[jl, j * 128:(j + 1) * 128, 4096 + c0:4096 + c0 + sc],
                         win, KD, ML_W, stage_cols=1024)
        ng = S.sb([128, ML_W], F32, name="ml_ng")
        k.dma(ng[:], W["mlstm_norm_g"].t[jl].partition_broadcast(128), [W["mlstm_norm_g"]], [ng])
        for sq in seqs:
            if not sq["want_out"]:
                continue
            Tn = sq["Tn"]
            for t0 in range(0, Tn, 128):
                hT = k.rot("mz_hT", [128, 8, 128], BF16, n=2)
                k.dma(hT[:], sq["HTs"].t[:, t0:t0 + 128].rearrange("(j p) t -> p j t", p=128), [sq["HTs"]], [hT])
                szg = k.rot("mz_szg", [128, ML_W], BF16, n=2)
                for nb in range(4):
                    ps = k.psum()
                    for j in range(KD):
                        k.mm(ps[:], hT[:, j, :], wz[:, j, nb * 512:(nb + 1) * 512], j == 0, j == KD - 1, [hT, wz], [ps])
                    tmp = k.rot("mz_tmp", [128, 512], F32, n=2)
                    k.act(tmp[:], ps[:], AF.Silu, [ps], [tmp])
                    k.tt("dve", szg[:, nb * 512:(nb + 1) * 512], tmp[:], ng[:, nb * 512:(nb + 1) * 512], ALU.mult,
                         [tmp, ng], [szg])
                k.dma(sq["SZs"].t[t0:t0 + 128, :], szg[:], [szg], [sq["SZs"]], q="pool")
    with k.scope():
        def wsrc(j, c0, sc):
            if c0 < 4096:
                return win.t[jl, j * 128:(j + 1) * 128, c0:c0 + sc]
            return win.t[jl, j * 128:(j + 1) * 128, 6144:6176]
        wq = S.sb([128, KD, 4128], BF16, name=f"ml_wq{li}")
        with k.scope():
            ii = 0
            for j in range(KD):
                for c0 in range(0, 4096, 1024):
                    stg = k.rot("mlstage", [128, 1024], F32, n=2)
                    k.dma(stg[:], win.t[jl, j * 128:(j + 1) * 128, c0:c0 + 1024], [win], [stg])
                    k.copy(("dve", "pool", "act")[ii % 3], wq[:, j, c0:c0 + 1024], stg[:], [stg], [wq])
                    ii += 1
                stg = k.rot("mlstage", [128, 1024], F32, n=2)
                k.dma(stg[:, 0:32], win.t[jl, j * 128:(j + 1) * 128, 6144:6176], [win], [stg])
                k.copy("dve", wq[:, j, 4096:4128], stg[:, 0:32], [stg], [wq])
        gbb = S.sb([128, 32], F32, name="ml_gbb")
        k.dma(gbb[:], W["mlstm_gate_b"].t[jl].rearrange("a b c -> (a b c)").partition_broadcast(128),
              [W["mlstm_gate_b"]], [gbb])
        Cf = [[S.sb([128, 257], F32, name=f"ml_Cf{d}_{h}") for h in range(ML_H)] for d in range(2)]
        Cb = [[S.sb([128, 257], BF16, name=f"ml_Cb{d}_{h}") for h in range(ML_H)] for d in range(2)]
        for d in range(2):
            for h in range(ML_H):
                k.memset("pool", Cf[d][h][:], 0.0, [Cf[d][h]])
                k.memset("pool", Cb[d][h][:], 0.0, [Cb[d][h]])
        for sq in seqs:
            for d in range(2):
                _mlstm_sweep(k, sq, d, wq, gbb, Cf[d], Cb[d], scr)
    with k.scope():
        w_out = load_w_bf16(k, f"ml_wout{li}", lambda j, c0, sc: W["mlstm_w_out"].t[jl, j * 128:(j + 1) * 128, c0:c0 + sc],
                            W["mlstm_w_out"], 16, D)
        for sq in seqs:
            if sq["want_out"]:
                pass_out(k, sq["YGs"], 16, w_out, sq["Tn"], sq["src"], sq["M"], sq["dst"])


def _ml_state(k, Kh, Vp, Cf, Cb, bg):
    for h in range(ML_H):
        psC = k.psum()
        k.mm(psC[:, 0:257], Kh[:, h, :], Vp[:, h, :], True, True, [Kh, Vp], [psC])
        k.stt(Cf[h][:], Cf[h][:], bg[:, 8 + h:9 + h], psC[:, 0:257], ALU.mult, ALU.add, [Cf[h], bg, psC], [Cf[h]])
        k.copy("act" if h % 2 == 0 else "pool", Cb[h][:], Cf[h][:], [Cf[h]], [Cb[h]])


def _mlstm_sweep(k, sq, d, wq, gbb, Cf, Cb, scr):
    nc, S = k.nc, k.S
    Tn = sq["Tn"]
    nch = Tn // 128
    HTs, HFs, SZs, YGs = sq["HTs"], sq["HFs"], sq["SZs"], sq["YGs"]
    tri = k.m_le if d == 0 else k.m_ge
    order = range(nch) if d == 0 else range(nch - 1, -1, -1)
    fin = (d == 1) and sq["want_out"]
    dkscale = ML_DK ** -0.5

    def ones_col(b):
        k.memset("pool", b[:, :, 256:257], 1.0, [b])

    for c in order:
        t0 = c * 128
        hT = k.rot("ms_hT", [128, 8, 128], BF16, n=2)
        k.dma(hT[:], HTs.t[:, t0:t0 + 128].rearrange("(j p) t -> p j t", p=128), [HTs], [hT])
        psg = k.psum()
        for j in range(KD):
            k.mm(psg[:, 0:32], hT[:, j, :], wq[:, j, 4096:4128], j == 0, j == KD - 1, [hT, wq], [psg])
        g = k.rot("ms_g", [128, 32], F32, n=2)
        k.tt("dve", g[:], psg[:, 0:32], gbb[:], ALU.add, [psg, gbb], [g])
        gi = g[:, d * 16:d * 16 + 8]
        gf = g[:, d * 16 + 8:d * 16 + 16]
        ef = k.rot("ms_ef", [128, 8], F32, n=2)
        k.act(ef[:], gf, AF.Exp, [g], [ef], scale=-1.0)
        lfn = k.rot("ms_lfn", [128, 8], F32, n=2)
        k.act(lfn[:], ef[:], AF.Ln, [ef, k.oneb], [lfn], bias=k.oneb[:, 0:1])
        qT = k.rot("ms_qT", [128, 8, 128], BF16, n=2)
        kT = k.rot("ms_kT", [128, 8, 128], BF16, n=2)
        for (dst, col0, scl) in ((qT, 0, None), (kT, 1024, dkscale)):
            for hb in range(2):
                ps = k.psum()
                for hh in range(4):
                    h = hb * 4 + hh
                    for j in range(KD):
                        k.mm(ps[:, hh * 128:(hh + 1) * 128], wq[:, j, col0 + h * 128:col0 + (h + 1) * 128], hT[:, j, :],
                             j == 0, j == KD - 1, [wq, hT], [ps])
                dv = dst[:, hb * 4:(hb + 1) * 4, :]
                sv = ps[:, :].rearrange("p (h t) -> p h t", h=4)
                if scl is None:
                    k.copy("act", dv, sv, [ps], [dst])
                else:
                    k.ts("dve", dv, sv, scl, None, ALU.mult, None, [ps], [dst])
        psF = k.psum()
        k.mm(psF[:, 0:8], tri[:], lfn[:], True, True, [tri, lfn], [psF])
        k.mm(psF[:, 8:16], k.ones[:, 0:128], lfn[:], True, True, [k.ones, lfn], [psF])
        Fs = k.rot("ms_Fs", [128, 16], F32, n=2)
        k.copy("dve", Fs[:], psF[:, 0:16], [psF], [Fs])
        bg = k.rot("ms_bg", [128, 16], F32, n=2)
        k.act(bg[:], Fs[:], AF.Exp, [Fs], [bg], scale=-1.0)
        al = k.rot("ms_al", [128, 8], F32, n=2)
        k.tt("dve", al[:], Fs[:, 0:8], gi, ALU.add, [Fs, g], [al])
        k.act(al[:], al[:], AF.Exp, [al], [al])
        om = k.rot("ms_om", [128, 8], F32, n=2)
        k.stt(om[:], al[:], dkscale, bg[:, 8:16], ALU.mult, ALU.mult, [al, bg], [om])
        Kh = k.rot("ms_Kh", [128, 8, 128], BF16, n=2)
        for hb in range(2):
            ps = k.psum()
            for j in range(KD):
                k.mm(ps[:], hT[:, j, :], wq[:, j, 1024 + hb * 512:1024 + (hb + 1) * 512], j == 0, j == KD - 1, [hT, wq], [ps])
            for hh in range(4):
                h = hb * 4 + hh
                if hb == 0:
                    k.ts("dve", Kh[:, h, :], ps[:, hh * 128:(hh + 1) * 128], om[:, h:h + 1], None, ALU.mult, None,
                         [ps, om], [Kh])
                else:
                    k.act(Kh[:, h, :], ps[:, hh * 128:(hh + 1) * 128], AF.Identity, [ps, om], [Kh], scale=om[:, h:h + 1])
        Vp = k.rot("ms_Vp", [128, 8, 257], BF16, n=2, init=ones_col)
        for vb in range(4):
            ps = k.psum()
            for j in range(KD):
                k.mm(ps[:], hT[:, j, :], wq[:, j, 2048 + vb * 512:2048 + (vb + 1) * 512], j == 0, j == KD - 1, [hT, wq], [ps])
            dv = Vp[:, vb * 2:(vb + 1) * 2, 0:256]
            sv = ps[:, :].rearrange("p (h v) -> p h v", h=2)
            k.copy("act" if vb % 2 == 0 else "dve", dv, sv, [ps], [Vp])
        psden = k.psum()
        psH = [None] * 4
        Ps = []
        mask = k.m_le if d == 0 else k.m_ge
        for h in range(ML_H):
            if h % 4 == 0:
                psS = k.psum()
            cs = slice((h % 4) * 128, (h % 4 + 1) * 128)
            k.mm(psS[:, cs], kT[:, h, :], qT[:, h, :], True, True, [kT, qT], [psS])
            P = k.rot("ms_P", [128, 128], BF16, n=4)
            k.stt(P[:], psS[:, cs], al[:, h:h + 1], mask[:], ALU.mult, ALU.mult, [psS, al, mask], [P])
            if h % 2 == 0:
                psH[h // 2] = k.psum()
            pH = psH[h // 2]
            hs_ = slice((h % 2) * 256, (h % 2 + 1) * 256)
            k.mm(pH[:, hs_], qT[:, h, :], Cb[h][:, 0:256], True, False, [qT, Cb[h]], [pH])
            k.mm(pH[:, hs_], P[:], Vp[:, h, 0:256], False, True, [P, Vp], [pH])
            k.mm(psden[:, h:h + 1], qT[:, h, :], Cb[h][:, 256:257], True, False, [qT, Cb[h]], [psden])
            k.mm(psden[:, h:h + 1], P[:], Vp[:, h, 256:257], False, True, [P, Vp], [psden])
        dd = k.rot("ms_dd", [128, 8], F32, n=2)
        k.tt("dve", dd[:], psden[:, 0:8], bg[:, 0:8], ALU.mult, [psden, bg], [dd])
        k.act(dd[:], dd[:], AF.Abs, [dd], [dd])
        k.ts("dve", dd[:], dd[:], 1.0, None, ALU.max, None, [dd], [dd])
        k.recip(dd[:], dd[:], [dd], [dd])
        scl = k.rot("ms_scl", [128, 8], F32, n=2)
        k.tt("dve", scl[:], dd[:], bg[:, 0:8], ALU.mult, [dd, bg], [scl])
        hd = k.rot("ms_hd", [128, ML_W], F32, n=1)
        if d == 0 or not fin:
            for h in range(ML_H):
                pH = psH[h // 2]
                hs_ = slice((h % 2) * 256, (h % 2 + 1) * 256)
                k.act(hd[:, h * 256:(h + 1) * 256], pH[:, hs_], AF.Identity, [pH, scl], [hd], scale=scl[:, h:h + 1])
            if d == 0:
                k.dma(HFs.t[t0:t0 + 128, :], hd[:], [hd], [HFs], q="act")
            _ml_state(k, Kh, Vp, Cf, Cb, bg)
        else:
            hf = k.rot("ms_hf", [128, ML_W], F32, n=1)
            k.dma(hf[:], HFs.t[t0:t0 + 128, :], [HFs], [hf])
            szg = k.rot("ms_szg", [128, ML_W], BF16, n=1)
            k.dma(szg[:], SZs.t[t0:t0 + 128, :], [SZs], [szg])
            for h in range(ML_H):
                pH = psH[h // 2]
                hs_ = slice((h % 2) * 256, (h % 2 + 1) * 256)
                k.stt(hd[:, h * 256:(h + 1) * 256], pH[:, hs_], scl[:, h:h + 1], hf[:, h * 256:(h + 1) * 256],
                      ALU.mult, ALU.add, [pH, scl, hf], [hd])
            _ml_state(k, Kh, Vp, Cf, Cb, bg)
            sqv = k.rot("ms_sq", [128, ML_W], F32, n=1)
            k.tt("pool", sqv[:], hd[:], hd[:], ALU.mult, [hd], [sqv])
            ss = k.rot("ms_ss", [128, 8], F32, n=2)
            S.op("dve", lambda: nc.vector.tensor_reduce(out=ss[:], in_=sqv[:, :].rearrange("p (h v) -> p h v", h=8),
                                                        axis=AX.X, op=ALU.add), reads=[sqv], writes=[ss])
            rs = rstd_of(k, ss, float(ML_DV), NORM_EPS, 8)
            yt = k.rot("ms_yt", [128, ML_W], F32, n=1)
            for h in range(ML_H):
                k.stt(yt[:, h * 256:(h + 1) * 256], hd[:, h * 256:(h + 1) * 256], rs[:, h:h + 1], szg[:, h * 256:(h + 1) * 256],
                      ALU.mult, ALU.mult, [hd, rs, szg], [yt])
            yT = k.rot("ms_yT", [128, 16, 128], BF16, n=2)
            for tb in range(4):
                ps = k.psum()
                for jj in range(4):
                    j = tb * 4 + jj
                    k.tr(ps[:, jj * 128:(jj + 1) * 128], yt[:, j * 128:(j + 1) * 128], k.ident[:], [yt, k.ident], [ps])
                k.copy("act" if tb % 2 == 0 else "dve", yT[:, tb * 4:(tb + 1) * 4, :],
                       ps[:, :].rearrange("p (j t) -> p j t", j=4), [ps], [yT])
            k.dma(YGs.t[:, t0:t0 + 128].rearrange("(j p) t -> p j t", p=128), yT[:], [yT], [YGs], q="act")
R7_L = 128
C_ED = float(np.exp(-0.5))


def _r7_load_h_sh(k, sq, t0, want_sh=True):
    Tn = sq["Tn"]
    HTs = sq["HTs"]
    hb = k.rot("r7_h", [128, 8, 128], BF16, n=1)
    k.dma(hb[:], HTs.t[:, t0:t0 + 128].rearrange("(j p) t -> p j t", p=128), [HTs], [hb])
    sh = k.rot("r7_sh", [128, 8, 128], BF16, n=1)

    def ld(j0, j1, c0, c1, s0):
        k.dma(sh[:, j0:j1, c0:c1], HTs.t[j0 * 128:j1 * 128, s0:s0 + (c1 - c0)].rearrange("(j p) t -> p j t", p=128),
              [HTs], [sh])

    def z(j0, j1, c0, c1):
        k.memset("pool", sh[:, j0:j1, c0:c1], 0.0, [sh])

    if sq["name"] == "x":
        ld(0, 2, 1, 128, t0)
        z(0, 2, 0, 1)
        z(0, 2, 64, 65)
        ld(2, 4, 0, 127, t0 + 1)
        z(2, 4, 63, 64)
        z(2, 4, 127, 128)
        if t0 >= 64:
            ld(4, 6, 0, 128, t0 - 64)
        else:
            ld(4, 6, 64, 128, 0)
            z(4, 6, 0, 64)
        if t0 + 192 <= Tn:
            ld(6, 8, 0, 128, t0 + 64)
        else:
            ld(6, 8, 0, 64, t0 + 64)
            z(6, 8, 64, 128)
    else:
        if t0 > 0:
            ld(0, 4, 0, 128, t0 - 1)
        else:
            ld(0, 4, 1, 128, 0)
            z(0, 4, 0, 1)
        if t0 + 129 <= Tn:
            ld(4, 8, 0, 128, t0 + 1)
        else:
            ld(4, 8, 0, 127, t0 + 1)
            z(4, 8, 127, 128)
    return hb, sh


def _bc(ap2, n):
    return ap2.unsqueeze(2).to_broadcast([128, 8, n])


def rwkv_layer(k, li, jl, W, seqs, scr):
    nc, S = k.nc, k.S
    for sq in seqs:
        pass_norm(k, sq["src"], sq["Tn"], sq["M"], sq["HTs"])
    wr = W["r7_w_rkvz"]
    with k.scope():
        def pload(name, ap, shape):
            b = S.sb(shape, F32, name=name)
            k.dma(b[:], ap, [W["r7_mu"]], [b], slow=True)
            return b
        mu = pload("r7_mu", W["r7_mu"].t[jl].rearrange("g (j p) -> p g j", p=128), [128, 6, 8])
        w0 = pload("r7_w0", W["r7_w0"].t[jl].rearrange("d (j p) -> p d j", p=128), [128, 2, 8])
        a0 = pload("r7_a0", W["r7_a0"].t[jl].rearrange("d (j p) -> p d j", p=128), [128, 2, 8])
        kkp = pload("r7_kk", W["r7_k_k"].t[jl].rearrange("(j p) -> p j", p=128), [128, 8])
        kap = pload("r7_ka", W["r7_k_a"].t[jl].rearrange("(j p) -> p j", p=128), [128, 8])
        rkp = pload("r7_rk", W["r7_r_k"].t[jl].rearrange("(j p) -> p j", p=128), [128, 8])
        omka = S.sb([128, 8], F32, name="r7_omka")
        k.ts("dve", omka[:], kap[:], -1.0, 1.0, ALU.mult, ALU.add, [kap], [omka])
        with k.scope():
            wz = load_w_bf16(k, "r7_wz", lambda j, c0, sc: wr.t[jl, 3, j * 128:(j + 1) * 128, c0:c0 + sc], wr, KD, D)
            for sq in seqs:
                if not sq["want_out"]:
                    continue
                for t0 in range(0, sq["Tn"], 128):
                    hb, sh = _r7_load_h_sh(k, sq, t0)
                    df = k.rot("rz_df", [128, 8, 128], F32, n=2)
                    k.tt("dve", df[:], sh[:], hb[:], ALU.subtract, [sh, hb], [df])
                    k.tt("pool", df[:], df[:], _bc(mu[:, 3, :], 128), ALU.mult, [df, mu], [df])
                    xz = k.rot("rz_xz", [128, 8, 128], BF16, n=2)
                    k.tt("dve", xz[:], df[:], hb[:], ALU.add, [df, hb], [xz])
                    szt = k.rot("rz_sz", [128, D], BF16, n=2)
                    for nb in range(2):
                        ps = k.psum()
                        for j in range(KD):
                            k.mm(ps[:], xz[:, j, :], wz[:, j, nb * 512:(nb + 1) * 512], j == 0, j == KD - 1, [xz, wz], [ps])
                        k.act(szt[:, nb * 512:(nb + 1) * 512], ps[:], AF.Silu, [ps], [szt])
                    k.dma(sq["SZs"].t[t0:t0 + 128, 0:D], szt[:], [szt], [sq["SZs"]], q="pool")
        with k.scope():
            wts = {}
            for gi, nm in ((0, "r"), (1, "k"), (2, "v")):
                wts[nm] = load_w_bf16(k, "r7_w" + nm, lambda j, c0, sc, gi=gi: wr.t[jl, gi, j * 128:(j + 1) * 128, c0:c0 + sc],
                                      wr, KD, D)
            lw1 = [load_w_bf16(k, f"r7_w1{d}", lambda j, c0, sc, d=d: W["r7_w1"].t[jl, d, j * 128:(j + 1) * 128, :], W["r7_w1"], KD, 64)
                   for d in range(2)]
            la1 = [load_w_bf16(k, f"r7_a1{d}", lambda j, c0, sc, d=d: W["r7_a1"].t[jl, d, j * 128:(j + 1) * 128, :], W["r7_a1"], KD, 64)
                   for d in range(2)]
            lw2 = [S.sb([64, D], BF16, name=f"r7_w2_{d}") for d in range(2)]
            la2 = [S.sb([64, D], BF16, name=f"r7_a2_{d}") for d in range(2)]
            with k.scope():
                for d in range(2):
                    for (lst, nm) in ((lw2, "r7_w2"), (la2, "r7_a2")):
                        stg = k.rot("r7_stg2", [64, D], F32, n=2)
                        k.dma(stg[:], W[nm].t[jl, d], [W[nm]], [stg])
                        k.copy("dve", lst[d][:], stg[:], [stg], [lst[d]])
            hmA = S.sb([128, 1], F32, name="r7_hmA")
            hmB = S.sb([128, 1], F32, name="r7_hmB")
            k.memset("pool", hmA[:], 0.0, [hmA])
            k.memset("pool", hmA[0:64, :], 1.0, [hmA])
            k.memset("pool", hmB[:], 1.0, [hmB])
            k.memset("pool", hmB[0:64, :], 0.0, [hmB])
            hsel = S.sb([128, 2], F32, name="r7_hsel")
            k.copy("pool", hsel[:, 0:1], hmA[:], [hmA], [hsel])
            k.copy("pool", hsel[:, 1:2], hmB[:], [hmB], [hsel])
            bones = S.sb([128, 128], F32, name="r7_bones")
            k.memset("pool", bones[:], 0.0, [bones])
            k.memset("pool", bones[0:64, 0:64], 1.0, [bones])
            k.memset("pool", bones[64:128, 64:128], 1.0, [bones])
            id4 = S.sb([128, 4, 128], BF16, name="r7_id4")
            for i4 in range(4):
                k.copy("pool", id4[:, i4, :], k.identb[:], [k.identb], [id4])
            smt = S.sb([128, 2, 7, 128], BF16, name="r7_smt")
            with k.scope():
                for l in range(7):
                    w2, w1 = 1 << (l + 1), 1 << l

                    def band(lo_off, hi_off, nm, w2=w2):
                        b = k.rot("r7_band_" + nm, [128, 128], F32, n=2)
                        S.op("pool", lambda: nc.gpsimd.affine_select(out=b[:], in_=k.ones[:, 0:128], pattern=[[1, 128]],
                                                                     compare_op=ALU.is_ge, fill=0.0, base=-lo_off, channel_multiplier=-w2),
                             reads=[k.ones], writes=[b])
                        S.op("pool", lambda: nc.gpsimd.affine_select(out=b[:], in_=b[:], pattern=[[-1, 128]],
                                                                     compare_op=ALU.is_gt, fill=0.0, base=hi_off, channel_multiplier=w2),
                             reads=[b], writes=[b])
                        return b
                    Lf = band(0, w1, "L")
                    Rt = band(w1, w2, "R")
                    for dd, (lt_, rh_) in ((0, (Rt, Lf)), (1, (Lf, Rt))):
                        ps = k.psum()
                        k.mm(ps[:, 0:128], lt_[:], rh_[:], True, True, [lt_, rh_], [ps])
                        k.copy("act", smt[:, dd, l, :], ps[:, 0:128], [ps], [smt])
            Z = [S.sb([128, 8, 64], F32, name=f"r7_Z{d}") for d in range(2)]
            Zb = [S.sb([128, 8, 64], BF16, name=f"r7_Zb{d}") for d in range(2)]
            for d in range(2):
                k.memset("pool", Z[d][:], 0.0, [Z[d]])
                k.memset("pool", Zb[d][:], 0.0, [Zb[d]])
            lng = S.sb([128, D], F32, name="r7_lng")
            k.dma(lng[:], W["r7_ln_g"].t[jl].partition_broadcast(128), [W["r7_ln_g"]], [lng])
            lnb = S.sb([128, D], F32, name="r7_lnb")
            k.dma(lnb[:], W["r7_ln_b"].t[jl].partition_broadcast(128), [W["r7_ln_b"]], [lnb])
            cst = dict(mu=mu, w0=w0, a0=a0, kkp=kkp, kap=kap, rkp=rkp, omka=omka, wts=wts, lw1=lw1, la1=la1, lw2=lw2, la2=la2,
                       hm=(hmA, hmB), hsel=hsel, bones=bones, lng=lng, lnb=lnb, id4=id4, smt=smt,
                       RS=dict(r=scr["R7r"], k=scr["R7k"], kk=scr["R7kk"], v=scr["R7v"]))
            for sq in seqs:
                for d in range(2):
                    nch = sq["Tn"] // 128
                    order = range(nch) if d == 0 else range(nch - 1, -1, -1)
                    for c in order:
                        _r7_chunk(k, sq, d, c * 128, cst, Z[d], Zb[d])
        with k.scope():
            w_out = load_w_bf16(k, "r7_wout", lambda j, c0, sc: W["r7_w_out"].t[jl, j * 128:(j + 1) * 128, c0:c0 + sc],
                                W["r7_w_out"], KD, D)
            for sq in seqs:
                if sq["want_out"]:
                    pass_out(k, sq["YGs"], KD, w_out, sq["Tn"], sq["src"], sq["M"], sq["dst"])


def _r7_chunk(k, sq, d, t0, c, Z, Zb):
    nc, S = k.nc, k.S
    mu, wts = c["mu"], c["wts"]
    fin = (d == 1) and sq["want_out"]
    F = lambda i: k.rot(f"r7_F{i}", [128, 8, 128], F32, n=1)
    hb, sh = _r7_load_h_sh(k, sq, t0)
    df = F(0)
    k.tt("dve", df[:], sh[:], hb[:], ALU.subtract, [sh, hb], [df])

    def lerp(g, eng="dve"):
        t = F(1)
        k.tt("pool", t[:], df[:], _bc(mu[:, g, :], 128), ALU.mult, [df, mu], [t])
        o = k.rot("r7_xs", [128, 8, 128], BF16, n=2)
        k.tt(eng, o[:], t[:], hb[:], ALU.add, [t, hb], [o])
        return o

    def proj_fm(xs, wt, evac):
        for jb in range(2):
            ps = k.psum()
            for jj in range(4):
                j = jb * 4 + jj
                for kk_ in range(KD):
                    k.mm(ps[:, jj * 128:(jj + 1) * 128], wt[:, kk_, j * 128:(j + 1) * 128], xs[:, kk_, :], kk_ == 0, kk_ == KD - 1,
                         [wt, xs], [ps])
            evac(ps, jb)

    def v4(ps):
        return ps[:, :].rearrange("p (j t) -> p j t", j=4)

    if DBG.get("r7") == "a":
        return
    rT = F(2)
    kT = F(3)
    Vb = k.rot("r7_Vb", [128, D], BF16, n=1)
    RS = c["RS"]
    reuse = (d == 1)
    rows = slice(t0, t0 + 128)
    if not reuse:
        xr = lerp(0)
        proj_fm(xr, wts["r"], lambda ps, jb: k.copy("act", rT[:, jb * 4:(jb + 1) * 4, :], v4(ps), [ps], [rT]))
        xk = lerp(1)
        proj_fm(xk, wts["k"], lambda ps, jb: k.copy("dve", kT[:, jb * 4:(jb + 1) * 4, :], v4(ps), [ps], [kT]))
        xv = lerp(2)
        for nb in range(2):
            ps = k.psum()
            for kk_ in range(KD):
                k.mm(ps[:], xv[:, kk_, :], wts["v"][:, kk_, nb * 512:(nb + 1) * 512], kk_ == 0, kk_ == KD - 1, [xv, wts["v"]], [ps])
            k.copy("act", Vb[:, nb * 512:(nb + 1) * 512], ps[:], [ps], [Vb])
        k.dma(RS["r"].t[rows, :].rearrange("p (j t) -> p j t", j=8), rT[:], [rT], [RS["r"]], q="act")
        k.dma(RS["k"].t[rows, :].rearrange("p (j t) -> p j t", j=8), kT[:], [kT], [RS["k"]], q="act")
        k.dma(RS["v"].t[rows, :], Vb[:], [Vb], [RS["v"]], q="act")
    if DBG.get("r7") == "b":
        return
    lows = []
    for (g, w1, fn) in ((4, c["lw1"][d], AF.Tanh), (5, c["la1"][d], AF.Identity)):
        xs_ = lerp(g)
        ps = k.psum()
        for kk_ in range(KD):
            k.mm(ps[0:64, 0:128], w1[:, kk_, :], xs_[:, kk_, :], kk_ == 0, kk_ == KD - 1, [w1, xs_], [ps])
        lo = k.rot("r7_lo", [64, 128], BF16, n=2)
        k.act(lo[:], ps[0:64, 0:128], fn, [ps], [lo])
        lows.append(lo)
    if reuse:
        k.dma(rT[:], RS["r"].t[rows, :].rearrange("p (j t) -> p j t", j=8), [RS["r"]], [rT])
        k.dma(kT[:], RS["k"].t[rows, :].rearrange("p (j t) -> p j t", j=8), [RS["k"]], [kT])
        k.dma(Vb[:], RS["v"].t[rows, :], [RS["v"]], [Vb])
    lwv = F(4)
    aT = F(5)
    for (lo, w2, dst, bias) in ((lows[0], c["lw2"][d], lwv, c["w0"]), (lows[1], c["la2"][d], aT, c["a0"])):
        for jb in range(2):
            ps = k.psum()
            for jj in range(4):
                j = jb * 4 + jj
                k.mm(ps[:, jj * 128:(jj + 1) * 128], w2[:, j * 128:(j + 1) * 128], lo[:], True, True, [w2, lo], [ps])
            for jj in range(4):
                j = jb * 4 + jj
                k.act(dst[:, j, :], ps[:, jj * 128:(jj + 1) * 128], AF.Sigmoid, [ps, bias], [dst], bias=bias[:, d, j:j + 1])
    k.ts("pool", lwv[:], lwv[:], -C_ED, None, ALU.mult, None, [lwv], [lwv])
    if DBG.get("r7") == "c":
        return
    k.dump("rT", rT[:], [128, 8, 128], F32, [rT])
    k.dump("kT", kT[:], [128, 8, 128], F32, [kT])
    k.dump("lw", lwv[:], [128, 8, 128], F32, [lwv])
    k.dump("aT", aT[:], [128, 8, 128], F32, [aT])
    k.dump("Vf", Vb[:], [128, D], BF16, [Vb])
    kkr = F(0)
    if reuse:
        k.dma(kkr[:], RS["kk"].t[rows, :].rearrange("p (j t) -> p j t", j=8), [RS["kk"]], [kkr])
    if not reuse:
        pass
        k.tt("dve", kkr[:], kT[:], _bc(c["kkp"][:, :], 128), ALU.mult, [kT, c["kkp"]], [kkr])
        kk2 = F(1)
        k.tt("pool", kk2[:], kkr[:], kkr[:], ALU.mult, [kkr], [kk2])
        nrm = F(6)
        for jb in range(2):
            ps = k.psum()
            for jj in range(4):
                j = jb * 4 + jj
                k.mm(ps[:, jj * 128:(jj + 1) * 128], c["bones"][:], kk2[:, j, :], True, True, [c["bones"], kk2], [ps])
            k.act(nrm[:, jb * 4:(jb + 1) * 4, :], v4(ps), AF.Sqrt, [ps], [nrm])
        k.ts("dve", nrm[:], nrm[:], 1e-12, None, ALU.max, None, [nrm], [nrm])
        k.recip(nrm[:], nrm[:], [nrm], [nrm])
        k.tt("pool", kkr[:], kkr[:], nrm[:], ALU.mult, [kkr, nrm], [kkr])
        k.dma(RS["kk"].t[rows, :].rearrange("p (j t) -> p j t", j=8), kkr[:], [kkr], [RS["kk"]], q="act")
    kd = F(7)
    k.tt("dve", kd[:], aT[:], _bc(c["kap"][:, :], 128), ALU.mult, [aT, c["kap"]], [kd])
    k.tt("pool", kd[:], kd[:], _bc(c["omka"][:, :], 128), ALU.add, [kd, c["omka"]], [kd])
    k.tt("dve", kd[:], kd[:], kT[:], ALU.mult, [kd, kT], [kd])
    if DBG.get("r7") == "d":
        return
    k.dump("kk", kkr[:], [128, 8, 128], F32, [kkr])
    k.dump("kd", kd[:], [128, 8, 128], F32, [kd])
    pr = F(1)
    k.tt("pool", pr[:], rT[:], _bc(c["rkp"][:, :], 128), ALU.mult, [rT, c["rkp"]], [pr])
    k.tt("dve", pr[:], pr[:], kd[:], ALU.mult, [pr, kd], [pr])
    psb = k.psum()
    for j in range(8):
        k.mm(psb[:, 2 * j:2 * j + 2], pr[:, j, :], c["hsel"][:], True, True, [pr, c["hsel"]], [psb])
    bs = k.rot("r7_bs", [128, 16], F32, n=2)
    k.copy("act", bs[:], psb[:, 0:16], [psb], [bs])
    if DBG.get("r7") == "e":
        return
    cs = F(1)
    for j in range(8):
        if d == 0:
            S.op("dve", lambda j=j: nc.vector.tensor_tensor_scan(out=cs[:, j, :], data0=k.ones[:, 0:128], data1=lwv[:, j, :], initial=0.0,
                                                                 op0=ALU.mult, op1=ALU.add), reads=[k.ones, lwv], writes=[cs])
        else:
            S.op("dve", lambda j=j: nc.vector.tensor_tensor_scan(out=cs[:, j, ::-1], data0=k.ones[:, 0:128], data1=lwv[:, j, ::-1],
                                                                 initial=0.0, op0=ALU.mult, op1=ALU.add),
                 reads=[k.ones, lwv], writes=[cs])
    gam = F(3)
    k.act(gam[:], cs[:], AF.Exp, [cs], [gam])
    gin = F(6)
    k.act(gin[:], cs[:], AF.Exp, [cs], [gin], scale=-1.0)
    gpv = lwv
    k.tt("pool", gpv[:], cs[:], lwv[:], ALU.subtract, [cs, lwv], [gpv])
    k.act(gpv[:], gpv[:], AF.Exp, [gpv], [gpv])
    k.dump("cs", cs[:], [128, 8, 128], F32, [cs])
    k.dump("gam", gam[:], [128, 8, 128], F32, [gam])
    k.dump("gin", gin[:], [128, 8, 128], F32, [gin])
    k.dump("gpv", gpv[:], [128, 8, 128], F32, [gpv])
    gL = k.rot("r7_gL", [128, 8], F32, n=2)
    lastcol = 127 if d == 0 else 0
    k.copy("pool", gL[:], gam[:, :, lastcol], [gam], [gL])
    if DBG.get("r7") == "f":
        return
    hmA, hmB = c["hm"]
    rt_ = rT
    k.tt("dve", rt_[:], rT[:], gam[:], ALU.mult, [rT, gam], [rt_])
    at_ = aT
    k.tt("dve", at_[:], kkr[:], aT[:], ALU.mult, [kkr, aT], [at_])
    k.tt("pool", at_[:], at_[:], gin[:], ALU.mult, [at_, gin], [at_])
    kt_ = kkr
    k.tt("pool", kt_[:], kkr[:], gpv[:], ALU.mult, [kkr, gpv], [kt_])
    kdt = kd
    k.tt("dve", kdt[:], kd[:], gin[:], ALU.mult, [kd, gin], [kdt])
    ops = {}
    ei = 0
    for nm, src in (("rt", rt_), ("kt", kt_), ("at", at_), ("kd", kdt)):
        hA = k.rot("r7_bA_" + nm, [128, 8, 128], BF16, n=1)
        hB = k.rot("r7_bB_" + nm, [128, 8, 128], BF16, n=1)
        if ei % 2 == 0:
            k.ts("dve", hA[:], src[:], hmA[:, 0:1], None, ALU.mult, None, [src, hmA], [hA])
            k.act(hB[:], src[:], AF.Identity, [src, hmB], [hB], scale=hmB[:, 0:1])
        else:
            k.act(hA[:], src[:], AF.Identity, [src, hmA], [hA], scale=hmA[:, 0:1])
            k.ts("dve", hB[:], src[:], hmB[:, 0:1], None, ALU.mult, None, [src, hmB], [hB])
        ei += 1
        ops[nm] = (None, hA, hB)
    if DBG.get("r7") == "g":
        return
    toks = {}
    for nm, src in (("at", at_), ("kd", kdt)):
        tk = k.rot("r7_tok_" + nm, [128, 8, 128], BF16, n=1)
        for jb in range(2):
            ps = k.psum()
            for jj in range(4):
                j = jb * 4 + jj
                k.tr(ps[:, jj * 128:(jj + 1) * 128], src[:, j, :], k.ident[:], [src, k.ident], [ps])
            k.copy("act" if jb == 0 else "dve", tk[:, jb * 4:(jb + 1) * 4, :], v4(ps), [ps], [tk])
        toks[nm] = tk
    if DBG.get("r7") == "h":
        return
    if d == 0:
        mb_lt, mb_gt, mb_le = k.m_lt, k.m_gt, k.m_le
    else:
        mb_lt, mb_gt, mb_le = k.m_gt, k.m_lt, k.m_ge
    b4 = lambda t: t[:, :].unsqueeze(1).to_broadcast([128, 4, 128])
    MkT = k.rot("r7_MkT", [128, 16, 128], BF16, n=1)
    NaT = k.rot("r7_NaT", [128, 16, 128], BF16, n=1)
    NkT = k.rot("r7_NkT", [128, 16, 128], BF16, n=1)
    TT = k.rot("r7_TT", [128, 16, 128], BF16, n=1)

    def hop(nm, h):
        return ops[nm][1 + (h % 2)][:, h // 2, :], ops[nm][1 + (h % 2)]

    def fop(nm, h):
        return hop(nm, h)

    ATs = []
    for hg in range(4):
        heads = range(hg * 4, hg * 4 + 4)
        hs4 = slice(hg * 4, hg * 4 + 4)

        def form(lnm, rnm, evac, heads=heads):
            ps = k.psum()
            for i4, h in enumerate(heads):
                la, lb_ = hop(lnm, h)
                ra, rb_ = fop(rnm, h)
                k.mm(ps[:, i4 * 128:(i4 + 1) * 128], la, ra, True, True, [lb_, rb_], [ps])
            evac(ps)
        AT = k.rot("r7_AT", [128, 4, 128], BF16, n=4)
        ATs.append(AT)
        form("kt", "at", lambda ps: k.stt(AT[:], v4(ps), -1.0, b4(mb_gt), ALU.mult, ALU.mult, [ps, mb_gt], [AT]))
        form("kd", "kt", lambda ps: k.tt("dve", MkT[:, hs4, :], v4(ps), b4(mb_lt), ALU.mult, [ps, mb_lt], [MkT]))
        form("at", "rt", lambda ps: k.tt("dve", NaT[:, hs4, :], v4(ps), b4(mb_le), ALU.mult, [ps, mb_le], [NaT]))
        form("kd", "rt", lambda ps: k.tt("dve", NkT[:, hs4, :], v4(ps), b4(mb_le), ALU.mult, [ps, mb_le], [NkT]))
    X = [c["id4"]] * 4
    XT = [c["id4"]] * 4
    for l in range(7):
        CTs, P1s, pss = [], [], []
        for hg in range(4):
            CT = k.rot("r7_CT", [128, 4, 128], BF16, n=4)
            k.tt("pool", CT[:], ATs[hg][:], c["smt"][:, d, l, :].unsqueeze(1).to_broadcast([128, 4, 128]), ALU.mult,
                 [ATs[hg], c["smt"]], [CT])
            CTs.append(CT)
        for hg in range(4):
            ps = k.psum()
            for i4 in range(4):
                k.mm(ps[:, i4 * 128:(i4 + 1) * 128], CTs[hg][:, i4, :], X[hg][:, i4, :], True, True, [CTs[hg], X[hg]], [ps])
            P1 = k.rot("r7_P1", [128, 4, 128], BF16, n=4)
            k.copy("act" if hg % 2 == 0 else "dve", P1[:], v4(ps), [ps], [P1])
            P1s.append(P1)
        for hg in range(4):
            ps = k.psum()
            for i4 in range(4):
                k.mm(ps[:, i4 * 128:(i4 + 1) * 128], k.identb[:], X[hg][:, i4, :], True, False, [k.identb, X[hg]], [ps])
                k.mm(ps[:, i4 * 128:(i4 + 1) * 128], XT[hg][:, i4, :], P1s[hg][:, i4, :], False, True, [XT[hg], P1s[hg]], [ps])
            pss.append(ps)
        Xn_l = []
        for hg in range(4):
            hs4 = slice(hg * 4, hg * 4 + 4)
            if l < 6:
                Xn = k.rot("r7_X", [128, 4, 128], BF16, n=4)
                k.copy("dve" if hg % 2 == 0 else "act", Xn[:], v4(pss[hg]), [pss[hg]], [Xn])
                Xn_l.append(Xn)
            else:
                k.copy("dve" if hg % 2 == 0 else "act", TT[:, hs4, :], v4(pss[hg]), [pss[hg]], [TT])
        if l < 6:
            XTn_l = []
            for hg in range(4):
                ps = k.psum()
                for i4 in range(4):
                    k.mm(ps[:, i4 * 128:(i4 + 1) * 128], Xn_l[hg][:, i4, :], k.identb[:], True, True, [Xn_l[hg], k.identb], [ps])
                XTn = k.rot("r7_XT", [128, 4, 128], BF16, n=4)
                k.copy("act" if hg % 2 == 0 else "dve", XTn[:], v4(ps), [ps], [XTn])
                XTn_l.append(XTn)
            X, XT = Xn_l, XTn_l
    k.dump("TT", TT[:], [128, 16, 128], BF16, [TT])
    k.dump("MkT", MkT[:], [128, 16, 128], BF16, [MkT])
    k.dump("NaT", NaT[:], [128, 16, 128], BF16, [NaT])
    k.dump("NkT", NkT[:], [128, 16, 128], BF16, [NkT])
    def v8(ps):
        return ps[:, :].rearrange("p (h v) -> p h v", h=8)
    Bs = k.rot("r7_Bs", [128, 16, 64], BF16, n=1)
    for hb_ in range(2):
        ps = k.psum()
        for i8 in range(8):
            h = hb_ * 8 + i8
            cs_ = slice(i8 * 64, (i8 + 1) * 64)
            k.mm(ps[:, cs_], MkT[:, h, :], Vb[:, h * 64:(h + 1) * 64], True, False, [MkT, Vb], [ps])
            la, lb_ = hop("kt", h)
            k.mm(ps[:, cs_], la, Zb[:, h // 2, :], False, True, [lb_, Zb], [ps])
        k.copy("act" if hb_ == 0 else "dve", Bs[:, hb_ * 8:(hb_ + 1) * 8, :], v8(ps), [ps], [Bs])
    Us = k.rot("r7_Us", [128, 16, 64], BF16, n=1)
    for hb_ in range(2):
        ps = k.psum()
        for i8 in range(8):
            h = hb_ * 8 + i8
            k.mm(ps[:, i8 * 64:(i8 + 1) * 64], TT[:, h, :], Bs[:, h, :], True, True, [TT, Bs], [ps])
        if hb_ == 0:
            k.act(Us[:, 0:8, :], v8(ps), AF.Identity, [ps], [Us], scale=-1.0)
        else:
            k.ts("dve", Us[:, 8:16, :], v8(ps), -1.0, None, ALU.mult, None, [ps], [Us])
    psY = [k.psum(), k.psum()]
    for h in range(16):
        ps = psY[h // 8]
        cs_ = slice((h % 8) * 64, (h % 8 + 1) * 64)
        la, lb_ = hop("rt", h)
        k.mm(ps[:, cs_], la, Zb[:, h // 2, :], True, False, [lb_, Zb], [ps])
        k.mm(ps[:, cs_], NaT[:, h, :], Us[:, h, :], False, False, [NaT, Us], [ps])
        k.mm(ps[:, cs_], NkT[:, h, :], Vb[:, h * 64:(h + 1) * 64], False, True, [NkT, Vb], [ps])
    yv = F(2)
    yv2 = V2(yv)
    bsum = bs
    if fin:
        yf_ = F(3)
        yf = V2(yf_)
        k.dma(yf[:], sq["HFs"].t[t0:t0 + 128, 0:D], [sq["HFs"]], [yf_])
        bsf = k.rot("r7_bsf", [128, 16], F32, n=2)
        k.dma(bsf[:], sq["HFs"].t[t0:t0 + 128, D:D + 16], [sq["HFs"]], [bsf])
        for hb_ in range(2):
            k.tt("dve", yv2[:, hb_ * 512:(hb_ + 1) * 512], psY[hb_][:], yf[:, hb_ * 512:(hb_ + 1) * 512], ALU.add, [psY[hb_], yf_], [yv])
        k.tt("pool", bsf[:], bsf[:], bs[:], ALU.add, [bsf, bs], [bsf])
        bsum = bsf
    else:
        k.copy("act", yv2[:, 0:512], psY[0][:], [psY[0]], [yv])
        k.copy("dve", yv2[:, 512:1024], psY[1][:], [psY[1]], [yv])
        if d == 0:
            k.dma(sq["HFs"].t[t0:t0 + 128, 0:D], yv2[:], [yv], [sq["HFs"]], q="act")
            k.dma(sq["HFs"].t[t0:t0 + 128, D:D + 16], bs[:], [bs], [sq["HFs"]], q="act")
    if DBG.get("r7") == "j":
        return
    k.dump("yv", yv[:], [128, 8, 128], F32, [yv])
    k.dump("Us", Us[:], [128, 16, 64], BF16, [Us])
    k.dump("Bs", Bs[:], [128, 16, 64], BF16, [Bs])
    for jb in range(2):
        ps = k.psum()
        for jj in range(4):
            j = jb * 4 + jj
            cs_ = slice(jj * 128, (jj + 1) * 128)
            k.mm(ps[:, cs_], toks["at"][:, j, :], Us[:, 2 * j:2 * j + 2, :].rearrange("p h v -> p (h v)"), True, False, [toks["at"], Us], [ps])
            k.mm(ps[:, cs_], toks["kd"][:, j, :], Vb[:, j * 128:(j + 1) * 128], False, True, [toks["kd"], Vb], [ps])
        pv = ps[:, :].rearrange("p (j x) -> p j x", j=4)
        js = slice(jb * 4, jb * 4 + 4)
        k.tt("dve", Z[0:64, js, :], Z[0:64, js, :], pv[0:64, :, 0:64], ALU.add, [Z, ps], [Z])
        k.tt("dve", Z[64:128, js, :], Z[64:128, js, :], pv[64:128, :, 64:128], ALU.add, [Z, ps], [Z])
    k.tt("pool", Z[:], Z[:], gL[:, :].unsqueeze(2).to_broadcast([128, 8, 64]), ALU.mult, [Z, gL], [Z])
    k.copy("act", Zb[:], Z[:], [Z], [Zb])
    k.dump("Z", Z[:], [128, 8, 64], F32, [Z])
    if not fin:
        return
    if DBG.get("r7") == "l":
        return
    y3 = yv2[:, :].rearrange("p (h c) -> p h c", h=16)
    st1 = k.rot("r7_st1", [128, 16], F32, n=2)
    S.op("dve", lambda: nc.vector.tensor_reduce(out=st1[:], in_=y3, axis=AX.X, op=ALU.add), reads=[yv], writes=[st1])
    k.ts("dve", st1[:], st1[:], -1.0 / 64.0, None, ALU.mult, None, [st1], [st1])
    b16 = lambda t: t[:, :].unsqueeze(2).to_broadcast([128, 16, 64])
    k.tt("pool", y3, y3, b16(st1), ALU.add, [yv, st1], [yv])
    sqv_ = F(5)
    sqv = V2(sqv_)
    k.tt("dve", sqv[:], yv2[:], yv2[:], ALU.mult, [yv], [sqv_])
    st2 = k.rot("r7_st2", [128, 16], F32, n=2)
    S.op("dve", lambda: nc.vector.tensor_reduce(out=st2[:], in_=sqv[:, :].rearrange("p (h c) -> p h c", h=16), axis=AX.X, op=ALU.add),
         reads=[sqv_], writes=[st2])
    rs = rstd_of(k, st2, 64.0, GN_EPS, 16)
    k.tt("pool", y3, y3, b16(rs), ALU.mult, [yv, rs], [yv])
    k.tt("dve", yv2[:], yv2[:], c["lng"][:], ALU.mult, [yv, c["lng"]], [yv])
    k.tt("pool", yv2[:], yv2[:], c["lnb"][:], ALU.add, [yv, c["lnb"]], [yv])
    bon = sqv
    k.tt("dve", bon[:, :].rearrange("p (h c) -> p h c", h=16), Vb[:, :].rearrange("p (h c) -> p h c", h=16), b16(bsum), ALU.mult,
         [Vb, bsum], [sqv_])
    k.tt("pool", yv2[:], yv2[:], bon[:], ALU.add, [yv, sqv_], [yv])
    szt_ = k.rot("r7_xs", [128, 8, 128], BF16, n=2)
    szt = V2(szt_)
    k.dma(szt[:], sq["SZs"].t[t0:t0 + 128, 0:D], [sq["SZs"]], [szt_])
    k.tt("dve", yv2[:], yv2[:], szt[:], ALU.mult, [yv, szt_], [yv])
    yT = k.rot("r7_xs", [128, 8, 128], BF16, n=2)
    for tb in range(2):
        ps = k.psum()
        for jj in range(4):
            j = tb * 4 + jj
            k.tr(ps[:, jj * 128:(jj + 1) * 128], yv2[:, j * 128:(j + 1) * 128], k.ident[:], [yv, k.ident], [ps])
        k.copy("act" if tb == 0 else "dve", yT[:, tb * 4:(tb + 1) * 4, :], v4(ps), [ps], [yT])
    k.dma(sq["YGs"].t[0:D, t0:t0 + 128].rearrange("(j p) t -> p j t", p=128), yT[:], [yT], [sq["YGs"]], q="act")


WNAMES = ["norm_g", "mod_w", "mod_b", "final_g",
          "lru_w_in", "lru_conv_w", "lru_conv_b", "lru_gate_w", "lru_gate_b", "lru_lam", "lru_w_out",
          "mlstm_w_in", "mlstm_gate_b", "mlstm_norm_g", "mlstm_w_out",
          "r7_mu", "r7_w_rkvz", "r7_w0", "r7_w1", "r7_w2", "r7_a0", "r7_a1", "r7_a2",
          "r7_k_k", "r7_k_a", "r7_r_k", "r7_ln_g", "r7_ln_b", "r7_w_out"]


def build(T, CTX, kinds, wshapes):
    nc = bass.Bass("TRN2", target_bir_lowering=False)
    with contextlib.ExitStack() as st:
        S = Sch(nc, st)
        k = K(nc, S)
        W = {}
        W["x"] = Buf(nc.dram_tensor("x", [T, D], F32, kind="ExternalInput"), "x")
        W["ctx"] = Buf(nc.dram_tensor("ctx", [CTX, D], F32, kind="ExternalInput"), "ctx")
        W["c"] = Buf(nc.dram_tensor("c", [D], F32, kind="ExternalInput"), "c")
        W["c_ctx"] = Buf(nc.dram_tensor("c_ctx", [D], F32, kind="ExternalInput"), "c_ctx")
        for n in WNAMES:
            W[n] = Buf(nc.dram_tensor(n, list(wshapes[n]), F32, kind="ExternalInput"), n)
        OUT = Buf(nc.dram_tensor("out", [T, D], F32, kind="ExternalOutput"), "out")
        Tm = max(T, CTX)
        scr = {
            "HT": S.dram("s_HT", [D, Tm], BF16),
            "UT": S.dram("s_UT", [LRU_W, Tm], F32),
            "SZ": S.dram("s_SZ", [LRU_W, Tm], BF16),
            "UC": S.dram("s_UC", [LRU_W, Tm], F32),
            "HF": S.dram("s_HF", [LRU_W, Tm], F32),
            "YG": S.dram("s_YG", [2048, Tm], BF16),
            "YGC": S.dram("s_YGC", [2048, CTX], BF16),
            "R7r": S.dram("s_R7r", [Tm, D], F32),
            "R7k": S.dram("s_R7k", [Tm, D], F32),
            "R7kk": S.dram("s_R7kk", [Tm, D], F32),
            "R7v": S.dram("s_R7v", [Tm, D], BF16),
            "HTC": S.dram("s_HTC", [D, CTX], BF16),
            "HFT": S.dram("s_HFT", [T, 2048], F32),
            "HFTC": S.dram("s_HFTC", [CTX, 2048], F32),
            "SZT": S.dram("s_SZT", [T, 2048], BF16),
            "SZTC": S.dram("s_SZTC", [CTX, 2048], BF16),
        }
        XR = [S.dram("s_XR0", [T, D], F32), S.dram("s_XR1", [T, D], F32)]
        CR = [S.dram("s_CR0", [CTX, D], F32), S.dram("s_CR1", [CTX, D], F32)]

        k.modscr = {"x": S.dram("s_MODX", [128, 3072], F32), "c": S.dram("s_MODC", [128, 3072], F32)}
        make_consts(k)
        k.epsb = {}
        for e in (NORM_EPS, GN_EPS):
            b = S.sb([128, 1], F32, name=f"eps{len(k.epsb)}")
            k.memset("pool", b[:], e, [b])
            k.epsb[e] = b
        k.oneb = S.sb([128, 1], F32, name="oneb")
        k.memset("pool", k.oneb[:], 1.0, [k.oneb])
        emit_cond(k, W)

        xcur, ccur = W["x"], W["ctx"]
        cnt = {0: 0, 1: 0, 2: 0}
        depth = len(kinds)
        for li, kind in enumerate(kinds):
            jl = cnt[kind]
            cnt[kind] += 1
            with_ctx = li < depth - 1
            mod = emit_mod(k, li, W)
            if DBG.get("stop") == "mod":
                break
            xdst = XR[li % 2]
            cdst = CR[li % 2]
            seqs = [
                dict(name="c", src=ccur, Tn=CTX, M=mod["c"], dst=cdst, want_out=with_ctx,
                     HTs=scr["HTC"], HFs=scr["HFTC"], SZs=scr["SZTC"], YGs=scr["YGC"]),
                dict(name="x", src=xcur, Tn=T, M=mod["x"], dst=xdst, want_out=True,
                     HTs=scr["HT"], HFs=scr["HFT"], SZs=scr["SZT"], YGs=scr["YG"]),
            ]
            if kind == 0:
                lru_layer(k, li, jl, W, seqs, scr)
            elif kind == 1:
                mlstm_layer(k, li, jl, W, seqs, scr)
            else:
                rwkv_layer(k, li, jl, W, seqs, scr)
            if DBG.get("stop") and DBG.get("stop") != "p3x":
                break
            xcur = xdst
            if with_ctx:
                ccur = cdst
        fg = S.sb([128, 1024], F32, name="final_g")
        k.dma(fg[:], W["final_g"].t[:].partition_broadcast(128), [W["final_g"]], [fg])
        if DBG.get("raw"):
            src = scr[DBG["raw"]] if DBG["raw"] in scr else xcur
            with k.scope():
                for t0 in range(0, T, 128):
                    tb = k.rot("rawb", [128, 1024], F32, n=2)
                    k.dma(tb[:], src.t[t0:t0 + 128, 0:1024], [src], [tb])
                    k.dma(OUT.t[t0:t0 + 128, :], tb[:], [tb], [OUT], q="pool")
        else:
            pass_final(k, xcur, T, fg, OUT)
        S.finish([OUT] + k.dump_bufs)
        print(f"[build] insts={S.n_inst} waits={S.n_wait}")
    return nc


_CACHE = {}


def run(inputs, kinds, n_cores=2):
    x = np.ascontiguousarray(inputs["x"], dtype=np.float32)
    B, T, _ = x.shape
    CTX = inputs["ctx"].shape[1]
    wshapes = {n: inputs[n].shape for n in WNAMES}
    key = (T, CTX, tuple(kinds))
    if key not in _CACHE:
        _CACHE[key] = build(T, CTX, kinds, wshapes)
    nc = _CACHE[key]
    in_maps = []
    for b in range(B):
        m = {"x": x[b], "ctx": np.ascontiguousarray(inputs["ctx"][b], dtype=np.float32),
             "c": np.ascontiguousarray(inputs["c"][b], dtype=np.float32),
             "c_ctx": np.ascontiguousarray(inputs["c_ctx"], dtype=np.float32)}
        for n in WNAMES:
            m[n] = np.ascontiguousarray(inputs[n], dtype=np.float32)
        in_maps.append(m)
    res = run_bass_kernel_spmd(nc, in_maps, core_ids=list(range(B)))
    DBG["results"] = res.results
    return np.stack([res.results[b]["out"] for b in range(B)], axis=0)


def kernel(**inputs):
    return run(inputs, [0, 1, 2, 0])
```
